# Optimizing a Trainium2 kernel written in Bass

```python
import jax, jax.numpy as jnp
from jax import lax
import numpy as np

D_MODEL = 1024
BATCH = 2
SEQ = 16384
DEPTH = 4

D_FF = 2816
LRU_WIDTH = 1024
LRU_HEADS = 4
LRU_HEAD_DIM = LRU_WIDTH // LRU_HEADS
LRU_CONV = 4
LRU_PAD = (2, 1)
LRU_C = 8.0
SC_WIDTH = 512
SC_CONV = 3
SC_PAD = (1, 1)
SGU_WIDTH = 512
SGU_HEADS = 4
SGU_HEAD_DIM = SGU_WIDTH // SGU_HEADS
CHUNK = 128
N_BRANCH = 3
EPS = 1e-6

_PART = (LRU_WIDTH, LRU_WIDTH, SC_WIDTH, SC_WIDTH, SC_WIDTH,
         SGU_WIDTH, SGU_WIDTH, N_BRANCH * D_MODEL)
D_IN = sum(_PART)
SPLIT_POINTS = tuple(int(p) for p in np.cumsum(_PART)[:-1])

kernel_name = "hybrid_rglru_shortconv_sgu_encoder"


def rmsnorm(x, g):
    xf = x.astype(jnp.float32)
    y = xf * lax.rsqrt(jnp.mean(xf * xf, axis=-1, keepdims=True) + EPS)
    return y.astype(x.dtype) * g


def layernorm(x, g, b):
    xf = x.astype(jnp.float32)
    mu = jnp.mean(xf, axis=-1, keepdims=True)
    var = jnp.mean(jnp.square(xf - mu), axis=-1, keepdims=True)
    return ((xf - mu) * lax.rsqrt(var + EPS)).astype(x.dtype) * g + b


def swiglu(h, w_gate, w_up, w_down):
    return (jax.nn.silu(h @ w_gate) * (h @ w_up)) @ w_down


def depthwise_conv(x, w, pad):
    c = x.shape[-1]
    return lax.conv_general_dilated(
        x, w[:, None, :], window_strides=(1,), padding=[pad],
        dimension_numbers=("NWC", "WIO", "NWC"), feature_group_count=c)


def _lin_combine(left, right):
    a_l, b_l = left
    a_r, b_r = right
    return a_l * a_r, a_r * b_l + b_r


def rg_lru(x, w_a, b_a, w_x, b_x, lam, reverse):
    bsz, s, wdt = x.shape
    xh = x.reshape(bsz, s, LRU_HEADS, LRU_HEAD_DIM)
    r = jax.nn.sigmoid(jnp.einsum("bshd,hde->bshe", xh, w_a).reshape(bsz, s, wdt) + b_a)
    i = jax.nn.sigmoid(jnp.einsum("bshd,hde->bshe", xh, w_x).reshape(bsz, s, wdt) + b_x)
    log_a = (-LRU_C * jax.nn.softplus(-lam.astype(jnp.float32))) * r.astype(jnp.float32)
    a = jnp.exp(log_a)
    u = (i * x).astype(jnp.float32) * jnp.sqrt(-jnp.expm1(2.0 * log_a))
    _, h = lax.associative_scan(_lin_combine, (a, u), reverse=reverse, axis=1)
    return h.astype(x.dtype)


def spatial_gating(u, v, ln_g, ln_b, w_s, b_s):
    u = jax.nn.gelu(u)
    v = layernorm(jax.nn.gelu(v), ln_g, ln_b)
    bsz, s, _ = v.shape
    vc = v.reshape(bsz, s // CHUNK, CHUNK, SGU_HEADS, SGU_HEAD_DIM)
    mixed = jnp.einsum("gpq,bnqgc->bnpgc", w_s, vc) + b_s.T[:, :, None]
    return u * mixed.reshape(bsz, s, SGU_WIDTH)


def mixer_block(h, w_in, lru_conv_w, lru_conv_b, lru_wa, lru_ba, lru_wx, lru_bx,
                lru_lambda, lru_w_out, sc_conv_w, sc_w_out, sgu_ln_g, sgu_ln_b,
                sgu_w_s, sgu_b, sgu_w_out, w_o):
    z = h @ w_in
    lru_gate, lru_x, sc_b, sc_c, sc_x, sgu_u, sgu_v, merge = jnp.split(z, SPLIT_POINTS, axis=-1)
    xc = depthwise_conv(lru_x, lru_conv_w, LRU_PAD) + lru_conv_b
    h_fwd = rg_lru(xc, lru_wa[0], lru_ba[0], lru_wx[0], lru_bx[0], lru_lambda[0], reverse=False)
    h_bwd = rg_lru(xc, lru_wa[1], lru_ba[1], lru_wx[1], lru_bx[1], lru_lambda[1], reverse=True)
    y_a = ((h_fwd + h_bwd) * jax.nn.gelu(lru_gate)) @ lru_w_out
    y_b = (sc_b * depthwise_conv(sc_c * sc_x, sc_conv_w, SC_PAD)) @ sc_w_out
    y_c = spatial_gating(sgu_u, sgu_v, sgu_ln_g, sgu_ln_b, sgu_w_s, sgu_b) @ sgu_w_out
    g = jax.nn.sigmoid(merge).reshape(*merge.shape[:-1], N_BRANCH, D_MODEL)
    m = g[..., 0, :] * y_a + g[..., 1, :] * y_b + g[..., 2, :] * y_c
    return m @ w_o


def setup_inputs(seed: int = 0) -> dict:
    key = jax.random.key(seed)
    ks = iter(jax.random.split(key, 32))
    L, D, F = DEPTH, D_MODEL, D_FF

    def w(shape, fan_in):
        return jax.random.normal(next(ks), shape, jnp.float32) * (fan_in ** -0.5)

    def gain(shape):
        return 1.0 + 0.02 * jax.random.normal(next(ks), shape, jnp.float32)

    def bias(shape):
        return 0.02 * jax.random.normal(next(ks), shape, jnp.float32)

    a0 = jax.random.uniform(next(ks), (L, 2, LRU_WIDTH), jnp.float32, 0.9, 0.999)
    s0 = a0 ** (1.0 / LRU_C)
    lru_lambda = jnp.log(s0) - jnp.log1p(-s0)
    return {
        "x": jax.random.normal(next(ks), (BATCH, SEQ, D), jnp.float32),
        "ffn1_pre_g": gain((L, D)),
        "ffn1_w_gate": w((L, D, F), D),
        "ffn1_w_up": w((L, D, F), D),
        "ffn1_w_down": w((L, F, D), F),
        "ffn1_post_g": gain((L, D)),
        "mix_pre_g": gain((L, D)),
        "w_in": w((L, D, D_IN), D),
        "lru_conv_w": w((L, LRU_CONV, LRU_WIDTH), LRU_CONV),
        "lru_conv_b": bias((L, LRU_WIDTH)),
        "lru_wa": w((L, 2, LRU_HEADS, LRU_HEAD_DIM, LRU_HEAD_DIM), LRU_HEAD_DIM),
        "lru_ba": bias((L, 2, LRU_WIDTH)),
        "lru_wx": w((L, 2, LRU_HEADS, LRU_HEAD_DIM, LRU_HEAD_DIM), LRU_HEAD_DIM),
        "lru_bx": bias((L, 2, LRU_WIDTH)),
        "lru_lambda": lru_lambda,
        "lru_w_out": w((L, LRU_WIDTH, D), LRU_WIDTH),
        "sc_conv_w": w((L, SC_CONV, SC_WIDTH), SC_CONV),
        "sc_w_out": w((L, SC_WIDTH, D), SC_WIDTH),
        "sgu_ln_g": gain((L, SGU_WIDTH)),
        "sgu_ln_b": bias((L, SGU_WIDTH)),
        "sgu_w_s": w((L, SGU_HEADS, CHUNK, CHUNK), CHUNK),
        "sgu_b": bias((L, SGU_HEADS, CHUNK)),
        "sgu_w_out": w((L, SGU_WIDTH, D), SGU_WIDTH),
        "w_o": w((L, D, D), D),
        "mix_post_g": gain((L, D)),
        "ffn2_pre_g": gain((L, D)),
        "ffn2_w_gate": w((L, D, F), D),
        "ffn2_w_up": w((L, D, F), D),
        "ffn2_w_down": w((L, F, D), F),
        "ffn2_post_g": gain((L, D)),
    }


def reference(x, ffn1_pre_g, ffn1_w_gate, ffn1_w_up, ffn1_w_down, ffn1_post_g,
              mix_pre_g, w_in, lru_conv_w, lru_conv_b, lru_wa, lru_ba, lru_wx, lru_bx,
              lru_lambda, lru_w_out, sc_conv_w, sc_w_out, sgu_ln_g, sgu_ln_b,
              sgu_w_s, sgu_b, sgu_w_out, w_o, mix_post_g,
              ffn2_pre_g, ffn2_w_gate, ffn2_w_up, ffn2_w_down, ffn2_post_g):
    for l in range(DEPTH):
        f1 = swiglu(rmsnorm(x, ffn1_pre_g[l]), ffn1_w_gate[l], ffn1_w_up[l], ffn1_w_down[l])
        x = x + 0.5 * rmsnorm(f1, ffn1_post_g[l])
        mx = mixer_block(rmsnorm(x, mix_pre_g[l]), w_in[l], lru_conv_w[l], lru_conv_b[l],
                         lru_wa[l], lru_ba[l], lru_wx[l], lru_bx[l], lru_lambda[l],
                         lru_w_out[l], sc_conv_w[l], sc_w_out[l], sgu_ln_g[l], sgu_ln_b[l],
                         sgu_w_s[l], sgu_b[l], sgu_w_out[l], w_o[l])
        x = x + rmsnorm(mx, mix_post_g[l])
        f2 = swiglu(rmsnorm(x, ffn2_pre_g[l]), ffn2_w_gate[l], ffn2_w_up[l], ffn2_w_down[l])
        x = x + 0.5 * rmsnorm(f2, ffn2_post_g[l])
    return x
```

```python
import numpy as np
import ml_dtypes
import concourse.bass as bass
import concourse.mybir as mybir
from concourse.bass_utils import run_bass_kernel_spmd

F32 = mybir.dt.float32
BF16 = mybir.dt.bfloat16
AF = mybir.ActivationFunctionType
ALU = mybir.AluOpType

D = 1024
DC = 8
FF = 2816
FC = 22
NT = 512
EPS = 1e-6
DIN = 7680
NCORES = 8
SLOT_BLKS = 24
NSLOT = 3

ENGS = ("pe", "act", "dve", "pool", "sp")


class Prog:
    def __init__(self):
        self.q = {e: [] for e in ENGS}
        self.cnt = {}
        self.waited = {e: {} for e in ENGS}
        self.lastw = {}
        self.readers = {}
        self.sem = {}
        self.semnames = []

    def new_sem(self, name):
        self.semnames.append(name)
        self.cnt[name] = 0
        return name

    def _deps(self, eng, reads, writes, skip_same=False):
        need = {}
        for k in reads:
            t = self.lastw.get(k)
            if t:
                need[t[0]] = max(need.get(t[0], 0), t[1])
        for k in writes:
            t = self.lastw.get(k)
            if t:
                need[t[0]] = max(need.get(t[0], 0), t[1])
            for e, c in self.readers.get(k, {}).items():
                if e == eng:
                    continue
                need[e] = max(need.get(e, 0), c)
        for p, c in need.items():
            if p == eng and (skip_same or eng == "pe"):
                continue
            if self.waited[eng].get(p, 0) < c:
                self.waited[eng][p] = c
                self.q[eng].append(("wait", p, c))

    def op(self, eng, fn, reads=(), writes=()):
        self._deps(eng, reads, writes)
        self.cnt[eng] += 1
        c = self.cnt[eng]
        self.q[eng].append(("op", fn, eng, 1))
        for k in reads:
            self.readers.setdefault(k, {})[eng] = c
        for k in writes:
            self.lastw[k] = (eng, c)
            self.readers[k] = {}
        return (eng, c)

    def mm_group(self, fns, reads=(), writes=()):
        self._deps("pe", reads, writes)
        self.cnt["pe"] += 1
        c = self.cnt["pe"]
        for f in fns[:-1]:
            self.q["pe"].append(("raw", f))
        self.q["pe"].append(("op", fns[-1], "pe", 1))
        for k in reads:
            self.readers.setdefault(k, {})["pe"] = c
        for k in writes:
            self.lastw[k] = ("pe", c)
            self.readers[k] = {}

    def dma(self, queue, semname, fn, reads=(), writes=()):
        self._deps(queue, reads, writes)
        if semname[:2] in ("st", "ld", "xl", "xs") and self.cnt[semname] > self.waited[queue].get(semname, 0):
            self.waited[queue][semname] = self.cnt[semname]
            self.q[queue].append(("wait", semname, self.cnt[semname]))
        self.cnt[semname] += 16
        c = self.cnt[semname]
        self.q[queue].append(("op", fn, semname, 16))
        for k in reads:
            self.readers.setdefault(k, {})[semname] = c
        for k in writes:
            self.lastw[k] = (semname, c)
            self.readers[k] = {}
        return (semname, c)

    def wait_all(self, eng, names):
        for p in names:
            c = self.cnt[p]
            if c and self.waited[eng].get(p, 0) < c:
                self.waited[eng][p] = c
                self.q[eng].append(("wait", p, c))

    def replay(self, eng, h, sems):
        for it in self.q[eng]:
            if it[0] == "wait":
                h.wait_ge(sems[it[1]], it[2])
            elif it[0] == "raw":
                it[1](h)
            else:
                it[1](h).then_inc(sems[it[2]], it[3])


def _blk(W, kc, mc):
    return W[kc * 128:(kc + 1) * 128, mc * 128:(mc + 1) * 128]


def _load_img(blocks):
    return np.concatenate(blocks, axis=1)


class WPack:
    def __init__(self):
        self.parts = []
        self.off = 0
        self.groups = {}

    def add(self, group, blocks):
        assert len(blocks) <= SLOT_BLKS
        img = _load_img(blocks).astype(np.float32)
        self.groups.setdefault(group, []).append((self.off, len(blocks)))
        self.parts.append(img.reshape(-1))
        self.off += img.size

    def flat(self):
        return np.concatenate(self.parts)


def pack_ffn(wp, pref, wg, wu, wd):
    for j in range(FC):
        wp.add(pref + "gu", [_blk(wg, k, j) for k in range(DC)] + [_blk(wu, k, j) for k in range(DC)])
    for m in range(DC):
        wp.add(pref + "dn", [_blk(wd, k, m) for k in range(FC)])


ZC = dict(gate=0, lrux=8, scb=16, scc=20, scx=24, sgu=28, sgv=32, mrg=36)


def pack_a0(wp, w_in):
    for grp in ([0, 1, 2], [3, 4, 5], [6, 7]):
        wp.add("A_lrux", [_blk(w_in, k, ZC["lrux"] + m) for m in grp for k in range(DC)])
    for grp in ([20, 21, 22], [23, 24, 25], [26, 27]):
        wp.add("A_sccx", [_blk(w_in, k, m) for m in grp for k in range(DC)])


def pack_a(wp, wa, wx):
    for hd in range(4):
        bl = []
        for d in range(2):
            for W in (wa, wx):
                Wh = W[d, hd]
                for mm in range(2):
                    for kk in range(2):
                        bl.append(_blk(Wh, kk, mm))
        wp.add("A_gates", bl)


BZ_COLS = list(range(16, 20)) + list(range(28, 32)) + list(range(0, 8))


def pack_b(wp, w_in, lru_w_out, sc_w_out, sgu_w_out, w_o):
    for i in range(0, len(BZ_COLS), 3):
        grp = BZ_COLS[i:i + 3]
        wp.add("B_z", [_blk(w_in, k, m) for m in grp for k in range(DC)])
    for kg in ([0, 1, 2, 3, 4, 5], [6, 7]):
        wp.add("B_sgv", [_blk(w_in, k, ZC["sgv"] + m) for k in kg for m in range(4)])
    for m in range(DC):
        wp.add("B_mrg", [_blk(w_in, k, ZC["mrg"] + br * 8 + m) for br in range(3) for k in range(DC)])
        wp.add("B_br", [_blk(lru_w_out, k, m) for k in range(8)] + [_blk(sc_w_out, k, m) for k in range(4)]
               + [_blk(sgu_w_out, k, m) for k in range(4)])
    for grp in ([0, 1, 2], [3, 4, 5], [6, 7]):
        wp.add("B_wo", [_blk(w_o, k, m) for m in grp for k in range(DC)])


def chan_major(v):
    v = np.asarray(v, np.float32)
    return np.ascontiguousarray(v.reshape(-1, 128).T)


NU = 25
NU_BIG = 25 + 32
UW = 516
NCV = 180
CV = dict(ffn1_pre=0, ffn1_post=8, mix_pre=16, mix_post=24, ffn2_pre=32, ffn2_post=40,
          cw=48, cb=80, ba=88, bx=104, lam=120, scw=136, sc=148, sc2=164)


class Builder:
    def __init__(self, T, phase, wsize, groups, last=False):
        self.T = T
        self.NTL = T // NT
        self.phase = phase
        self.groups = groups
        self.wsize = wsize

    def U(self, i, w=NT, off=0):
        return self.ar[:, i, off:off + w]

    def B(self, i, half):
        return self.arb[:, i, half * NT:(half + 1) * NT]

    def hv(self, k):
        return self.B(k // 2, k % 2)

    def hkey(self, k):
        return ("ar", k // 2)

    def cvc(self, name, idx):
        o = CV[name] + idx
        return self.cv[:, o:o + 1]

    def mm(self, out, pairs, reads, writes):
        n = len(pairs)
        self.P.mm_group([lambda h, a=a, b=b, i=i: h.matmul(out, a, b, start=(i == 0), stop=(i == n - 1))
                         for i, (a, b) in enumerate(pairs)], reads=reads, writes=writes)

    def wload(self, group, idx):
        P = self.P
        off, nblk = self.groups[group][idx]
        s = self.wslot
        self.wslot = (s + 1) % NSLOT
        dst = self.wring[:, s, 0:nblk * 128]
        src = self.wbf[off:off + nblk * 128 * 128].rearrange("(p f) -> p f", p=128)
        P.dma("sp", f"w{s}{self.sfx}", lambda h, dst=dst, src=src: h.dma_start(out=dst, in_=src),
              reads=[self.wbfkey], writes=[("wr", s)])
        return s

    def wblk(self, s, i, n=1):
        return self.wring[:, s, i * 128:(i + n) * 128]

    def norm_stats(self, src_fn, src_keys):
        P = self.P
        SQ, RS = 23, 24
        for c in range(DC):
            P.op("act", lambda h, c=c: h.activation(self.U(SQ), src_fn(c), AF.Square),
                 reads=[src_keys(c)], writes=[("ar", SQ)])
            P.mm_group([lambda h, c=c: h.matmul(self.ps[6][:], self.ones[:], self.U(SQ),
                                                start=(c == 0), stop=(c == DC - 1))],
                       reads=[("ar", SQ), "ones"], writes=[("ps", 6)])
        P.op("act", lambda h: h.activation(self.U(RS), self.ps[6][:], AF.Sqrt, bias=self.epsb[:, 0:1], scale=1.0),
             reads=[("ps", 6), "ones"], writes=[("ar", RS)])
        P.op("dve", lambda h: h.reciprocal(self.U(RS), self.U(RS)), reads=[("ar", RS)], writes=[("ar", RS)])

    def Xv(self, c, j):
        return self.X[:, c, j * NT:(j + 1) * NT]

    def Xk(self, c, j):
        return ("X", c, j)

    zeros_key = ("ar", 21)

    def zeros_ap(self):
        return self.U(21)

    def x_need(self, j):
        pass

    def x_done(self, j):
        pass

    def make_h(self, j, gname):
        P = self.P
        self.x_need(j)
        ts = slice(j * NT, (j + 1) * NT)
        self.norm_stats(lambda c: self.Xv(c, j), lambda c: self.Xk(c, j))
        for c in range(DC):
            P.op("dve", lambda h, c=c: h.scalar_tensor_tensor(self.hv(c), self.Xv(c, j), self.cvc(gname, c),
                                                              self.U(24), ALU.mult, ALU.mult),
                 reads=[self.Xk(c, j), ("ar", 24), "cv"], writes=[self.hkey(c)])

    def post_residual(self, j, gname, f1u):
        P = self.P
        ts = slice(j * NT, (j + 1) * NT)
        self.norm_stats(lambda c: self.U(f1u + c), lambda c: ("ar", f1u + c))
        for c in range(DC):
            P.op("dve", lambda h, c=c: h.scalar_tensor_tensor(self.U(f1u + c), self.U(f1u + c), self.cvc(gname, c),
                                                              self.U(24), ALU.mult, ALU.mult),
                 reads=[("ar", f1u + c), ("ar", 24), "cv"], writes=[("ar", f1u + c)])
            P.op("pool", lambda h, c=c: h.tensor_tensor(self.Xv(c, j), self.Xv(c, j), self.U(f1u + c), ALU.add),
                 reads=[("ar", f1u + c), self.Xk(c, j)], writes=[self.Xk(c, j)])
        self.x_done(j)

    def ffn_tile(self, j, pref):
        P = self.P
        ACT0, F1, SIL = 4, 15, 22
        self.make_h(j, pref + "_pre")
        hreads = [self.hkey(k) for k in range(DC)]
        for f in range(FC):
            s = self.wload(pref + "gu", f)
            pb = f % 2
            self.mm(self.ps[pb][:], [(self.wblk(s, k), self.hv(k)) for k in range(DC)],
                    reads=[("wr", s)] + hreads, writes=[("ps", pb)])
            self.mm(self.ps[2 + pb][:], [(self.wblk(s, DC + k), self.hv(k)) for k in range(DC)],
                    reads=[("wr", s)] + hreads, writes=[("ps", 2 + pb)])
            P.op("act", lambda h, pb=pb: h.activation(self.U(SIL), self.ps[pb][:], AF.Silu),
                 reads=[("ps", pb)], writes=[("ar", SIL)])
            P.op("dve", lambda h, pb=pb, f=f: h.tensor_tensor(self.B(ACT0 + f // 2, f % 2), self.U(SIL), self.ps[2 + pb][:], ALU.mult),
                 reads=[("ar", SIL), ("ps", 2 + pb)], writes=[("ar", ACT0 + f // 2)])
        areads = [("ar", ACT0 + k) for k in range(FC // 2)]
        for m in range(DC):
            s = self.wload(pref + "dn", m)
            pb = 4 + m % 2
            self.mm(self.ps[pb][:], [(self.wblk(s, k), self.B(ACT0 + k // 2, k % 2)) for k in range(FC)],
                    reads=[("wr", s)] + areads, writes=[("ps", pb)])
            P.op("act", lambda h, pb=pb, m=m: h.activation(self.U(F1 + m), self.ps[pb][:], AF.Copy),
                 reads=[("ps", pb)], writes=[("ar", F1 + m)])
        self.post_residual(j, pref + "_post", F1)

    def a0_tile(self, j):
        P = self.P
        self.make_h(j, "mix_pre")
        hreads = [self.hkey(k) for k in range(DC)]
        ts = slice(j * NT, (j + 1) * NT)
        oc = 0
        for grp_name, nld in (("A_lrux", 3), ("A_sccx", 3)):
            for li in range(nld):
                s = self.wload(grp_name, li)
                nout = self.groups[grp_name][li][1] // DC
                for mi in range(nout):
                    pb = oc % 2
                    st = 4 + oc % 3
                    self.mm(self.ps[pb][:], [(self.wblk(s, mi * DC + k), self.hv(k)) for k in range(DC)],
                            reads=[("wr", s)] + hreads, writes=[("ps", pb)])
                    P.op("act", lambda h, pb=pb, st=st: h.activation(self.U(st), self.ps[pb][:], AF.Copy),
                         reads=[("ps", pb)], writes=[("ar", st)])
                    if oc < 8:
                        dst = self.zl[oc * 128:(oc + 1) * 128, self.zo[0] + j * NT:self.zo[0] + (j + 1) * NT]
                    elif oc < 12:
                        dst = self.zc[(oc - 8) * 128:(oc - 7) * 128, self.zo[1] + j * NT:self.zo[1] + (j + 1) * NT]
                    else:
                        dst = self.zx[(oc - 12) * 128:(oc - 11) * 128, self.zo[1] + j * NT:self.zo[1] + (j + 1) * NT]
                    P.dma("pool", f"st{oc % 3}{self.sfx}", lambda h, dst=dst, st=st: h.dma_start(out=dst, in_=self.U(st)),
                          reads=[("ar", st)], writes=[("z", oc, j)])
                    oc += 1

    def stage_a_all(self):
        seq = [(j, hd) for j in range(self.NTL) for hd in range(4)]
        self.sa_front(*seq[0])
        for k in range(len(seq)):
            if k + 1 < len(seq):
                self.sa_front(*seq[k + 1])
            self.sa_back(*seq[k])

    def sa_front(self, j, hd):
        P = self.P
        if True:
            rot = hd % 2
            XC = [3 + 3 * rot, 4 + 3 * rot]
            XCB = 5 + 3 * rot
            for mm_ in range(2):
                c = 2 * hd + mm_
                lx = (self.lxi % 3)
                self.lxi += 1
                P.dma("pool", f"ld{lx}{self.sfx}", lambda h, c=c, lx=lx: h.dma_start(out=self.U(lx, NT + 3),
                                                                           in_=self.zlp[c * 128:(c + 1) * 128, j * NT:j * NT + NT + 3]),
                      reads=[("z", c, jj) for jj in (j - 1, j, j + 1)] + ["zpad"], writes=[("ar", lx)])
                xc = XC[mm_]
                P.op("dve", lambda h, c=c, lx=lx, xc=xc: h.tensor_scalar(self.U(xc), self.U(lx, NT, 0), self.cvc("cw", c),
                                                                         self.cvc("cb", c), ALU.mult, ALU.add),
                     reads=[("ar", lx), "cv"], writes=[("ar", xc)])
                for k in range(1, 4):
                    P.op("dve", lambda h, c=c, lx=lx, xc=xc, k=k: h.scalar_tensor_tensor(
                        self.U(xc), self.U(lx, NT, k), self.cvc("cw", k * 8 + c), self.U(xc), ALU.mult, ALU.add),
                        reads=[("ar", lx), ("ar", xc), "cv"], writes=[("ar", xc)])
                P.op("pool", lambda h, xc=xc, mm_=mm_, XCB=XCB: h.tensor_copy(self.B(XCB, mm_), self.U(xc)),
                     reads=[("ar", xc)], writes=[("ar", XCB)])

    def sa_back(self, j, hd):
        P = self.P
        rev = lambda t: bass.AP(t.tensor, t.offset + NT - 1, [list(t.ap[0]), [-1, NT]])
        ts = slice(j * NT, (j + 1) * NT)
        if True:
            rot = hd % 2
            XC = [3 + 3 * rot, 4 + 3 * rot]
            XCB = 5 + 3 * rot
            s = self.wload("A_gates", hd)
            sets = {}
            for mm_ in range(2):
                for d in range(2):
                    S0 = self.SA0 + 4 * ((hd % 2) * 4 + mm_ * 2 + d)
                    sets[(mm_, d)] = (S0, S0 + 1, S0 + 2, S0 + 3)
                    for g in range(2):
                        pb = (d * 2 + g) * 2 + mm_
                        base = ((d * 2 + g) * 2 + mm_) * 2
                        self.mm(self.ps[pb][:], [(self.wblk(s, base + kk), self.B(XCB, kk)) for kk in range(2)],
                                reads=[("wr", s), ("ar", XCB)], writes=[("ps", pb)])
            for mm_ in range(2):
                c = 2 * hd + mm_
                for d in range(2):
                    RA, SH, IU, PP = sets[(mm_, d)]
                    for g in range(2):
                        pb = (d * 2 + g) * 2 + mm_
                        dstu = RA if g == 0 else IU
                        bname = "ba" if g == 0 else "bx"
                        P.op("act", lambda h, pb=pb, dstu=dstu, bname=bname, d=d, c=c: h.activation(
                            self.U(dstu), self.ps[pb][:], AF.Sigmoid, bias=self.cvc(bname, d * 8 + c), scale=1.0),
                            reads=[("ps", pb), "cv"], writes=[("ar", dstu)])
            for mm_ in range(2):
                c = 2 * hd + mm_
                for d in range(2):
                    RA, SH, IU, PP = sets[(mm_, d)]
                    P.op("act", lambda h, d=d, c=c, RA=RA, SH=SH: h.activation(self.U(SH), self.U(RA), AF.Exp, scale=self.cvc("sc2", d * 8 + c)),
                         reads=[("ar", RA), "cv"], writes=[("ar", SH)])
                    P.op("act", lambda h, d=d, c=c, RA=RA: h.activation(self.U(RA), self.U(RA), AF.Exp, scale=self.cvc("sc", d * 8 + c)),
                         reads=[("ar", RA), "cv"], writes=[("ar", RA)])
            for mm_ in range(2):
                for d in range(2):
                    RA, SH, IU, PP = sets[(mm_, d)]
                    P.op("act", lambda h, SH=SH: h.activation(self.U(SH), self.U(SH), AF.Sqrt, bias=self.oneb[:, 0:1], scale=-1.0),
                         reads=[("ar", SH), "ones"], writes=[("ar", SH)])
            for mm_ in range(2):
                c = 2 * hd + mm_
                xc = XC[mm_]
                for d in range(2):
                    RA, SH, IU, PP = sets[(mm_, d)]
                    P.op("dve", lambda h, IU=IU, xc=xc: h.tensor_tensor(self.U(IU), self.U(IU), self.U(xc), ALU.mult),
                         reads=[("ar", IU), ("ar", xc)], writes=[("ar", IU)])
                    P.op("dve", lambda h, IU=IU, SH=SH: h.tensor_tensor(self.U(IU), self.U(IU), self.U(SH), ALU.mult),
                         reads=[("ar", IU), ("ar", SH)], writes=[("ar", IU)])
                    w = (lambda t: t) if d == 0 else rev
                    P.op("dve", lambda h, w=w, SH=SH, RA=RA, IU=IU: h.tensor_tensor_scan(w(self.U(SH)), w(self.U(RA)), w(self.U(IU)), 0.0, ALU.mult, ALU.add),
                         reads=[("ar", RA), ("ar", IU), ("ar", SH)], writes=[("ar", SH)])
                    P.op("dve", lambda h, w=w, PP=PP, RA=RA: h.tensor_tensor_scan(w(self.U(PP)), w(self.U(RA)), w(self.zeros_ap()), 1.0, ALU.mult, ALU.add),
                         reads=[("ar", RA), self.zeros_key], writes=[("ar", PP)])
                    e = NT - 1 if d == 0 else 0
                    for q, src in ((2 * d, PP), (2 * d + 1, SH)):
                        o = (q * DC + c) * self.NTL + j
                        P.op("pool", lambda h, o=o, src=src, e=e: h.tensor_copy(self.summ[:, o:o + 1], self.U(src, 1, e)),
                             reads=[("ar", src)], writes=["summ"])
                (RAf, SHf, IUf, PPf), (RAb, SHb, IUb, PPb) = sets[(mm_, 0)], sets[(mm_, 1)]
                P.op("pool", lambda h, SHf=SHf, SHb=SHb: h.tensor_tensor(self.U(SHf), self.U(SHf), self.U(SHb), ALU.add),
                     reads=[("ar", SHf), ("ar", SHb)], writes=[("ar", SHf)])
                for dst, src in ((self.hs, SHf), (self.pf, PPf), (self.pb, PPb)):
                    k3 = self.sti % 3
                    self.sti += 1
                    P.dma("pool", f"st{k3}{self.sfx}", lambda h, dst=dst, src=src, c=c: h.dma_start(out=dst[c * 128:(c + 1) * 128, ts], in_=self.U(src)),
                          reads=[("ar", src)], writes=[("spill", c, j)])

    def sv(self, q, j):
        t = self.summ
        base = t[:, (q * DC) * self.NTL + j:(q * DC) * self.NTL + j + 1]
        return bass.AP(base.tensor, base.offset, [list(base.ap[0]), [self.NTL, DC]])

    def compose_e2(self):
        P = self.P
        NTL = self.NTL
        e = lambda q: self.e2t[:, q * DC:(q + 1) * DC]
        ops = []
        ops.append(lambda h: h.tensor_copy(e(0), self.sv(0, 0)))
        ops.append(lambda h: h.tensor_copy(e(1), self.sv(1, 0)))
        for j in range(1, NTL):
            ops.append(lambda h, j=j: h.tensor_tensor(e(1), e(1), self.sv(0, j), ALU.mult))
            ops.append(lambda h, j=j: h.tensor_tensor(e(1), e(1), self.sv(1, j), ALU.add))
            ops.append(lambda h, j=j: h.tensor_tensor(e(0), e(0), self.sv(0, j), ALU.mult))
        ops.append(lambda h: h.tensor_copy(e(2), self.sv(2, NTL - 1)))
        ops.append(lambda h: h.tensor_copy(e(3), self.sv(3, NTL - 1)))
        for j in range(NTL - 2, -1, -1):
            ops.append(lambda h, j=j: h.tensor_tensor(e(3), e(3), self.sv(2, j), ALU.mult))
            ops.append(lambda h, j=j: h.tensor_tensor(e(3), e(3), self.sv(3, j), ALU.add))
            ops.append(lambda h, j=j: h.tensor_tensor(e(2), e(2), self.sv(2, j), ALU.mult))
        for f in ops:
            P.op("dve", f, reads=["summ", "e2t"], writes=["e2t"])

    def carries(self):
        P = self.P
        NTL = self.NTL
        r = lambda n, q: self.rcv[:, (n * 4 + q) * DC:(n * 4 + q + 1) * DC]
        tmp = self.e2t[:, 0:DC]

        def ct(d, j):
            b = self.ct[:, (d * DC) * NTL + j:(d * DC) * NTL + j + 1]
            return bass.AP(b.tensor, b.offset, [list(b.ap[0]), [NTL, DC]])
        ops = []
        for d, (n0, qa, qh) in enumerate(((0, 0, 1), (3, 2, 3))):
            ops.append(lambda h, n0=n0, qa=qa, qh=qh: h.tensor_tensor(tmp, r(n0 + 1, qa), r(n0 + 2, qh), ALU.mult))
            ops.append(lambda h, n0=n0, qh=qh: h.tensor_tensor(tmp, tmp, r(n0 + 1, qh), ALU.add))
            ops.append(lambda h, n0=n0, qa=qa: h.tensor_tensor(tmp, tmp, r(n0, qa), ALU.mult))
            j0 = 0 if d == 0 else NTL - 1
            ops.append(lambda h, n0=n0, qh=qh, d=d, j0=j0: h.tensor_tensor(ct(d, j0), tmp, r(n0, qh), ALU.add))
            seq = range(0, NTL - 1) if d == 0 else range(NTL - 1, 0, -1)
            for j in seq:
                jn = j + 1 if d == 0 else j - 1
                ops.append(lambda h, d=d, j=j, jn=jn, qa=qa: h.tensor_tensor(ct(d, jn), ct(d, j), self.sv(qa, j), ALU.mult))
                ops.append(lambda h, d=d, j=j, jn=jn, qh=qh: h.tensor_tensor(ct(d, jn), ct(d, jn), self.sv(qh, j), ALU.add))
        for f in ops:
            P.op("dve", f, reads=["summ", "e2t", "rcv", "ct"], writes=["ct", "e2t"])
        self.ctv = lambda d, c, j: self.ct[:, (d * DC + c) * NTL + j:(d * DC + c) * NTL + j + 1]

    def stage_b_tile(self, j):
        P = self.P
        ts = slice(j * NT, (j + 1) * NT)
        self.make_h(j, "mix_pre")
        hreads = [self.hkey(k) for k in range(DC)]
        YA, YB, YC, MM, UU = 4, 8, 10, 12, 16
        ST = (20, 21, 22)
        WC, WX, TMP = 23, 24, 22
        oc = 0
        nld = len(self.groups["B_z"])
        for li in range(nld):
            s = self.wload("B_z", li)
            nout = self.groups["B_z"][li][1] // DC
            for mi in range(nout):
                pb = oc % 2
                self.mm(self.ps[pb][:], [(self.wblk(s, mi * DC + k), self.hv(k)) for k in range(DC)],
                        reads=[("wr", s)] + hreads, writes=[("ps", pb)])
                if oc < 4:
                    c = oc
                    for (u, src) in ((WC, self.zcp), (WX, self.zxp)):
                        zi = (8 if u == WC else 12) + c
                        P.dma("pool", f"ld{u - 23}{self.sfx}", lambda h, u=u, src=src, c=c: h.dma_start(
                            out=self.U(u, NT + 2), in_=src[c * 128:(c + 1) * 128, j * NT:j * NT + NT + 2]),
                            reads=[("z", zi, jj) for jj in (j - 1, j, j + 1)] + ["zpad"], writes=[("ar", u)])
                    P.op("dve", lambda h: h.tensor_tensor(self.U(WC, NT + 2), self.U(WC, NT + 2), self.U(WX, NT + 2), ALU.mult),
                         reads=[("ar", WC), ("ar", WX)], writes=[("ar", WC)])
                    P.op("dve", lambda h, c=c: h.tensor_scalar(self.U(TMP), self.U(WC, NT, 0), self.cvc("scw", c), None, ALU.mult),
                         reads=[("ar", WC), "cv"], writes=[("ar", TMP)])
                    for k in (1, 2):
                        P.op("dve", lambda h, c=c, k=k: h.scalar_tensor_tensor(self.U(TMP), self.U(WC, NT, k), self.cvc("scw", k * 4 + c),
                                                                               self.U(TMP), ALU.mult, ALU.add),
                             reads=[("ar", WC), ("ar", TMP), "cv"], writes=[("ar", TMP)])
                    P.op("dve", lambda h, c=c, pb=pb: h.tensor_tensor(self.B(YB + c // 2, c % 2), self.U(TMP), self.ps[pb][:], ALU.mult),
                         reads=[("ar", TMP), ("ps", pb)], writes=[("ar", YB + c // 2)])
                elif oc < 8:
                    c = oc - 4
                    P.op("act", lambda h, c=c, pb=pb: h.activation(self.U(UU + c), self.ps[pb][:], AF.Gelu_apprx_tanh),
                         reads=[("ps", pb)], writes=[("ar", UU + c)])
                else:
                    c = oc - 8
                    for u, src in zip(ST, (self.hs, self.pf, self.pb)):
                        P.dma("pool", f"st{u - 20}{self.sfx}", lambda h, u=u, src=src, c=c: h.dma_start(out=self.U(u), in_=src[c * 128:(c + 1) * 128, ts]),
                              reads=[("spill", c, j)], writes=[("ar", u)])
                    P.op("dve", lambda h, c=c: h.scalar_tensor_tensor(self.U(ST[0]), self.U(ST[1]), self.ctv(0, c, j), self.U(ST[0]), ALU.mult, ALU.add),
                         reads=[("ar", ST[0]), ("ar", ST[1]), "ct"], writes=[("ar", ST[0])])
                    P.op("dve", lambda h, c=c: h.scalar_tensor_tensor(self.U(ST[0]), self.U(ST[2]), self.ctv(1, c, j), self.U(ST[0]), ALU.mult, ALU.add),
                         reads=[("ar", ST[0]), ("ar", ST[2]), "ct"], writes=[("ar", ST[0])])
                    P.op("act", lambda h, pb=pb: h.activation(self.U(TMP), self.ps[pb][:], AF.Gelu_apprx_tanh),
                         reads=[("ps", pb)], writes=[("ar", TMP)])
                    P.op("dve", lambda h, c=c: h.tensor_tensor(self.B(YA + c // 2, c % 2), self.U(ST[0]), self.U(TMP), ALU.mult),
                         reads=[("ar", ST[0]), ("ar", TMP)], writes=[("ar", YA + c // 2)])
                oc += 1
        s0 = self.wload("B_sgv", 0)
        s1 = self.wload("B_sgv", 1)
        V, VN = 20, 21
        for n in range(4):
            pb = 2 + n % 2
            pairs = []
            for k in range(DC):
                sl, kk = (s0, k) if k < 6 else (s1, k - 6)
                pairs.append((self.hv(k)[:, n * 128:(n + 1) * 128], self.wblk(sl, kk * 4, 4)))
            self.mm(self.ps[pb][:], pairs, reads=[("wr", s0), ("wr", s1)] + hreads, writes=[("ps", pb)])
            P.op("act", lambda h, pb=pb: h.activation(self.U(V), self.ps[pb][:], AF.Gelu_apprx_tanh),
                 reads=[("ps", pb)], writes=[("ar", V)])
            P.op("dve", lambda h: h.bn_stats(self.bst[:, 0:6], self.U(V)), reads=[("ar", V)], writes=["bst"])
            P.op("dve", lambda h: h.bn_aggr(self.bst[:, 6:8], self.bst[:, 0:6]), reads=["bst"], writes=["bst"])
            P.op("act", lambda h: h.activation(self.bst[:, 8:9], self.bst[:, 7:8], AF.Sqrt, bias=self.eps1[:, 0:1], scale=1.0),
                 reads=["bst", "ones"], writes=["bst"])
            P.op("dve", lambda h: h.reciprocal(self.bst[:, 8:9], self.bst[:, 8:9]), reads=["bst"], writes=["bst"])
            P.op("dve", lambda h: h.tensor_scalar(self.U(V), self.U(V), self.bst[:, 6:7], self.bst[:, 8:9], ALU.subtract, ALU.mult),
                 reads=["bst", ("ar", V)], writes=[("ar", V)])
            P.op("dve", lambda h: h.tensor_tensor(self.U(V), self.U(V), self.lnc[:, 0, :], ALU.mult),
                 reads=[("ar", V), "lnc"], writes=[("ar", V)])
            P.op("dve", lambda h, n=n: h.tensor_tensor(self.B(VN + n // 2, n % 2), self.U(V), self.lnc[:, 1, :], ALU.add),
                 reads=[("ar", V), "lnc"], writes=[("ar", VN + n // 2)])
        for g in range(4):
            pb = 2 + g
            fns = []
            for n in range(4):
                o = self.ps[pb][:, n * 128:(n + 1) * 128]
                fns.append(lambda h, o=o, g=g, n=n: h.matmul(o, self.B(VN + n // 2, n % 2)[:, g * 128:(g + 1) * 128], self.wst[:, g * 128:(g + 1) * 128],
                                                            start=True, stop=False))
                fns.append(lambda h, o=o, g=g: h.matmul(o, self.ones[0:1, :], self.bsr[0:1, g * 128:(g + 1) * 128], start=False, stop=True))
            P.mm_group(fns, reads=[("ar", VN), ("ar", VN + 1), "wst", "ones", "bsr"], writes=[("ps", pb)])
            P.op("dve", lambda h, g=g, pb=pb: h.tensor_tensor(self.B(YC + g // 2, g % 2), self.U(UU + g), self.ps[pb][:], ALU.mult),
                 reads=[("ar", UU + g), ("ps", pb)], writes=[("ar", YC + g // 2)])
        GM = (16, 17, 18)
        T1, T2 = 19, 20
        yar = [("ar", YA + k) for k in range(4)]
        for m in range(DC):
            sg = self.wload("B_mrg", m)
            sb_ = self.wload("B_br", m)
            for br in range(3):
                self.mm(self.ps[br][:], [(self.wblk(sg, br * DC + k), self.hv(k)) for k in range(DC)],
                        reads=[("wr", sg)] + hreads, writes=[("ps", br)])
                P.op("act", lambda h, br=br: h.activation(self.U(GM[br]), self.ps[br][:], AF.Sigmoid),
                     reads=[("ps", br)], writes=[("ar", GM[br])])
            self.mm(self.ps[3][:], [(self.wblk(sb_, k), self.B(YA + k // 2, k % 2)) for k in range(8)],
                    reads=[("wr", sb_)] + yar, writes=[("ps", 3)])
            self.mm(self.ps[4][:], [(self.wblk(sb_, 8 + k), self.B(YB + k // 2, k % 2)) for k in range(4)],
                    reads=[("wr", sb_), ("ar", YB), ("ar", YB + 1)], writes=[("ps", 4)])
            self.mm(self.ps[5][:], [(self.wblk(sb_, 12 + k), self.B(YC + k // 2, k % 2)) for k in range(4)],
                    reads=[("wr", sb_), ("ar", YC), ("ar", YC + 1)], writes=[("ps", 5)])
            P.op("dve", lambda h: h.tensor_tensor(self.U(T1), self.U(GM[0]), self.ps[3][:], ALU.mult),
                 reads=[("ar", GM[0]), ("ps", 3)], writes=[("ar", T1)])
            P.op("dve", lambda h: h.tensor_tensor(self.U(T2), self.U(GM[1]), self.ps[4][:], ALU.mult),
                 reads=[("ar", GM[1]), ("ps", 4)], writes=[("ar", T2)])
            P.op("pool", lambda h: h.tensor_tensor(self.U(T1), self.U(T1), self.U(T2), ALU.add),
                 reads=[("ar", T1), ("ar", T2)], writes=[("ar", T1)])
            P.op("dve", lambda h: h.tensor_tensor(self.U(T2), self.U(GM[2]), self.ps[5][:], ALU.mult),
                 reads=[("ar", GM[2]), ("ps", 5), ("ar", T2)], writes=[("ar", T2)])
            P.op("pool", lambda h, m=m: h.tensor_tensor(self.B(MM + m // 2, m % 2), self.U(T1), self.U(T2), ALU.add),
                 reads=[("ar", T1), ("ar", T2)], writes=[("ar", MM + m // 2)])
        F1 = 4
        mreads = [("ar", MM + k) for k in range(4)]
        oc = 0
        for li in range(3):
            s = self.wload("B_wo", li)
            nout = self.groups["B_wo"][li][1] // DC
            for mi in range(nout):
                pb = oc % 2
                self.mm(self.ps[pb][:], [(self.wblk(s, mi * DC + k), self.B(MM + k // 2, k % 2)) for k in range(DC)],
                        reads=[("wr", s)] + mreads, writes=[("ps", pb)])
                P.op("act", lambda h, pb=pb, oc=oc: h.activation(self.U(F1 + oc), self.ps[pb][:], AF.Copy),
                     reads=[("ps", pb)], writes=[("ar", F1 + oc)])
                oc += 1
        self.post_residual(j, "mix_post", F1)

    def build(self):
        T, NTL, ph = self.T, self.NTL, self.phase
        nc = bass.Bass("TRN2", target_bir_lowering=False)
        self.nc = nc
        P = self.P = Prog()
        ein = lambda n, sh: nc.dram_tensor(n, sh, F32, kind="ExternalInput").ap()
        eout = lambda n, sh: nc.dram_tensor(n, sh, F32, kind="ExternalOutput").ap()
        wf = ein("wf", [self.wsize])
        self.wbf = nc.dram_tensor("wbf", [self.wsize], BF16).ap()
        cvin = ein("cvin", [128, NCV])
        if ph in ("P1", "P3"):
            xin = ein("xin", [D, T])
            xout = eout("xout", [D, T])
        if ph == "P1":
            self.zl, self.zc, self.zx = eout("zl", [D, T]), eout("zc", [512, T]), eout("zx", [512, T])
        if ph == "P2":
            self.zlp = ein("zlp", [D, T + 3])
            self.hs, self.pf, self.pb = eout("hs", [D, T]), eout("pf", [D, T]), eout("pb", [D, T])
            summ_o = eout("summ", [128, 4 * DC * NTL])
            e2_o = eout("e2", [128, 4 * DC])
        if ph == "P3":
            self.hs, self.pf, self.pb = ein("hs", [D, T]), ein("pf", [D, T]), ein("pb", [D, T])
            self.zcp, self.zxp = ein("zcp", [512, T + 2]), ein("zxp", [512, T + 2])
            summ_i = ein("summ", [128, 4 * DC * NTL])
            rcv_i = ein("rcv", [128, 6 * 4 * DC])
            lnc_i = ein("lnc", [128, 2 * 512])
            wst_i = ein("wst", [128, 4 * 128])
            bsr_i = ein("bsr", [1, 512])
        self.CVT = 128 * 8192
        self.sfx = ""
        self.wbfkey = "wbf"
        self.zo = (0, 0)

        for e in ENGS:
            P.new_sem(e)
        for s in range(NSLOT):
            P.new_sem(f"w{s}")
        for n in ("xio", "cvt", "cst", "st0", "st1", "st2", "ld0", "ld1", "ld2"):
            P.new_sem(n)

        from contextlib import ExitStack
        with ExitStack() as es:
            def sb(name, shape, dt):
                return es.enter_context(nc.sbuf_tensor(name, shape, dt))
            if ph in ("P1", "P3"):
                self.X = sb("X", [128, DC, T], F32)
            nu = NU_BIG if ph == "P2" else NU
            self.SA0 = 25
            self.ar = sb("arena", [128, nu, UW], F32)
            self.arb = self.ar[:].bitcast(BF16)
            self.wring = sb("wring", [128, NSLOT, SLOT_BLKS * 128], BF16)
            self.ones = sb("ones", [128, 128], F32)
            self.epsb = sb("epsb", [128, 1], F32)
            self.eps1 = sb("eps1", [128, 1], F32)
            self.oneb = sb("oneb", [128, 1], F32)
            self.cv = sb("cv", [128, NCV], F32)
            self.summ = sb("summ_t", [128, 4 * DC * NTL], F32)
            self.e2t = sb("e2t", [128, 4 * DC], F32)
            self.ct = sb("ct", [128, 2 * DC * NTL], F32)
            self.bst = sb("bst", [128, 16], F32)
            if ph == "P3":
                self.rcv = sb("rcv_t", [128, 6 * 4 * DC], F32)
                self.lnc = sb("lnc_t", [128, 2, 512], F32)
                self.wst = sb("wst_t", [128, 512], BF16)
                self.bsr = sb("bsr_t", [1, 512], F32)
            self.ps = [es.enter_context(nc.psum_tensor(f"ps{i}", [128, NT], F32)) for i in range(8)]
            self.wslot = 0
            self.lxi = self.seti = self.gpi = self.sti = 0
            sems = {n: es.enter_context(nc.semaphore(n)) for n in P.semnames}

            P.op("pool", lambda h: h.memset(self.epsb[:], float(D * EPS)), writes=["ones"])
            P.op("pool", lambda h: h.memset(self.eps1[:], float(EPS)), writes=["ones"])
            P.op("pool", lambda h: h.memset(self.oneb[:], 1.0), writes=["ones"])
            P.op("pool", lambda h: h.memset(self.ones[:], 1.0), writes=["ones"])
            P.op("pool", lambda h: h.memset(self.U(21, UW), 0.0), writes=[("ar", 21)])
            P.dma("pool", "cst", lambda h: h.dma_start(out=self.cv[:], in_=cvin[:, :]), writes=["cv"])
            if ph == "P3":
                P.dma("pool", "cst", lambda h: h.dma_start(out=self.rcv[:], in_=rcv_i[:, :]), writes=["rcv"])
                P.dma("pool", "cst", lambda h: h.dma_start(out=self.summ[:], in_=summ_i[:, :]), writes=["summ"])
                P.dma("pool", "cst", lambda h: h.dma_start(out=self.lnc[:], in_=lnc_i.rearrange("p (a b) -> p a b", a=2)), writes=["lnc"])
                P.dma("pool", "cst", lambda h: h.dma_start(out=self.U(0), in_=wst_i[:, :]), writes=["wstf", ("ar", 0)])
                P.dma("pool", "cst", lambda h: h.dma_start(out=self.bsr[:], in_=bsr_i[:, :]), writes=["ones"])
            for k in ("cv", "rcv", "summ", "lnc", "wstf", "ones", ("ar", 0)):
                if k in P.lastw and P.lastw[k][0] == "cst":
                    P.lastw[k] = ("cst", P.cnt["cst"])
            n = self.wsize
            npieces = (n + self.CVT - 1) // self.CVT
            for p in range(npieces):
                a, b = p * self.CVT, min(n, (p + 1) * self.CVT)
                src = wf[a:b].rearrange("(p f) -> p f", p=128)
                dst = self.wbf[a:b].rearrange("(p f) -> p f", p=128)
                P.dma("pool", "cvt", lambda h, src=src, dst=dst: h.dma_start(out=dst, in_=src), writes=["wbf"])
            P.op("dve", lambda h: h.tensor_scalar(self.cv[:, 0:48], self.cv[:, 0:48], 32.0, None, ALU.mult), reads=["cv"], writes=["cv"])
            for nm in ("ffn1_post", "ffn2_post"):
                P.op("dve", lambda h, nm=nm: h.tensor_scalar(self.cv[:, CV[nm]:CV[nm] + 8], self.cv[:, CV[nm]:CV[nm] + 8], 0.5, None, ALU.mult),
                     reads=["cv"], writes=["cv"])
            if ph == "P2":
                sc = self.cv[:, CV["sc"]:CV["sc"] + 16]
                sc2 = self.cv[:, CV["sc2"]:CV["sc2"] + 16]
                lam = self.cv[:, CV["lam"]:CV["lam"] + 16]
                P.op("act", lambda h: h.activation(sc, lam, AF.Exp, scale=-1.0), reads=["cv"], writes=["cv"])
                P.op("act", lambda h: h.activation(sc, sc, AF.Ln, bias=self.oneb[:, 0:1], scale=1.0), reads=["cv", "ones"], writes=["cv"])
                P.op("dve", lambda h: h.tensor_scalar(sc2, sc, -16.0, None, ALU.mult), reads=["cv"], writes=["cv"])
                P.op("dve", lambda h: h.tensor_scalar(sc, sc, -8.0, None, ALU.mult), reads=["cv"], writes=["cv"])
            if ph == "P3":
                P.op("act", lambda h: h.activation(self.wst[:], self.U(0), AF.Copy), reads=["wstf", ("ar", 0)], writes=["wst"])
                self.carries()
            if ph in ("P1", "P3"):
                for j in range(NTL):
                    for c in range(DC):
                        P.dma("sp", "xio", lambda h, c=c, j=j: h.dma_start(out=self.X[:, c, j * NT:(j + 1) * NT],
                                                                           in_=xin[c * 128:(c + 1) * 128, j * NT:(j + 1) * NT]),
                              writes=[("X", c, j)])
                for j in range(NTL):
                    for c in range(DC):
                        P.lastw[("X", c, j)] = ("xio", P.cnt["xio"])
            if ph == "P1":
                for j in range(NTL):
                    self.ffn_tile(j, "ffn1")
                for j in range(NTL):
                    self.a0_tile(j)
            elif ph == "P2":
                self.stage_a_all()
                self.compose_e2()
                P.dma("pool", "cst", lambda h: h.dma_start(out=summ_o[:, :], in_=self.summ[:]), reads=["summ"])
                P.dma("pool", "cst", lambda h: h.dma_start(out=e2_o[:, :], in_=self.e2t[:]), reads=["e2t"])
            else:
                for j in range(NTL):
                    self.stage_b_tile(j)
                for j in range(NTL):
                    self.ffn_tile(j, "ffn2")
            if ph in ("P1", "P3"):
                for j in range(NTL):
                    for c in range(DC):
                        P.dma("sp", "xio", lambda h, c=c, j=j: h.dma_start(out=xout[c * 128:(c + 1) * 128, j * NT:(j + 1) * NT],
                                                                           in_=self.X[:, c, j * NT:(j + 1) * NT]),
                              reads=[("X", c, j)])
            P.wait_all("sp", ["xio"])
            P.wait_all("pool", ["cvt", "cst", "st0", "st1", "st2", "ld0", "ld1", "ld2"])

            with nc.Block() as block:
                @block.sync
                def _(e):
                    P.replay("sp", e, sems)

                @block.gpsimd
                def _(e):
                    P.replay("pool", e, sems)

                @block.tensor
                def _(e):
                    P.replay("pe", e, sems)

                @block.scalar
                def _(e):
                    P.replay("act", e, sems)

                @block.vector
                def _(e):
                    P.replay("dve", e, sems)
        return nc


class FusedBuilder(Builder):
    def __init__(self, T, L, wsizes, lgroups):
        self.T = T
        self.NTL = T // NT
        self.L = L
        self.phase = "F"
        self.wsizes = wsizes
        self.lgroups = lgroups

    zeros_key = "zeros"

    def zeros_ap(self):
        return self.zt[:]

    def Xv(self, c, j):
        return self.XT[:, j % 2, c, :]

    def Xk(self, c, j):
        return ("XT", j % 2, c)

    def _xload(self, j):
        if j in self.xloaded or j >= self.NTL:
            return
        self.xloaded.add(j)
        sl = j % 2
        src = self.xsrc.rearrange("(c p) t -> p c t", p=128)[:, :, j * NT:(j + 1) * NT]
        self.P.dma("sp", f"xl{sl}{self.sfx}", lambda h, sl=sl, src=src: h.dma_start(out=self.XT[:, sl, :, :], in_=src),
                   reads=[(self.xsrc_key, j)], writes=[("XT", sl, c) for c in range(DC)])

    def x_need(self, j):
        self._xload(j)
        self._xload(j + 1)

    def x_done(self, j):
        sl = j % 2
        dst = self.xdst.rearrange("(c p) t -> p c t", p=128)[:, :, j * NT:(j + 1) * NT]
        self.P.dma("pool", f"xs{sl}{self.sfx}", lambda h, sl=sl, dst=dst: h.dma_start(out=dst, in_=self.XT[:, sl, :, :]),
                   reads=[("XT", sl, c) for c in range(DC)], writes=[(self.xdst_key, j)])

    def build(self):
        T, NTL, L = self.T, self.NTL, self.L
        nc = bass.Bass("TRN2", target_bir_lowering=False)
        self.nc = nc
        P = self.P = Prog()
        ein = lambda n, sh: nc.dram_tensor(n, sh, F32, kind="ExternalInput").ap()
        xin = ein("xin", [D, T])
        xout = nc.dram_tensor("xout", [D, T], F32, kind="ExternalOutput").ap()
        wfs = [ein(f"wf{l}", [self.wsizes[l]]) for l in range(L)]
        self.wbfs = [nc.dram_tensor(f"wbf{l}", [self.wsizes[l]], BF16).ap() for l in range(L)]
        cvin = ein("cvin", [L * 128, NCV])
        lnc_i = ein("lnc", [L * 128, 2 * 512])
        wst_i = ein("wst", [L * 128, 4 * 128])
        bsr_i = ein("bsr", [L, 512])
        scr = lambda n, sh: nc.dram_tensor(n, sh, F32).ap()
        Xd = scr("Xd", [D, T])
        self.zlp = self.zl = scr("zlp", [D, T + 3])
        self.zcp = self.zc = scr("zcp", [512, T + 2])
        self.zxp = self.zx = scr("zxp", [512, T + 2])
        self.hs, self.pf, self.pb = scr("hs", [D, T]), scr("pf", [D, T]), scr("pb", [D, T])
        self.zo = (2, 1)
        self.CVT = 128 * 8192

        for e in ENGS:
            P.new_sem(e)
        P.new_sem("cst")
        for l in range(L):
            P.new_sem(f"cvt{l}")
            P.new_sem(f"cst_{l}")
            for s in range(NSLOT):
                P.new_sem(f"w{s}_{l}")
            for n in ("st0", "st1", "st2", "ld0", "ld1", "ld2", "xl0", "xl1", "xs0", "xs1"):
                P.new_sem(f"{n}_{l}")

        from contextlib import ExitStack
        with ExitStack() as es:
            def sb(name, shape, dt):
                return es.enter_context(nc.sbuf_tensor(name, shape, dt))
            self.XT = sb("XT", [128, 2, DC, NT], F32)
            self.zt = sb("zt", [128, NT], F32)
            self.SA0 = 25
            self.ar = sb("arena", [128, NU_BIG, UW], F32)
            self.arb = self.ar[:].bitcast(BF16)
            self.wring = sb("wring", [128, NSLOT, SLOT_BLKS * 128], BF16)
            self.ones = sb("ones", [128, 128], F32)
            self.epsb = sb("epsb", [128, 1], F32)
            self.eps1 = sb("eps1", [128, 1], F32)
            self.oneb = sb("oneb", [128, 1], F32)
            self.cv = sb("cv", [128, NCV], F32)
            self.summ = sb("summ_t", [128, 4 * DC * NTL], F32)
            self.e2t = sb("e2t", [128, 4 * DC], F32)
            self.ct = sb("ct", [128, 2 * DC * NTL], F32)
            self.bst = sb("bst", [128, 16], F32)
            self.rcv = sb("rcv_t", [128, 6 * 4 * DC], F32)
            self.lnc = sb("lnc_t", [128, 2, 512], F32)
            self.wst = sb("wst_t", [128, 512], BF16)
            self.bsr = sb("bsr_t", [1, 512], F32)
            self.ps = [es.enter_context(nc.psum_tensor(f"ps{i}", [128, NT], F32)) for i in range(8)]
            self.wslot = 0
            self.lxi = self.seti = self.gpi = self.sti = 0
            sems = {n: es.enter_context(nc.semaphore(n)) for n in P.semnames}

            P.op("pool", lambda h: h.memset(self.epsb[:], float(D * EPS)), writes=["ones"])
            P.op("pool", lambda h: h.memset(self.eps1[:], float(EPS)), writes=["ones"])
            P.op("pool", lambda h: h.memset(self.oneb[:], 1.0), writes=["ones"])
            P.op("pool", lambda h: h.memset(self.ones[:], 1.0), writes=["ones"])
            P.op("pool", lambda h: h.memset(self.rcv[:], 0.0), writes=["rcv"])
            P.op("pool", lambda h: h.memset(self.zt[:], 0.0), writes=["zeros"])
            P.op("pool", lambda h: h.memset(self.U(21, UW), 0.0), writes=[("ar", 21)])
            for (zt, nch, cols) in ((self.zlp, DC, ((0, 2), (T + 2, T + 3))), (self.zcp, 4, ((0, 1), (T + 1, T + 2))),
                                    (self.zxp, 4, ((0, 1), (T + 1, T + 2)))):
                for (a, b) in cols:
                    dst = zt.rearrange("(c p) t -> p c t", p=128)[:, :, a:b]
                    srcz = self.ar[:, 21, 0:nch * (b - a)].rearrange("p (c t) -> p c t", c=nch)
                    P.dma("pool", "cst", lambda h, dst=dst, srcz=srcz: h.dma_start(out=dst, in_=srcz, allow_slow_non_contiguous=True),
                          reads=[("ar", 21)], writes=["zpad"])
            P.lastw["zpad"] = ("cst", P.cnt["cst"])
            for l in range(L):
                n = self.wsizes[l]
                npieces = (n + self.CVT - 1) // self.CVT
                for p in range(npieces):
                    a, b = p * self.CVT, min(n, (p + 1) * self.CVT)
                    src = wfs[l][a:b].rearrange("(p f) -> p f", p=128)
                    dst = self.wbfs[l][a:b].rearrange("(p f) -> p f", p=128)
                    P.dma("pool", f"cvt{l}", lambda h, src=src, dst=dst: h.dma_start(out=dst, in_=src), writes=[("wbf", l)])

            for l in range(L):
                self.sfx = f"_{l}"
                self.groups = self.lgroups[l]
                self.wbf = self.wbfs[l]
                self.wbfkey = ("wbf", l)
                cs = f"cst_{l}"
                P.dma("pool", cs, lambda h, l=l: h.dma_start(out=self.cv[:], in_=cvin[l * 128:(l + 1) * 128, :]), writes=["cv"])
                P.dma("pool", cs, lambda h, l=l: h.dma_start(out=self.lnc[:], in_=lnc_i[l * 128:(l + 1) * 128, :].rearrange("p (a b) -> p a b", a=2)),
                      writes=["lnc"])
                P.dma("pool", cs, lambda h, l=l: h.dma_start(out=self.U(0), in_=wst_i[l * 128:(l + 1) * 128, :]), writes=["wstf", ("ar", 0)])
                P.dma("pool", cs, lambda h, l=l: h.dma_start(out=self.bsr[:], in_=bsr_i[l:l + 1, :]), writes=["bsr"])
                for k in ("cv", "lnc", "wstf", "bsr", ("ar", 0)):
                    P.lastw[k] = (cs, P.cnt[cs])
                P.op("dve", lambda h: h.tensor_scalar(self.cv[:, 0:48], self.cv[:, 0:48], 32.0, None, ALU.mult), reads=["cv"], writes=["cv"])
                for nm in ("ffn1_post", "ffn2_post"):
                    P.op("dve", lambda h, nm=nm: h.tensor_scalar(self.cv[:, CV[nm]:CV[nm] + 8], self.cv[:, CV[nm]:CV[nm] + 8], 0.5, None, ALU.mult),
                         reads=["cv"], writes=["cv"])
                sc = self.cv[:, CV["sc"]:CV["sc"] + 16]
                sc2 = self.cv[:, CV["sc2"]:CV["sc2"] + 16]
                lam = self.cv[:, CV["lam"]:CV["lam"] + 16]
                P.op("act", lambda h: h.activation(sc, lam, AF.Exp, scale=-1.0), reads=["cv"], writes=["cv"])
                P.op("act", lambda h: h.activation(sc, sc, AF.Ln, bias=self.oneb[:, 0:1], scale=1.0), reads=["cv", "ones"], writes=["cv"])
                P.op("dve", lambda h: h.tensor_scalar(sc2, sc, -16.0, None, ALU.mult), reads=["cv"], writes=["cv"])
                P.op("dve", lambda h: h.tensor_scalar(sc, sc, -8.0, None, ALU.mult), reads=["cv"], writes=["cv"])
                P.op("act", lambda h: h.activation(self.wst[:], self.U(0), AF.Copy), reads=["wstf", ("ar", 0)], writes=["wst"])
                self.xsrc, self.xsrc_key = (xin, "Xin") if l == 0 else (Xd, "Xd")
                self.xdst, self.xdst_key = Xd, "Xd"
                self.xloaded = set()
                for j in range(NTL):
                    self.ffn_tile(j, "ffn1")
                self.xsrc, self.xsrc_key = Xd, "Xd"
                self.xloaded = set()
                for j in range(NTL):
                    self.a0_tile(j)
                self.stage_a_all()
                self.carries()
                self.xloaded = set()
                for j in range(NTL):
                    self.stage_b_tile(j)
                if l == L - 1:
                    self.xdst, self.xdst_key = xout, "Xout"
                self.xloaded = set()
                for j in range(NTL):
                    self.ffn_tile(j, "ffn2")
            P.wait_all("sp", [n for n in P.semnames if n[:2] in ("xl",) or n[0] == "w"])
            P.wait_all("pool", [n for n in P.semnames if n[:2] in ("cv", "cs", "st", "ld", "xs")])

            with nc.Block() as block:
                @block.sync
                def _(e):
                    P.replay("sp", e, sems)

                @block.gpsimd
                def _(e):
                    P.replay("pool", e, sems)

                @block.tensor
                def _(e):
                    P.replay("pe", e, sems)

                @block.scalar
                def _(e):
                    P.replay("act", e, sems)

                @block.vector
                def _(e):
                    P.replay("dve", e, sems)
        return nc


_NC_CACHE = {}


def _get_nc(T, phase, wsize, groups):
    key = (T, phase, wsize)
    if key not in _NC_CACHE:
        _NC_CACHE[key] = Builder(T, phase, wsize, groups).build()
    return _NC_CACHE[key]


def _cvin(inp, l):
    cv = np.zeros((128, NCV), np.float32)
    for nm in ("ffn1_pre", "ffn1_post", "mix_pre", "mix_post", "ffn2_pre", "ffn2_post"):
        cv[:, CV[nm]:CV[nm] + 8] = chan_major(inp[nm + "_g"][l])
    for k in range(4):
        cv[:, CV["cw"] + k * 8:CV["cw"] + (k + 1) * 8] = chan_major(inp["lru_conv_w"][l, k])
    cv[:, CV["cb"]:CV["cb"] + 8] = chan_major(inp["lru_conv_b"][l])
    for d in range(2):
        cv[:, CV["ba"] + d * 8:CV["ba"] + (d + 1) * 8] = chan_major(inp["lru_ba"][l, d])
        cv[:, CV["bx"] + d * 8:CV["bx"] + (d + 1) * 8] = chan_major(inp["lru_bx"][l, d])
        cv[:, CV["lam"] + d * 8:CV["lam"] + (d + 1) * 8] = chan_major(inp["lru_lambda"][l, d])
    for k in range(3):
        cv[:, CV["scw"] + k * 4:CV["scw"] + (k + 1) * 4] = chan_major(inp["sc_conv_w"][l, k])
    return cv


def _pad_halo(zs, nl, nr):
    out = []
    for c in range(NCORES):
        z = zs[c]
        C, T = z.shape
        p = np.zeros((C, nl + T + nr), np.float32)
        p[:, nl:nl + T] = z
        if c % 4 != 0:
            p[:, 0:nl] = zs[c - 1][:, T - nl:T]
        if c % 4 != 3:
            p[:, nl + T:] = zs[c + 1][:, 0:nr]
        out.append(p)
    return out


FUSED = True


def _pack_layer(inp, l):
    wp = WPack()
    pack_ffn(wp, "ffn1", inp["ffn1_w_gate"][l], inp["ffn1_w_up"][l], inp["ffn1_w_down"][l])
    pack_a0(wp, inp["w_in"][l])
    pack_a(wp, inp["lru_wa"][l], inp["lru_wx"][l])
    pack_b(wp, inp["w_in"][l], inp["lru_w_out"][l], inp["sc_w_out"][l], inp["sgu_w_out"][l], inp["w_o"][l])
    pack_ffn(wp, "ffn2", inp["ffn2_w_gate"][l], inp["ffn2_w_up"][l], inp["ffn2_w_down"][l])
    return wp


def kernel_fused(inp):
    x = inp["x"]
    Bsz, S, _ = x.shape
    L = inp["w_in"].shape[0]
    T = S
    packs = [_pack_layer(inp, l) for l in range(L)]
    flats = [wp.flat() for wp in packs]
    key = ("F", T, L)
    if key not in _NC_CACHE:
        _NC_CACHE[key] = FusedBuilder(T, L, [f.size for f in flats], [wp.groups for wp in packs]).build()
    nc = _NC_CACHE[key]
    cvin = np.concatenate([_cvin(inp, l) for l in range(L)], axis=0)
    lnc = np.concatenate([np.concatenate([np.broadcast_to(inp["sgu_ln_g"][l][None, :], (128, 512)),
                                          np.broadcast_to(inp["sgu_ln_b"][l][None, :], (128, 512))], axis=1) for l in range(L)],
                         axis=0).astype(np.float32)
    wst = np.concatenate([np.transpose(inp["sgu_w_s"][l], (2, 0, 1)).reshape(128, 512) for l in range(L)], axis=0).astype(np.float32)
    bsr = np.ascontiguousarray(inp["sgu_b"].reshape(L, 512)).astype(np.float32)
    common = {"cvin": np.ascontiguousarray(cvin), "lnc": np.ascontiguousarray(lnc), "wst": np.ascontiguousarray(wst), "bsr": bsr}
    for l in range(L):
        common[f"wf{l}"] = flats[l]
    cores = list(range(Bsz))
    in_maps = [dict(common, xin=np.ascontiguousarray(x[b].T)) for b in cores]
    res = run_bass_kernel_spmd(nc, in_maps, core_ids=cores).results
    out = np.zeros((Bsz, S, D), np.float32)
    for b in cores:
        out[b] = res[b]["xout"].T
    return out


def kernel(**inp):
    inp = {k: np.asarray(v) for k, v in inp.items()}
    if FUSED:
        return kernel_fused(inp)
    return kernel_unfused(inp)


def kernel_unfused(inp):
    x = inp["x"]
    Bsz, S, _ = x.shape
    L = inp["w_in"].shape[0]
    T = S * Bsz // NCORES
    cps = NCORES // Bsz
    assert cps == 4
    cores = list(range(NCORES))
    Xs = [np.ascontiguousarray(x[c // cps, (c % cps) * T:(c % cps + 1) * T, :].T) for c in cores]
    for l in range(L):
        cv = _cvin(inp, l)
        wp = WPack()
        pack_ffn(wp, "ffn1", inp["ffn1_w_gate"][l], inp["ffn1_w_up"][l], inp["ffn1_w_down"][l])
        pack_a0(wp, inp["w_in"][l])
        flat = wp.flat()
        nc = _get_nc(T, "P1", flat.size, wp.groups)
        res = run_bass_kernel_spmd(nc, [{"xin": Xs[c], "wf": flat, "cvin": cv} for c in cores], core_ids=cores).results
        Xs = [res[c]["xout"] for c in cores]
        zlp = _pad_halo([res[c]["zl"] for c in cores], 2, 1)
        zcp = _pad_halo([res[c]["zc"] for c in cores], 1, 1)
        zxp = _pad_halo([res[c]["zx"] for c in cores], 1, 1)
        wp = WPack()
        pack_a(wp, inp["lru_wa"][l], inp["lru_wx"][l])
        flat = wp.flat()
        nc = _get_nc(T, "P2", flat.size, wp.groups)
        res = run_bass_kernel_spmd(nc, [{"zlp": zlp[c], "wf": flat, "cvin": cv} for c in cores], core_ids=cores).results
        hs = [res[c]["hs"] for c in cores]
        pf = [res[c]["pf"] for c in cores]
        pb = [res[c]["pb"] for c in cores]
        summ = [res[c]["summ"] for c in cores]
        e2 = [res[c]["e2"] for c in cores]
        rcv = []
        for c in cores:
            r = np.zeros((128, 6, 4 * DC), np.float32)
            for n in range(3):
                if c % cps - n - 1 >= 0:
                    r[:, n] = e2[c - n - 1]
                if c % cps + n + 1 < cps:
                    r[:, 3 + n] = e2[c + n + 1]
            rcv.append(r.reshape(128, -1))
        wp = WPack()
        pack_b(wp, inp["w_in"][l], inp["lru_w_out"][l], inp["sc_w_out"][l], inp["sgu_w_out"][l], inp["w_o"][l])
        pack_ffn(wp, "ffn2", inp["ffn2_w_gate"][l], inp["ffn2_w_up"][l], inp["ffn2_w_down"][l])
        flat = wp.flat()
        lnc = np.concatenate([np.broadcast_to(inp["sgu_ln_g"][l][None, :], (128, 512)),
                              np.broadcast_to(inp["sgu_ln_b"][l][None, :], (128, 512))], axis=1).astype(np.float32)
        wst = np.ascontiguousarray(np.transpose(inp["sgu_w_s"][l], (2, 0, 1))).reshape(128, 512).astype(np.float32)
        bsr = np.ascontiguousarray(inp["sgu_b"][l].reshape(1, 512)).astype(np.float32)
        nc = _get_nc(T, "P3", flat.size, wp.groups)
        res = run_bass_kernel_spmd(nc, [{"xin": Xs[c], "wf": flat, "cvin": cv, "hs": hs[c], "pf": pf[c], "pb": pb[c],
                                         "zcp": zcp[c], "zxp": zxp[c], "summ": summ[c], "rcv": rcv[c],
                                         "lnc": np.ascontiguousarray(lnc), "wst": wst, "bsr": bsr} for c in cores],
                                   core_ids=cores).results
        Xs = [res[c]["xout"] for c in cores]
    out = np.zeros((Bsz, S, D), np.float32)
    for c in cores:
        out[c // cps, (c % cps) * T:(c % cps + 1) * T, :] = Xs[c].T
    return out
```

```python
import numpy as np
import ml_dtypes
import concourse.bass as bass
import concourse.mybir as mybir
from concourse.bass_utils import run_bass_kernel_spmd

F32 = mybir.dt.float32
BF16 = mybir.dt.bfloat16
AF = mybir.ActivationFunctionType
ALU = mybir.AluOpType

D = 1024
DC = 8
FF = 2816
FC = 22
NT = 512
EPS = 1e-6
DIN = 7680
NCORES = 8
SLOT_BLKS = 24
NSLOT = 3

ENGS = ("pe", "act", "dve", "pool", "sp")


class Prog:
    def __init__(self):
        self.q = {e: [] for e in ENGS}
        self.cnt = {}
        self.waited = {e: {} for e in ENGS}
        self.lastw = {}
        self.readers = {}
        self.sem = {}
        self.semnames = []

    def new_sem(self, name):
        self.semnames.append(name)
        self.cnt[name] = 0
        return name

    def _deps(self, eng, reads, writes, skip_same=False):
        need = {}
        for k in reads:
            t = self.lastw.get(k)
            if t:
                need[t[0]] = max(need.get(t[0], 0), t[1])
        for k in writes:
            t = self.lastw.get(k)
            if t:
                need[t[0]] = max(need.get(t[0], 0), t[1])
            for e, c in self.readers.get(k, {}).items():
                if e == eng:
                    continue
                need[e] = max(need.get(e, 0), c)
        for p, c in need.items():
            if p == eng and (skip_same or eng == "pe"):
                continue
            if self.waited[eng].get(p, 0) < c:
                self.waited[eng][p] = c
                self.q[eng].append(("wait", p, c))

    def op(self, eng, fn, reads=(), writes=()):
        self._deps(eng, reads, writes)
        self.cnt[eng] += 1
        c = self.cnt[eng]
        self.q[eng].append(("op", fn, eng, 1))
        for k in reads:
            self.readers.setdefault(k, {})[eng] = c
        for k in writes:
            self.lastw[k] = (eng, c)
            self.readers[k] = {}
        return (eng, c)

    def mm_group(self, fns, reads=(), writes=()):
        self._deps("pe", reads, writes)
        self.cnt["pe"] += 1
        c = self.cnt["pe"]
        for f in fns[:-1]:
            self.q["pe"].append(("raw", f))
        self.q["pe"].append(("op", fns[-1], "pe", 1))
        for k in reads:
            self.readers.setdefault(k, {})["pe"] = c
        for k in writes:
            self.lastw[k] = ("pe", c)
            self.readers[k] = {}

    def dma(self, queue, semname, fn, reads=(), writes=()):
        self._deps(queue, reads, writes)
        if semname[:2] in ("st", "ld", "xl", "xs") and self.cnt[semname] > self.waited[queue].get(semname, 0):
            self.waited[queue][semname] = self.cnt[semname]
            self.q[queue].append(("wait", semname, self.cnt[semname]))
        self.cnt[semname] += 16
        c = self.cnt[semname]
        self.q[queue].append(("op", fn, semname, 16))
        for k in reads:
            self.readers.setdefault(k, {})[semname] = c
        for k in writes:
            self.lastw[k] = (semname, c)
            self.readers[k] = {}
        return (semname, c)

    def wait_all(self, eng, names):
        for p in names:
            c = self.cnt[p]
            if c and self.waited[eng].get(p, 0) < c:
                self.waited[eng][p] = c
                self.q[eng].append(("wait", p, c))

    def replay(self, eng, h, sems):
        for it in self.q[eng]:
            if it[0] == "wait":
                h.wait_ge(sems[it[1]], it[2])
            elif it[0] == "raw":
                it[1](h)
            else:
                it[1](h).then_inc(sems[it[2]], it[3])


def _blk(W, kc, mc):
    return W[kc * 128:(kc + 1) * 128, mc * 128:(mc + 1) * 128]


def _load_img(blocks):
    return np.concatenate(blocks, axis=1)


class WPack:
    def __init__(self):
        self.parts = []
        self.off = 0
        self.groups = {}

    def add(self, group, blocks):
        assert len(blocks) <= SLOT_BLKS
        img = _load_img(blocks).astype(np.float32)
        self.groups.setdefault(group, []).append((self.off, len(blocks)))
        self.parts.append(img.reshape(-1))
        self.off += img.size

    def flat(self):
        return np.concatenate(self.parts)


def pack_ffn(wp, pref, wg, wu, wd):
    for j in range(FC):
        wp.add(pref + "gu", [_blk(wg, k, j) for k in range(DC)] + [_blk(wu, k, j) for k in range(DC)])
    for m in range(DC):
        wp.add(pref + "dn", [_blk(wd, k, m) for k in range(FC)])


ZC = dict(gate=0, lrux=8, scb=16, scc=20, scx=24, sgu=28, sgv=32, mrg=36)


def pack_a0(wp, w_in):
    for grp in ([0, 1, 2], [3, 4, 5], [6, 7]):
        wp.add("A_lrux", [_blk(w_in, k, ZC["lrux"] + m) for m in grp for k in range(DC)])
    for grp in ([20, 21, 22], [23, 24, 25], [26, 27]):
        wp.add("A_sccx", [_blk(w_in, k, m) for m in grp for k in range(DC)])


def pack_a(wp, wa, wx):
    for hd in range(4):
        bl = []
        for d in range(2):
            for W in (wa, wx):
                Wh = W[d, hd]
                for mm in range(2):
                    for kk in range(2):
                        bl.append(_blk(Wh, kk, mm))
        wp.add("A_gates", bl)


BZ_COLS = list(range(16, 20)) + list(range(28, 32)) + list(range(0, 8))


def pack_b(wp, w_in, lru_w_out, sc_w_out, sgu_w_out, w_o):
    for i in range(0, len(BZ_COLS), 3):
        grp = BZ_COLS[i:i + 3]
        wp.add("B_z", [_blk(w_in, k, m) for m in grp for k in range(DC)])
    for kg in ([0, 1, 2, 3, 4, 5], [6, 7]):
        wp.add("B_sgv", [_blk(w_in, k, ZC["sgv"] + m) for k in kg for m in range(4)])
    for m in range(DC):
        wp.add("B_mrg", [_blk(w_in, k, ZC["mrg"] + br * 8 + m) for br in range(3) for k in range(DC)])
        wp.add("B_br", [_blk(lru_w_out, k, m) for k in range(8)] + [_blk(sc_w_out, k, m) for k in range(4)]
               + [_blk(sgu_w_out, k, m) for k in range(4)])
    for grp in ([0, 1, 2], [3, 4, 5], [6, 7]):
        wp.add("B_wo", [_blk(w_o, k, m) for m in grp for k in range(DC)])


def chan_major(v):
    v = np.asarray(v, np.float32)
    return np.ascontiguousarray(v.reshape(-1, 128).T)


NU = 25
NU_BIG = 25 + 32
UW = 516
NCV = 180
CV = dict(ffn1_pre=0, ffn1_post=8, mix_pre=16, mix_post=24, ffn2_pre=32, ffn2_post=40,
          cw=48, cb=80, ba=88, bx=104, lam=120, scw=136, sc=148, sc2=164)


class Builder:
    def __init__(self, T, phase, wsize, groups, last=False):
        self.T = T
        self.NTL = T // NT
        self.phase = phase
        self.groups = groups
        self.wsize = wsize

    def U(self, i, w=NT, off=0):
        return self.ar[:, i, off:off + w]

    def B(self, i, half):
        return self.arb[:, i, half * NT:(half + 1) * NT]

    def hv(self, k):
        return self.B(k // 2, k % 2)

    def hkey(self, k):
        return ("ar", k // 2)

    def cvc(self, name, idx):
        o = CV[name] + idx
        return self.cv[:, o:o + 1]

    def mm(self, out, pairs, reads, writes):
        n = len(pairs)
        self.P.mm_group([lambda h, a=a, b=b, i=i: h.matmul(out, a, b, start=(i == 0), stop=(i == n - 1))
                         for i, (a, b) in enumerate(pairs)], reads=reads, writes=writes)

    def wload(self, group, idx):
        P = self.P
        off, nblk = self.groups[group][idx]
        s = self.wslot
        self.wslot = (s + 1) % NSLOT
        dst = self.wring[:, s, 0:nblk * 128]
        src = self.wbf[off:off + nblk * 128 * 128].rearrange("(p f) -> p f", p=128)
        P.dma("sp", f"w{s}{self.sfx}", lambda h, dst=dst, src=src: h.dma_start(out=dst, in_=src),
              reads=[self.wbfkey], writes=[("wr", s)])
        return s

    def wblk(self, s, i, n=1):
        return self.wring[:, s, i * 128:(i + n) * 128]

    def norm_stats(self, src_fn, src_keys, hl):
        P = self.P
        RS = 24
        for c in range(DC):
            u = hl + c % 2
            P.op("act", lambda h, c=c, u=u: h.activation(self.B(u, 0), src_fn(c), AF.Square),
                 reads=[src_keys(c)], writes=[("ar", u)])
            P.mm_group([lambda h, c=c, u=u: h.matmul(self.ps[6][:], self.onesb[:], self.B(u, 0),
                                                     start=(c == 0), stop=(c == DC - 1))],
                       reads=[("ar", u), "ones"], writes=[("ps", 6)])
        P.op("act", lambda h: h.activation(self.U(RS), self.ps[6][:], AF.Sqrt, bias=self.epsb[:, 0:1], scale=1.0),
             reads=[("ps", 6), "ones"], writes=[("ar", RS)])
        P.op("dve", lambda h: h.reciprocal(self.U(RS), self.U(RS)), reads=[("ar", RS)], writes=[("ar", RS)])

    def Xv(self, c, j):
        return self.X[:, c, j * NT:(j + 1) * NT]

    def Xk(self, c, j):
        return ("X", c, j)

    zeros_key = ("ar", 21)

    def zeros_ap(self):
        return self.U(21)

    def x_need(self, j):
        pass

    def x_done(self, j):
        pass

    def make_h(self, j, gname, hl):
        P = self.P
        self.x_need(j)
        ts = slice(j * NT, (j + 1) * NT)
        self.norm_stats(lambda c: self.Xv(c, j), lambda c: self.Xk(c, j), hl)
        for c in range(DC):
            P.op("dve", lambda h, c=c: h.scalar_tensor_tensor(self.hv(c), self.Xv(c, j), self.cvc(gname, c),
                                                              self.U(24), ALU.mult, ALU.mult),
                 reads=[self.Xk(c, j), ("ar", 24), "cv"], writes=[self.hkey(c)])

    def post_residual(self, j, gname, f1u, hl):
        P = self.P
        ts = slice(j * NT, (j + 1) * NT)
        self.norm_stats(lambda c: self.U(f1u + c), lambda c: ("ar", f1u + c), hl)
        for c in range(DC):
            P.op("dve", lambda h, c=c: h.scalar_tensor_tensor(self.U(f1u + c), self.U(f1u + c), self.cvc(gname, c),
                                                              self.U(24), ALU.mult, ALU.mult),
                 reads=[("ar", f1u + c), ("ar", 24), "cv"], writes=[("ar", f1u + c)])
            P.op("pool", lambda h, c=c: h.tensor_tensor(self.Xv(c, j), self.Xv(c, j), self.U(f1u + c), ALU.add),
                 reads=[("ar", f1u + c), self.Xk(c, j)], writes=[self.Xk(c, j)])
        self.x_done(j)

    def ffn_tile(self, j, pref):
        P = self.P
        ACT0, F1, SIL = 4, 15, 22
        self.make_h(j, pref + "_pre", 4)
        hreads = [self.hkey(k) for k in range(DC)]
        for f in range(FC):
            s = self.wload(pref + "gu", f)
            pb = f % 2
            self.mm(self.ps[pb][:], [(self.wblk(s, k), self.hv(k)) for k in range(DC)],
                    reads=[("wr", s)] + hreads, writes=[("ps", pb)])
            self.mm(self.ps[2 + pb][:], [(self.wblk(s, DC + k), self.hv(k)) for k in range(DC)],
                    reads=[("wr", s)] + hreads, writes=[("ps", 2 + pb)])
            P.op("act", lambda h, pb=pb: h.activation(self.U(SIL), self.ps[pb][:], AF.Silu),
                 reads=[("ps", pb)], writes=[("ar", SIL)])
            P.op("dve", lambda h, pb=pb, f=f: h.tensor_tensor(self.B(ACT0 + f // 2, f % 2), self.U(SIL), self.ps[2 + pb][:], ALU.mult),
                 reads=[("ar", SIL), ("ps", 2 + pb)], writes=[("ar", ACT0 + f // 2)])
        areads = [("ar", ACT0 + k) for k in range(FC // 2)]
        for m in range(DC):
            s = self.wload(pref + "dn", m)
            pb = 4 + m % 2
            self.mm(self.ps[pb][:], [(self.wblk(s, k), self.B(ACT0 + k // 2, k % 2)) for k in range(FC)],
                    reads=[("wr", s)] + areads, writes=[("ps", pb)])
            P.op("act", lambda h, pb=pb, m=m: h.activation(self.U(F1 + m), self.ps[pb][:], AF.Copy),
                 reads=[("ps", pb)], writes=[("ar", F1 + m)])
        self.post_residual(j, pref + "_post", F1, 4)

    def a0_tile(self, j):
        P = self.P
        self.make_h(j, "mix_pre", 8)
        hreads = [self.hkey(k) for k in range(DC)]
        ts = slice(j * NT, (j + 1) * NT)
        oc = 0
        for grp_name, nld in (("A_lrux", 3), ("A_sccx", 3)):
            for li in range(nld):
                s = self.wload(grp_name, li)
                nout = self.groups[grp_name][li][1] // DC
                for mi in range(nout):
                    pb = oc % 2
                    st = 4 + oc % 3
                    self.mm(self.ps[pb][:], [(self.wblk(s, mi * DC + k), self.hv(k)) for k in range(DC)],
                            reads=[("wr", s)] + hreads, writes=[("ps", pb)])
                    P.op("act", lambda h, pb=pb, st=st: h.activation(self.U(st), self.ps[pb][:], AF.Copy),
                         reads=[("ps", pb)], writes=[("ar", st)])
                    if oc < 8:
                        dst = self.zl[oc * 128:(oc + 1) * 128, self.zo[0] + j * NT:self.zo[0] + (j + 1) * NT]
                    elif oc < 12:
                        dst = self.zc[(oc - 8) * 128:(oc - 7) * 128, self.zo[1] + j * NT:self.zo[1] + (j + 1) * NT]
                    else:
                        dst = self.zx[(oc - 12) * 128:(oc - 11) * 128, self.zo[1] + j * NT:self.zo[1] + (j + 1) * NT]
                    P.dma("pool", f"st{oc % 3}{self.sfx}", lambda h, dst=dst, st=st: h.dma_start(out=dst, in_=self.U(st)),
                          reads=[("ar", st)], writes=[("z", oc, j)])
                    oc += 1

    def stage_a_all(self):
        seq = [(j, hd) for j in range(self.NTL) for hd in range(4)]
        self.sa_front(*seq[0])
        for k in range(len(seq)):
            if k + 1 < len(seq):
                self.sa_front(*seq[k + 1])
            self.sa_back(*seq[k])

    def sa_front(self, j, hd):
        P = self.P
        if True:
            rot = hd % 2
            XC = [3 + 3 * rot, 4 + 3 * rot]
            XCB = 5 + 3 * rot
            for mm_ in range(2):
                c = 2 * hd + mm_
                lx = (self.lxi % 3)
                self.lxi += 1
                P.dma("pool", f"ld{lx}{self.sfx}", lambda h, c=c, lx=lx: h.dma_start(out=self.U(lx, NT + 3),
                                                                           in_=self.zlp[c * 128:(c + 1) * 128, j * NT:j * NT + NT + 3]),
                      reads=[("z", c, jj) for jj in (j - 1, j, j + 1)] + ["zpad"], writes=[("ar", lx)])
                xc = XC[mm_]
                P.op("dve", lambda h, c=c, lx=lx, xc=xc: h.tensor_scalar(self.U(xc), self.U(lx, NT, 0), self.cvc("cw", c),
                                                                         self.cvc("cb", c), ALU.mult, ALU.add),
                     reads=[("ar", lx), "cv"], writes=[("ar", xc)])
                for k in range(1, 4):
                    P.op("dve", lambda h, c=c, lx=lx, xc=xc, k=k: h.scalar_tensor_tensor(
                        self.U(xc), self.U(lx, NT, k), self.cvc("cw", k * 8 + c), self.U(xc), ALU.mult, ALU.add),
                        reads=[("ar", lx), ("ar", xc), "cv"], writes=[("ar", xc)])
                P.op("pool", lambda h, xc=xc, mm_=mm_, XCB=XCB: h.tensor_copy(self.B(XCB, mm_), self.U(xc)),
                     reads=[("ar", xc)], writes=[("ar", XCB)])

    def sa_back(self, j, hd):
        P = self.P
        rev = lambda t: bass.AP(t.tensor, t.offset + NT - 1, [list(t.ap[0]), [-1, NT]])
        ts = slice(j * NT, (j + 1) * NT)
        if True:
            rot = hd % 2
            XC = [3 + 3 * rot, 4 + 3 * rot]
            XCB = 5 + 3 * rot
            s = self.wload("A_gates", hd)
            sets = {}
            for mm_ in range(2):
                for d in range(2):
                    S0 = self.SA0 + 4 * ((hd % 2) * 4 + mm_ * 2 + d)
                    sets[(mm_, d)] = (S0, S0 + 1, S0 + 2, S0 + 3)
                    for g in range(2):
                        pb = (d * 2 + g) * 2 + mm_
                        base = ((d * 2 + g) * 2 + mm_) * 2
                        self.mm(self.ps[pb][:], [(self.wblk(s, base + kk), self.B(XCB, kk)) for kk in range(2)],
                                reads=[("wr", s), ("ar", XCB)], writes=[("ps", pb)])
            for mm_ in range(2):
                c = 2 * hd + mm_
                for d in range(2):
                    RA, SH, IU, PP = sets[(mm_, d)]
                    for g in range(2):
                        pb = (d * 2 + g) * 2 + mm_
                        dstu = RA if g == 0 else IU
                        bname = "ba" if g == 0 else "bx"
                        P.op("act", lambda h, pb=pb, dstu=dstu, bname=bname, d=d, c=c: h.activation(
                            self.U(dstu), self.ps[pb][:], AF.Sigmoid, bias=self.cvc(bname, d * 8 + c), scale=1.0),
                            reads=[("ps", pb), "cv"], writes=[("ar", dstu)])
            for mm_ in range(2):
                c = 2 * hd + mm_
                for d in range(2):
                    RA, SH, IU, PP = sets[(mm_, d)]
                    P.op("act", lambda h, d=d, c=c, RA=RA, SH=SH: h.activation(self.U(SH), self.U(RA), AF.Exp, scale=self.cvc("sc2", d * 8 + c)),
                         reads=[("ar", RA), "cv"], writes=[("ar", SH)])
                    P.op("act", lambda h, d=d, c=c, RA=RA: h.activation(self.U(RA), self.U(RA), AF.Exp, scale=self.cvc("sc", d * 8 + c)),
                         reads=[("ar", RA), "cv"], writes=[("ar", RA)])
            for mm_ in range(2):
                for d in range(2):
                    RA, SH, IU, PP = sets[(mm_, d)]
                    P.op("act", lambda h, SH=SH: h.activation(self.U(SH), self.U(SH), AF.Sqrt, bias=self.oneb[:, 0:1], scale=-1.0),
                         reads=[("ar", SH), "ones"], writes=[("ar", SH)])
            for mm_ in range(2):
                c = 2 * hd + mm_
                xc = XC[mm_]
                for d in range(2):
                    RA, SH, IU, PP = sets[(mm_, d)]
                    P.op("dve", lambda h, IU=IU, xc=xc: h.tensor_tensor(self.U(IU), self.U(IU), self.U(xc), ALU.mult),
                         reads=[("ar", IU), ("ar", xc)], writes=[("ar", IU)])
                    P.op("dve", lambda h, IU=IU, SH=SH: h.tensor_tensor(self.U(IU), self.U(IU), self.U(SH), ALU.mult),
                         reads=[("ar", IU), ("ar", SH)], writes=[("ar", IU)])
                    w = (lambda t: t) if d == 0 else rev
                    P.op("dve", lambda h, w=w, SH=SH, RA=RA, IU=IU: h.tensor_tensor_scan(w(self.U(SH)), w(self.U(RA)), w(self.U(IU)), 0.0, ALU.mult, ALU.add),
                         reads=[("ar", RA), ("ar", IU), ("ar", SH)], writes=[("ar", SH)])
                    P.op("dve", lambda h, w=w, PP=PP, RA=RA: h.tensor_tensor_scan(w(self.U(PP)), w(self.U(RA)), w(self.zeros_ap()), 1.0, ALU.mult, ALU.add),
                         reads=[("ar", RA), self.zeros_key], writes=[("ar", PP)])
                    e = NT - 1 if d == 0 else 0
                    for q, src in ((2 * d, PP), (2 * d + 1, SH)):
                        o = (q * DC + c) * self.NTL + j
                        P.op("pool", lambda h, o=o, src=src, e=e: h.tensor_copy(self.summ[:, o:o + 1], self.U(src, 1, e)),
                             reads=[("ar", src)], writes=["summ"])
                (RAf, SHf, IUf, PPf), (RAb, SHb, IUb, PPb) = sets[(mm_, 0)], sets[(mm_, 1)]
                P.op("pool", lambda h, SHf=SHf, SHb=SHb: h.tensor_tensor(self.U(SHf), self.U(SHf), self.U(SHb), ALU.add),
                     reads=[("ar", SHf), ("ar", SHb)], writes=[("ar", SHf)])
                for dst, src in ((self.hs, SHf), (self.pf, PPf), (self.pb, PPb)):
                    k3 = self.sti % 3
                    self.sti += 1
                    P.dma("pool", f"st{k3}{self.sfx}", lambda h, dst=dst, src=src, c=c: h.dma_start(out=dst[c * 128:(c + 1) * 128, ts], in_=self.U(src)),
                          reads=[("ar", src)], writes=[("spill", c, j)])

    def sv(self, q, j):
        t = self.summ
        base = t[:, (q * DC) * self.NTL + j:(q * DC) * self.NTL + j + 1]
        return bass.AP(base.tensor, base.offset, [list(base.ap[0]), [self.NTL, DC]])

    def compose_e2(self):
        P = self.P
        NTL = self.NTL
        e = lambda q: self.e2t[:, q * DC:(q + 1) * DC]
        ops = []
        ops.append(lambda h: h.tensor_copy(e(0), self.sv(0, 0)))
        ops.append(lambda h: h.tensor_copy(e(1), self.sv(1, 0)))
        for j in range(1, NTL):
            ops.append(lambda h, j=j: h.tensor_tensor(e(1), e(1), self.sv(0, j), ALU.mult))
            ops.append(lambda h, j=j: h.tensor_tensor(e(1), e(1), self.sv(1, j), ALU.add))
            ops.append(lambda h, j=j: h.tensor_tensor(e(0), e(0), self.sv(0, j), ALU.mult))
        ops.append(lambda h: h.tensor_copy(e(2), self.sv(2, NTL - 1)))
        ops.append(lambda h: h.tensor_copy(e(3), self.sv(3, NTL - 1)))
        for j in range(NTL - 2, -1, -1):
            ops.append(lambda h, j=j: h.tensor_tensor(e(3), e(3), self.sv(2, j), ALU.mult))
            ops.append(lambda h, j=j: h.tensor_tensor(e(3), e(3), self.sv(3, j), ALU.add))
            ops.append(lambda h, j=j: h.tensor_tensor(e(2), e(2), self.sv(2, j), ALU.mult))
        for f in ops:
            P.op("dve", f, reads=["summ", "e2t"], writes=["e2t"])

    def carries(self):
        P = self.P
        NTL = self.NTL
        r = lambda n, q: self.rcv[:, (n * 4 + q) * DC:(n * 4 + q + 1) * DC]
        tmp = self.e2t[:, 0:DC]

        def ct(d, j):
            b = self.ct[:, (d * DC) * NTL + j:(d * DC) * NTL + j + 1]
            return bass.AP(b.tensor, b.offset, [list(b.ap[0]), [NTL, DC]])
        ops = []
        for d, (n0, qa, qh) in enumerate(((0, 0, 1), (3, 2, 3))):
            ops.append(lambda h, n0=n0, qa=qa, qh=qh: h.tensor_tensor(tmp, r(n0 + 1, qa), r(n0 + 2, qh), ALU.mult))
            ops.append(lambda h, n0=n0, qh=qh: h.tensor_tensor(tmp, tmp, r(n0 + 1, qh), ALU.add))
            ops.append(lambda h, n0=n0, qa=qa: h.tensor_tensor(tmp, tmp, r(n0, qa), ALU.mult))
            j0 = 0 if d == 0 else NTL - 1
            ops.append(lambda h, n0=n0, qh=qh, d=d, j0=j0: h.tensor_tensor(ct(d, j0), tmp, r(n0, qh), ALU.add))
            seq = range(0, NTL - 1) if d == 0 else range(NTL - 1, 0, -1)
            for j in seq:
                jn = j + 1 if d == 0 else j - 1
                ops.append(lambda h, d=d, j=j, jn=jn, qa=qa: h.tensor_tensor(ct(d, jn), ct(d, j), self.sv(qa, j), ALU.mult))
                ops.append(lambda h, d=d, j=j, jn=jn, qh=qh: h.tensor_tensor(ct(d, jn), ct(d, jn), self.sv(qh, j), ALU.add))
        for f in ops:
            P.op("dve", f, reads=["summ", "e2t", "rcv", "ct"], writes=["ct", "e2t"])
        self.ctv = lambda d, c, j: self.ct[:, (d * DC + c) * NTL + j:(d * DC + c) * NTL + j + 1]

    def stage_b_tile(self, j):
        P = self.P
        ts = slice(j * NT, (j + 1) * NT)
        self.make_h(j, "mix_pre", 8)
        hreads = [self.hkey(k) for k in range(DC)]
        YA, YB, YC, MM, UU = 4, 8, 10, 12, 16
        ST = (20, 21, 22)
        WC, WX, TMP = 23, 24, 22
        oc = 0
        nld = len(self.groups["B_z"])
        for li in range(nld):
            s = self.wload("B_z", li)
            nout = self.groups["B_z"][li][1] // DC
            for mi in range(nout):
                pb = oc % 2
                self.mm(self.ps[pb][:], [(self.wblk(s, mi * DC + k), self.hv(k)) for k in range(DC)],
                        reads=[("wr", s)] + hreads, writes=[("ps", pb)])
                if oc < 4:
                    c = oc
                    for (u, src) in ((WC, self.zcp), (WX, self.zxp)):
                        zi = (8 if u == WC else 12) + c
                        P.dma("pool", f"ld{u - 23}{self.sfx}", lambda h, u=u, src=src, c=c: h.dma_start(
                            out=self.U(u, NT + 2), in_=src[c * 128:(c + 1) * 128, j * NT:j * NT + NT + 2]),
                            reads=[("z", zi, jj) for jj in (j - 1, j, j + 1)] + ["zpad"], writes=[("ar", u)])
                    P.op("dve", lambda h: h.tensor_tensor(self.U(WC, NT + 2), self.U(WC, NT + 2), self.U(WX, NT + 2), ALU.mult),
                         reads=[("ar", WC), ("ar", WX)], writes=[("ar", WC)])
                    P.op("dve", lambda h, c=c: h.tensor_scalar(self.U(TMP), self.U(WC, NT, 0), self.cvc("scw", c), None, ALU.mult),
                         reads=[("ar", WC), "cv"], writes=[("ar", TMP)])
                    for k in (1, 2):
                        P.op("dve", lambda h, c=c, k=k: h.scalar_tensor_tensor(self.U(TMP), self.U(WC, NT, k), self.cvc("scw", k * 4 + c),
                                                                               self.U(TMP), ALU.mult, ALU.add),
                             reads=[("ar", WC), ("ar", TMP), "cv"], writes=[("ar", TMP)])
                    P.op("dve", lambda h, c=c, pb=pb: h.tensor_tensor(self.B(YB + c // 2, c % 2), self.U(TMP), self.ps[pb][:], ALU.mult),
                         reads=[("ar", TMP), ("ps", pb)], writes=[("ar", YB + c // 2)])
                elif oc < 8:
                    c = oc - 4
                    P.op("act", lambda h, c=c, pb=pb: h.activation(self.U(UU + c), self.ps[pb][:], AF.Gelu_apprx_tanh),
                         reads=[("ps", pb)], writes=[("ar", UU + c)])
                else:
                    c = oc - 8
                    for u, src in zip(ST, (self.hs, self.pf, self.pb)):
                        P.dma("pool", f"st{u - 20}{self.sfx}", lambda h, u=u, src=src, c=c: h.dma_start(out=self.U(u), in_=src[c * 128:(c + 1) * 128, ts]),
                              reads=[("spill", c, j)], writes=[("ar", u)])
                    P.op("dve", lambda h, c=c: h.scalar_tensor_tensor(self.U(ST[0]), self.U(ST[1]), self.ctv(0, c, j), self.U(ST[0]), ALU.mult, ALU.add),
                         reads=[("ar", ST[0]), ("ar", ST[1]), "ct"], writes=[("ar", ST[0])])
                    P.op("dve", lambda h, c=c: h.scalar_tensor_tensor(self.U(ST[0]), self.U(ST[2]), self.ctv(1, c, j), self.U(ST[0]), ALU.mult, ALU.add),
                         reads=[("ar", ST[0]), ("ar", ST[2]), "ct"], writes=[("ar", ST[0])])
                    P.op("act", lambda h, pb=pb: h.activation(self.U(TMP), self.ps[pb][:], AF.Gelu_apprx_tanh),
                         reads=[("ps", pb)], writes=[("ar", TMP)])
                    P.op("dve", lambda h, c=c: h.tensor_tensor(self.B(YA + c // 2, c % 2), self.U(ST[0]), self.U(TMP), ALU.mult),
                         reads=[("ar", ST[0]), ("ar", TMP)], writes=[("ar", YA + c // 2)])
                oc += 1
        s0 = self.wload("B_sgv", 0)
        s1 = self.wload("B_sgv", 1)
        V, VN = 20, 21
        for n in range(4):
            pb = 2 + n % 2
            pairs = []
            for k in range(DC):
                sl, kk = (s0, k) if k < 6 else (s1, k - 6)
                pairs.append((self.hv(k)[:, n * 128:(n + 1) * 128], self.wblk(sl, kk * 4, 4)))
            self.mm(self.ps[pb][:], pairs, reads=[("wr", s0), ("wr", s1)] + hreads, writes=[("ps", pb)])
            P.op("act", lambda h, pb=pb: h.activation(self.U(V), self.ps[pb][:], AF.Gelu_apprx_tanh),
                 reads=[("ps", pb)], writes=[("ar", V)])
            P.op("dve", lambda h: h.bn_stats(self.bst[:, 0:6], self.U(V)), reads=[("ar", V)], writes=["bst"])
            P.op("dve", lambda h: h.bn_aggr(self.bst[:, 6:8], self.bst[:, 0:6]), reads=["bst"], writes=["bst"])
            P.op("act", lambda h: h.activation(self.bst[:, 8:9], self.bst[:, 7:8], AF.Sqrt, bias=self.eps1[:, 0:1], scale=1.0),
                 reads=["bst", "ones"], writes=["bst"])
            P.op("dve", lambda h: h.reciprocal(self.bst[:, 8:9], self.bst[:, 8:9]), reads=["bst"], writes=["bst"])
            P.op("dve", lambda h: h.tensor_scalar(self.U(V), self.U(V), self.bst[:, 6:7], self.bst[:, 8:9], ALU.subtract, ALU.mult),
                 reads=["bst", ("ar", V)], writes=[("ar", V)])
            P.op("dve", lambda h: h.tensor_tensor(self.U(V), self.U(V), self.lnc[:, 0, :], ALU.mult),
                 reads=[("ar", V), "lnc"], writes=[("ar", V)])
            P.op("dve", lambda h, n=n: h.tensor_tensor(self.B(VN + n // 2, n % 2), self.U(V), self.lnc[:, 1, :], ALU.add),
                 reads=[("ar", V), "lnc"], writes=[("ar", VN + n // 2)])
        for g in range(4):
            pb = 2 + g
            fns = []
            for n in range(4):
                o = self.ps[pb][:, n * 128:(n + 1) * 128]
                fns.append(lambda h, o=o, g=g, n=n: h.matmul(o, self.B(VN + n // 2, n % 2)[:, g * 128:(g + 1) * 128], self.wst[:, g * 128:(g + 1) * 128],
                                                            start=True, stop=False))
                fns.append(lambda h, o=o, g=g: h.matmul(o, self.ones[0:1, :], self.bsr[0:1, g * 128:(g + 1) * 128], start=False, stop=True))
            P.mm_group(fns, reads=[("ar", VN), ("ar", VN + 1), "wst", "ones", "bsr"], writes=[("ps", pb)])
            P.op("dve", lambda h, g=g, pb=pb: h.tensor_tensor(self.B(YC + g // 2, g % 2), self.U(UU + g), self.ps[pb][:], ALU.mult),
                 reads=[("ar", UU + g), ("ps", pb)], writes=[("ar", YC + g // 2)])
        GM = (16, 17, 18)
        T1, T2 = 19, 20
        yar = [("ar", YA + k) for k in range(4)]
        for m in range(DC):
            sg = self.wload("B_mrg", m)
            sb_ = self.wload("B_br", m)
            for br in range(3):
                self.mm(self.ps[br][:], [(self.wblk(sg, br * DC + k), self.hv(k)) for k in range(DC)],
                        reads=[("wr", sg)] + hreads, writes=[("ps", br)])
                P.op("act", lambda h, br=br: h.activation(self.U(GM[br]), self.ps[br][:], AF.Sigmoid),
                     reads=[("ps", br)], writes=[("ar", GM[br])])
            self.mm(self.ps[3][:], [(self.wblk(sb_, k), self.B(YA + k // 2, k % 2)) for k in range(8)],
                    reads=[("wr", sb_)] + yar, writes=[("ps", 3)])
            self.mm(self.ps[4][:], [(self.wblk(sb_, 8 + k), self.B(YB + k // 2, k % 2)) for k in range(4)],
                    reads=[("wr", sb_), ("ar", YB), ("ar", YB + 1)], writes=[("ps", 4)])
            self.mm(self.ps[5][:], [(self.wblk(sb_, 12 + k), self.B(YC + k // 2, k % 2)) for k in range(4)],
                    reads=[("wr", sb_), ("ar", YC), ("ar", YC + 1)], writes=[("ps", 5)])
            P.op("dve", lambda h: h.tensor_tensor(self.U(T1), self.U(GM[0]), self.ps[3][:], ALU.mult),
                 reads=[("ar", GM[0]), ("ps", 3)], writes=[("ar", T1)])
            P.op("dve", lambda h: h.tensor_tensor(self.U(T2), self.U(GM[1]), self.ps[4][:], ALU.mult),
                 reads=[("ar", GM[1]), ("ps", 4)], writes=[("ar", T2)])
            P.op("pool", lambda h: h.tensor_tensor(self.U(T1), self.U(T1), self.U(T2), ALU.add),
                 reads=[("ar", T1), ("ar", T2)], writes=[("ar", T1)])
            P.op("dve", lambda h: h.tensor_tensor(self.U(T2), self.U(GM[2]), self.ps[5][:], ALU.mult),
                 reads=[("ar", GM[2]), ("ps", 5), ("ar", T2)], writes=[("ar", T2)])
            P.op("pool", lambda h, m=m: h.tensor_tensor(self.B(MM + m // 2, m % 2), self.U(T1), self.U(T2), ALU.add),
                 reads=[("ar", T1), ("ar", T2)], writes=[("ar", MM + m // 2)])
        F1 = 4
        mreads = [("ar", MM + k) for k in range(4)]
        oc = 0
        for li in range(3):
            s = self.wload("B_wo", li)
            nout = self.groups["B_wo"][li][1] // DC
            for mi in range(nout):
                pb = oc % 2
                self.mm(self.ps[pb][:], [(self.wblk(s, mi * DC + k), self.B(MM + k // 2, k % 2)) for k in range(DC)],
                        reads=[("wr", s)] + mreads, writes=[("ps", pb)])
                P.op("act", lambda h, pb=pb, oc=oc: h.activation(self.U(F1 + oc), self.ps[pb][:], AF.Copy),
                     reads=[("ps", pb)], writes=[("ar", F1 + oc)])
                oc += 1
        self.post_residual(j, "mix_post", F1, 12)

    def build(self):
        T, NTL, ph = self.T, self.NTL, self.phase
        nc = bass.Bass("TRN2", target_bir_lowering=False)
        self.nc = nc
        P = self.P = Prog()
        ein = lambda n, sh: nc.dram_tensor(n, sh, F32, kind="ExternalInput").ap()
        eout = lambda n, sh: nc.dram_tensor(n, sh, F32, kind="ExternalOutput").ap()
        wf = ein("wf", [self.wsize])
        self.wbf = nc.dram_tensor("wbf", [self.wsize], BF16).ap()
        cvin = ein("cvin", [128, NCV])
        if ph in ("P1", "P3"):
            xin = ein("xin", [D, T])
            xout = eout("xout", [D, T])
        if ph == "P1":
            self.zl, self.zc, self.zx = eout("zl", [D, T]), eout("zc", [512, T]), eout("zx", [512, T])
        if ph == "P2":
            self.zlp = ein("zlp", [D, T + 3])
            self.hs, self.pf, self.pb = eout("hs", [D, T]), eout("pf", [D, T]), eout("pb", [D, T])
            summ_o = eout("summ", [128, 4 * DC * NTL])
            e2_o = eout("e2", [128, 4 * DC])
        if ph == "P3":
            self.hs, self.pf, self.pb = ein("hs", [D, T]), ein("pf", [D, T]), ein("pb", [D, T])
            self.zcp, self.zxp = ein("zcp", [512, T + 2]), ein("zxp", [512, T + 2])
            summ_i = ein("summ", [128, 4 * DC * NTL])
            rcv_i = ein("rcv", [128, 6 * 4 * DC])
            lnc_i = ein("lnc", [128, 2 * 512])
            wst_i = ein("wst", [128, 4 * 128])
            bsr_i = ein("bsr", [1, 512])
        self.CVT = 128 * 8192
        self.sfx = ""
        self.wbfkey = "wbf"
        self.zo = (0, 0)

        for e in ENGS:
            P.new_sem(e)
        for s in range(NSLOT):
            P.new_sem(f"w{s}")
        for n in ("xio", "cvt", "cst", "st0", "st1", "st2", "ld0", "ld1", "ld2"):
            P.new_sem(n)

        from contextlib import ExitStack
        with ExitStack() as es:
            def sb(name, shape, dt):
                return es.enter_context(nc.sbuf_tensor(name, shape, dt))
            if ph in ("P1", "P3"):
                self.X = sb("X", [128, DC, T], F32)
            nu = NU_BIG if ph == "P2" else NU
            self.SA0 = 25
            self.ar = sb("arena", [128, nu, UW], F32)
            self.arb = self.ar[:].bitcast(BF16)
            self.wring = sb("wring", [128, NSLOT, SLOT_BLKS * 128], BF16)
            self.ones = sb("ones", [128, 128], F32)
            self.onesb = sb("onesb", [128, 128], BF16)
            self.epsb = sb("epsb", [128, 1], F32)
            self.eps1 = sb("eps1", [128, 1], F32)
            self.oneb = sb("oneb", [128, 1], F32)
            self.cv = sb("cv", [128, NCV], F32)
            self.summ = sb("summ_t", [128, 4 * DC * NTL], F32)
            self.e2t = sb("e2t", [128, 4 * DC], F32)
            self.ct = sb("ct", [128, 2 * DC * NTL], F32)
            self.bst = sb("bst", [128, 16], F32)
            if ph == "P3":
                self.rcv = sb("rcv_t", [128, 6 * 4 * DC], F32)
                self.lnc = sb("lnc_t", [128, 2, 512], F32)
                self.wst = sb("wst_t", [128, 512], BF16)
                self.bsr = sb("bsr_t", [1, 512], F32)
            self.ps = [es.enter_context(nc.psum_tensor(f"ps{i}", [128, NT], F32)) for i in range(8)]
            self.wslot = 0
            self.lxi = self.seti = self.gpi = self.sti = 0
            sems = {n: es.enter_context(nc.semaphore(n)) for n in P.semnames}

            P.op("pool", lambda h: h.memset(self.epsb[:], float(D * EPS)), writes=["ones"])
            P.op("pool", lambda h: h.memset(self.eps1[:], float(EPS)), writes=["ones"])
            P.op("pool", lambda h: h.memset(self.oneb[:], 1.0), writes=["ones"])
            P.op("pool", lambda h: h.memset(self.ones[:], 1.0), writes=["ones"])
            P.op("pool", lambda h: h.memset(self.onesb[:], 1.0), writes=["ones"])
            P.op("pool", lambda h: h.memset(self.U(21, UW), 0.0), writes=[("ar", 21)])
            P.dma("pool", "cst", lambda h: h.dma_start(out=self.cv[:], in_=cvin[:, :]), writes=["cv"])
            if ph == "P3":
                P.dma("pool", "cst", lambda h: h.dma_start(out=self.rcv[:], in_=rcv_i[:, :]), writes=["rcv"])
                P.dma("pool", "cst", lambda h: h.dma_start(out=self.summ[:], in_=summ_i[:, :]), writes=["summ"])
                P.dma("pool", "cst", lambda h: h.dma_start(out=self.lnc[:], in_=lnc_i.rearrange("p (a b) -> p a b", a=2)), writes=["lnc"])
                P.dma("pool", "cst", lambda h: h.dma_start(out=self.U(0), in_=wst_i[:, :]), writes=["wstf", ("ar", 0)])
                P.dma("pool", "cst", lambda h: h.dma_start(out=self.bsr[:], in_=bsr_i[:, :]), writes=["ones"])
            for k in ("cv", "rcv", "summ", "lnc", "wstf", "ones", ("ar", 0)):
                if k in P.lastw and P.lastw[k][0] == "cst":
                    P.lastw[k] = ("cst", P.cnt["cst"])
            n = self.wsize
            npieces = (n + self.CVT - 1) // self.CVT
            for p in range(npieces):
                a, b = p * self.CVT, min(n, (p + 1) * self.CVT)
                src = wf[a:b].rearrange("(p f) -> p f", p=128)
                dst = self.wbf[a:b].rearrange("(p f) -> p f", p=128)
                P.dma("pool", "cvt", lambda h, src=src, dst=dst: h.dma_start(out=dst, in_=src), writes=["wbf"])
            P.op("dve", lambda h: h.tensor_scalar(self.cv[:, 0:48], self.cv[:, 0:48], 32.0, None, ALU.mult), reads=["cv"], writes=["cv"])
            for nm in ("ffn1_post", "ffn2_post"):
                P.op("dve", lambda h, nm=nm: h.tensor_scalar(self.cv[:, CV[nm]:CV[nm] + 8], self.cv[:, CV[nm]:CV[nm] + 8], 0.5, None, ALU.mult),
                     reads=["cv"], writes=["cv"])
            if ph == "P2":
                sc = self.cv[:, CV["sc"]:CV["sc"] + 16]
                sc2 = self.cv[:, CV["sc2"]:CV["sc2"] + 16]
                lam = self.cv[:, CV["lam"]:CV["lam"] + 16]
                P.op("act", lambda h: h.activation(sc, lam, AF.Exp, scale=-1.0), reads=["cv"], writes=["cv"])
                P.op("act", lambda h: h.activation(sc, sc, AF.Ln, bias=self.oneb[:, 0:1], scale=1.0), reads=["cv", "ones"], writes=["cv"])
                P.op("dve", lambda h: h.tensor_scalar(sc2, sc, -16.0, None, ALU.mult), reads=["cv"], writes=["cv"])
                P.op("dve", lambda h: h.tensor_scalar(sc, sc, -8.0, None, ALU.mult), reads=["cv"], writes=["cv"])
            if ph == "P3":
                P.op("act", lambda h: h.activation(self.wst[:], self.U(0), AF.Copy), reads=["wstf", ("ar", 0)], writes=["wst"])
                self.carries()
            if ph in ("P1", "P3"):
                for j in range(NTL):
                    for c in range(DC):
                        P.dma("sp", "xio", lambda h, c=c, j=j: h.dma_start(out=self.X[:, c, j * NT:(j + 1) * NT],
                                                                           in_=xin[c * 128:(c + 1) * 128, j * NT:(j + 1) * NT]),
                              writes=[("X", c, j)])
                for j in range(NTL):
                    for c in range(DC):
                        P.lastw[("X", c, j)] = ("xio", P.cnt["xio"])
            if ph == "P1":
                for j in range(NTL):
                    self.ffn_tile(j, "ffn1")
                for j in range(NTL):
                    self.a0_tile(j)
            elif ph == "P2":
                self.stage_a_all()
                self.compose_e2()
                P.dma("pool", "cst", lambda h: h.dma_start(out=summ_o[:, :], in_=self.summ[:]), reads=["summ"])
                P.dma("pool", "cst", lambda h: h.dma_start(out=e2_o[:, :], in_=self.e2t[:]), reads=["e2t"])
            else:
                for j in range(NTL):
                    self.stage_b_tile(j)
                for j in range(NTL):
                    self.ffn_tile(j, "ffn2")
            if ph in ("P1", "P3"):
                for j in range(NTL):
                    for c in range(DC):
                        P.dma("sp", "xio", lambda h, c=c, j=j: h.dma_start(out=xout[c * 128:(c + 1) * 128, j * NT:(j + 1) * NT],
                                                                           in_=self.X[:, c, j * NT:(j + 1) * NT]),
                              reads=[("X", c, j)])
            P.wait_all("sp", ["xio"])
            P.wait_all("pool", ["cvt", "cst", "st0", "st1", "st2", "ld0", "ld1", "ld2"])

            with nc.Block() as block:
                @block.sync
                def _(e):
                    P.replay("sp", e, sems)

                @block.gpsimd
                def _(e):
                    P.replay("pool", e, sems)

                @block.tensor
                def _(e):
                    P.replay("pe", e, sems)

                @block.scalar
                def _(e):
                    P.replay("act", e, sems)

                @block.vector
                def _(e):
                    P.replay("dve", e, sems)
        return nc


class FusedBuilder(Builder):
    def __init__(self, T, L, wsizes, lgroups):
        self.T = T
        self.NTL = T // NT
        self.L = L
        self.phase = "F"
        self.wsizes = wsizes
        self.lgroups = lgroups

    zeros_key = "zeros"

    def zeros_ap(self):
        return self.zt[:]

    def Xv(self, c, j):
        return self.XT[:, j % 2, c, :]

    def Xk(self, c, j):
        return ("XT", j % 2, c)

    def _xload(self, j):
        if j in self.xloaded or j >= self.NTL:
            return
        self.xloaded.add(j)
        sl = j % 2
        src = self.xsrc.rearrange("(c p) t -> p c t", p=128)[:, :, j * NT:(j + 1) * NT]
        self.P.dma("sp", f"xl{sl}{self.sfx}", lambda h, sl=sl, src=src: h.dma_start(out=self.XT[:, sl, :, :], in_=src),
                   reads=[(self.xsrc_key, j)], writes=[("XT", sl, c) for c in range(DC)])

    def x_need(self, j):
        self._xload(j)
        self._xload(j + 1)

    def x_done(self, j):
        sl = j % 2
        dst = self.xdst.rearrange("(c p) t -> p c t", p=128)[:, :, j * NT:(j + 1) * NT]
        self.P.dma("pool", f"xs{sl}{self.sfx}", lambda h, sl=sl, dst=dst: h.dma_start(out=dst, in_=self.XT[:, sl, :, :]),
                   reads=[("XT", sl, c) for c in range(DC)], writes=[(self.xdst_key, j)])

    def build(self):
        T, NTL, L = self.T, self.NTL, self.L
        nc = bass.Bass("TRN2", target_bir_lowering=False)
        self.nc = nc
        P = self.P = Prog()
        ein = lambda n, sh: nc.dram_tensor(n, sh, F32, kind="ExternalInput").ap()
        xin = ein("xin", [D, T])
        xout = nc.dram_tensor("xout", [D, T], F32, kind="ExternalOutput").ap()
        wfs = [ein(f"wf{l}", [self.wsizes[l]]) for l in range(L)]
        self.wbfs = [nc.dram_tensor(f"wbf{l}", [self.wsizes[l]], BF16).ap() for l in range(L)]
        cvin = ein("cvin", [L * 128, NCV])
        lnc_i = ein("lnc", [L * 128, 2 * 512])
        wst_i = ein("wst", [L * 128, 4 * 128])
        bsr_i = ein("bsr", [L, 512])
        scr = lambda n, sh: nc.dram_tensor(n, sh, F32).ap()
        Xd = scr("Xd", [D, T])
        self.zlp = self.zl = scr("zlp", [D, T + 3])
        self.zcp = self.zc = scr("zcp", [512, T + 2])
        self.zxp = self.zx = scr("zxp", [512, T + 2])
        self.hs, self.pf, self.pb = scr("hs", [D, T]), scr("pf", [D, T]), scr("pb", [D, T])
        self.zo = (2, 1)
        self.CVT = 128 * 8192

        for e in ENGS:
            P.new_sem(e)
        P.new_sem("cst")
        for l in range(L):
            P.new_sem(f"cvt{l}")
            P.new_sem(f"cst_{l}")
            for s in range(NSLOT):
                P.new_sem(f"w{s}_{l}")
            for n in ("st0", "st1", "st2", "ld0", "ld1", "ld2", "xl0", "xl1", "xs0", "xs1"):
                P.new_sem(f"{n}_{l}")

        from contextlib import ExitStack
        with ExitStack() as es:
            def sb(name, shape, dt):
                return es.enter_context(nc.sbuf_tensor(name, shape, dt))
            self.XT = sb("XT", [128, 2, DC, NT], F32)
            self.zt = sb("zt", [128, NT], F32)
            self.SA0 = 25
            self.ar = sb("arena", [128, NU_BIG, UW], F32)
            self.arb = self.ar[:].bitcast(BF16)
            self.wring = sb("wring", [128, NSLOT, SLOT_BLKS * 128], BF16)
            self.ones = sb("ones", [128, 128], F32)
            self.onesb = sb("onesb", [128, 128], BF16)
            self.epsb = sb("epsb", [128, 1], F32)
            self.eps1 = sb("eps1", [128, 1], F32)
            self.oneb = sb("oneb", [128, 1], F32)
            self.cv = sb("cv", [128, NCV], F32)
            self.summ = sb("summ_t", [128, 4 * DC * NTL], F32)
            self.e2t = sb("e2t", [128, 4 * DC], F32)
            self.ct = sb("ct", [128, 2 * DC * NTL], F32)
            self.bst = sb("bst", [128, 16], F32)
            self.rcv = sb("rcv_t", [128, 6 * 4 * DC], F32)
            self.lnc = sb("lnc_t", [128, 2, 512], F32)
            self.wst = sb("wst_t", [128, 512], BF16)
            self.bsr = sb("bsr_t", [1, 512], F32)
            self.ps = [es.enter_context(nc.psum_tensor(f"ps{i}", [128, NT], F32)) for i in range(8)]
            self.wslot = 0
            self.lxi = self.seti = self.gpi = self.sti = 0
            sems = {n: es.enter_context(nc.semaphore(n)) for n in P.semnames}

            P.op("pool", lambda h: h.memset(self.epsb[:], float(D * EPS)), writes=["ones"])
            P.op("pool", lambda h: h.memset(self.eps1[:], float(EPS)), writes=["ones"])
            P.op("pool", lambda h: h.memset(self.oneb[:], 1.0), writes=["ones"])
            P.op("pool", lambda h: h.memset(self.ones[:], 1.0), writes=["ones"])
            P.op("pool", lambda h: h.memset(self.onesb[:], 1.0), writes=["ones"])
            P.op("pool", lambda h: h.memset(self.rcv[:], 0.0), writes=["rcv"])
            P.op("pool", lambda h: h.memset(self.zt[:], 0.0), writes=["zeros"])
            P.op("pool", lambda h: h.memset(self.U(21, UW), 0.0), writes=[("ar", 21)])
            for (zt, nch, cols) in ((self.zlp, DC, ((0, 2), (T + 2, T + 3))), (self.zcp, 4, ((0, 1), (T + 1, T + 2))),
                                    (self.zxp, 4, ((0, 1), (T + 1, T + 2)))):
                for (a, b) in cols:
                    dst = zt.rearrange("(c p) t -> p c t", p=128)[:, :, a:b]
                    srcz = self.ar[:, 21, 0:nch * (b - a)].rearrange("p (c t) -> p c t", c=nch)
                    P.dma("pool", "cst", lambda h, dst=dst, srcz=srcz: h.dma_start(out=dst, in_=srcz, allow_slow_non_contiguous=True),
                          reads=[("ar", 21)], writes=["zpad"])
            P.lastw["zpad"] = ("cst", P.cnt["cst"])
            for l in range(L):
                n = self.wsizes[l]
                npieces = (n + self.CVT - 1) // self.CVT
                for p in range(npieces):
                    a, b = p * self.CVT, min(n, (p + 1) * self.CVT)
                    src = wfs[l][a:b].rearrange("(p f) -> p f", p=128)
                    dst = self.wbfs[l][a:b].rearrange("(p f) -> p f", p=128)
                    P.dma("pool", f"cvt{l}", lambda h, src=src, dst=dst: h.dma_start(out=dst, in_=src), writes=[("wbf", l)])

            for l in range(L):
                self.sfx = f"_{l}"
                self.groups = self.lgroups[l]
                self.wbf = self.wbfs[l]
                self.wbfkey = ("wbf", l)
                cs = f"cst_{l}"
                P.dma("pool", cs, lambda h, l=l: h.dma_start(out=self.cv[:], in_=cvin[l * 128:(l + 1) * 128, :]), writes=["cv"])
                P.dma("pool", cs, lambda h, l=l: h.dma_start(out=self.lnc[:], in_=lnc_i[l * 128:(l + 1) * 128, :].rearrange("p (a b) -> p a b", a=2)),
                      writes=["lnc"])
                P.dma("pool", cs, lambda h, l=l: h.dma_start(out=self.U(0), in_=wst_i[l * 128:(l + 1) * 128, :]), writes=["wstf", ("ar", 0)])
                P.dma("pool", cs, lambda h, l=l: h.dma_start(out=self.bsr[:], in_=bsr_i[l:l + 1, :]), writes=["bsr"])
                for k in ("cv", "lnc", "wstf", "bsr", ("ar", 0)):
                    P.lastw[k] = (cs, P.cnt[cs])
                P.op("dve", lambda h: h.tensor_scalar(self.cv[:, 0:48], self.cv[:, 0:48], 32.0, None, ALU.mult), reads=["cv"], writes=["cv"])
                for nm in ("ffn1_post", "ffn2_post"):
                    P.op("dve", lambda h, nm=nm: h.tensor_scalar(self.cv[:, CV[nm]:CV[nm] + 8], self.cv[:, CV[nm]:CV[nm] + 8], 0.5, None, ALU.mult),
                         reads=["cv"], writes=["cv"])
                sc = self.cv[:, CV["sc"]:CV["sc"] + 16]
                sc2 = self.cv[:, CV["sc2"]:CV["sc2"] + 16]
                lam = self.cv[:, CV["lam"]:CV["lam"] + 16]
                P.op("act", lambda h: h.activation(sc, lam, AF.Exp, scale=-1.0), reads=["cv"], writes=["cv"])
                P.op("act", lambda h: h.activation(sc, sc, AF.Ln, bias=self.oneb[:, 0:1], scale=1.0), reads=["cv", "ones"], writes=["cv"])
                P.op("dve", lambda h: h.tensor_scalar(sc2, sc, -16.0, None, ALU.mult), reads=["cv"], writes=["cv"])
                P.op("dve", lambda h: h.tensor_scalar(sc, sc, -8.0, None, ALU.mult), reads=["cv"], writes=["cv"])
                P.op("act", lambda h: h.activation(self.wst[:], self.U(0), AF.Copy), reads=["wstf", ("ar", 0)], writes=["wst"])
                self.xsrc, self.xsrc_key = (xin, "Xin") if l == 0 else (Xd, "Xd")
                self.xdst, self.xdst_key = Xd, "Xd"
                self.xloaded = set()
                for j in range(NTL):
                    self.ffn_tile(j, "ffn1")
                self.xsrc, self.xsrc_key = Xd, "Xd"
                self.xloaded = set()
                for j in range(NTL):
                    self.a0_tile(j)
                self.stage_a_all()
                self.carries()
                self.xloaded = set()
                for j in range(NTL):
                    self.stage_b_tile(j)
                if l == L - 1:
                    self.xdst, self.xdst_key = xout, "Xout"
                self.xloaded = set()
                for j in range(NTL):
                    self.ffn_tile(j, "ffn2")
            P.wait_all("sp", [n for n in P.semnames if n[:2] in ("xl",) or n[0] == "w"])
            P.wait_all("pool", [n for n in P.semnames if n[:2] in ("cv", "cs", "st", "ld", "xs")])

            with nc.Block() as block:
                @block.sync
                def _(e):
                    P.replay("sp", e, sems)

                @block.gpsimd
                def _(e):
                    P.replay("pool", e, sems)

                @block.tensor
                def _(e):
                    P.replay("pe", e, sems)

                @block.scalar
                def _(e):
                    P.replay("act", e, sems)

                @block.vector
                def _(e):
                    P.replay("dve", e, sems)
        return nc


_NC_CACHE = {}


def _get_nc(T, phase, wsize, groups):
    key = (T, phase, wsize)
    if key not in _NC_CACHE:
        _NC_CACHE[key] = Builder(T, phase, wsize, groups).build()
    return _NC_CACHE[key]


def _cvin(inp, l):
    cv = np.zeros((128, NCV), np.float32)
    for nm in ("ffn1_pre", "ffn1_post", "mix_pre", "mix_post", "ffn2_pre", "ffn2_post"):
        cv[:, CV[nm]:CV[nm] + 8] = chan_major(inp[nm + "_g"][l])
    for k in range(4):
        cv[:, CV["cw"] + k * 8:CV["cw"] + (k + 1) * 8] = chan_major(inp["lru_conv_w"][l, k])
    cv[:, CV["cb"]:CV["cb"] + 8] = chan_major(inp["lru_conv_b"][l])
    for d in range(2):
        cv[:, CV["ba"] + d * 8:CV["ba"] + (d + 1) * 8] = chan_major(inp["lru_ba"][l, d])
        cv[:, CV["bx"] + d * 8:CV["bx"] + (d + 1) * 8] = chan_major(inp["lru_bx"][l, d])
        cv[:, CV["lam"] + d * 8:CV["lam"] + (d + 1) * 8] = chan_major(inp["lru_lambda"][l, d])
    for k in range(3):
        cv[:, CV["scw"] + k * 4:CV["scw"] + (k + 1) * 4] = chan_major(inp["sc_conv_w"][l, k])
    return cv


def _pad_halo(zs, nl, nr):
    out = []
    for c in range(NCORES):
        z = zs[c]
        C, T = z.shape
        p = np.zeros((C, nl + T + nr), np.float32)
        p[:, nl:nl + T] = z
        if c % 4 != 0:
            p[:, 0:nl] = zs[c - 1][:, T - nl:T]
        if c % 4 != 3:
            p[:, nl + T:] = zs[c + 1][:, 0:nr]
        out.append(p)
    return out


FUSED = True


def _pack_layer(inp, l):
    wp = WPack()
    pack_ffn(wp, "ffn1", inp["ffn1_w_gate"][l], inp["ffn1_w_up"][l], inp["ffn1_w_down"][l])
    pack_a0(wp, inp["w_in"][l])
    pack_a(wp, inp["lru_wa"][l], inp["lru_wx"][l])
    pack_b(wp, inp["w_in"][l], inp["lru_w_out"][l], inp["sc_w_out"][l], inp["sgu_w_out"][l], inp["w_o"][l])
    pack_ffn(wp, "ffn2", inp["ffn2_w_gate"][l], inp["ffn2_w_up"][l], inp["ffn2_w_down"][l])
    return wp


def kernel_fused(inp):
    x = inp["x"]
    Bsz, S, _ = x.shape
    L = inp["w_in"].shape[0]
    T = S
    packs = [_pack_layer(inp, l) for l in range(L)]
    flats = [wp.flat() for wp in packs]
    key = ("F", T, L)
    if key not in _NC_CACHE:
        _NC_CACHE[key] = FusedBuilder(T, L, [f.size for f in flats], [wp.groups for wp in packs]).build()
    nc = _NC_CACHE[key]
    cvin = np.concatenate([_cvin(inp, l) for l in range(L)], axis=0)
    lnc = np.concatenate([np.concatenate([np.broadcast_to(inp["sgu_ln_g"][l][None, :], (128, 512)),
                                          np.broadcast_to(inp["sgu_ln_b"][l][None, :], (128, 512))], axis=1) for l in range(L)],
                         axis=0).astype(np.float32)
    wst = np.concatenate([np.transpose(inp["sgu_w_s"][l], (2, 0, 1)).reshape(128, 512) for l in range(L)], axis=0).astype(np.float32)
    bsr = np.ascontiguousarray(inp["sgu_b"].reshape(L, 512)).astype(np.float32)
    common = {"cvin": np.ascontiguousarray(cvin), "lnc": np.ascontiguousarray(lnc), "wst": np.ascontiguousarray(wst), "bsr": bsr}
    for l in range(L):
        common[f"wf{l}"] = flats[l]
    cores = list(range(Bsz))
    in_maps = [dict(common, xin=np.ascontiguousarray(x[b].T)) for b in cores]
    res = run_bass_kernel_spmd(nc, in_maps, core_ids=cores).results
    out = np.zeros((Bsz, S, D), np.float32)
    for b in cores:
        out[b] = res[b]["xout"].T
    return out


def kernel(**inp):
    inp = {k: np.asarray(v) for k, v in inp.items()}
    if FUSED:
        return kernel_fused(inp)
    return kernel_unfused(inp)


def kernel_unfused(inp):
    x = inp["x"]
    Bsz, S, _ = x.shape
    L = inp["w_in"].shape[0]
    T = S * Bsz // NCORES
    cps = NCORES // Bsz
    assert cps == 4
    cores = list(range(NCORES))
    Xs = [np.ascontiguousarray(x[c // cps, (c % cps) * T:(c % cps + 1) * T, :].T) for c in cores]
    for l in range(L):
        cv = _cvin(inp, l)
        wp = WPack()
        pack_ffn(wp, "ffn1", inp["ffn1_w_gate"][l], inp["ffn1_w_up"][l], inp["ffn1_w_down"][l])
        pack_a0(wp, inp["w_in"][l])
        flat = wp.flat()
        nc = _get_nc(T, "P1", flat.size, wp.groups)
        res = run_bass_kernel_spmd(nc, [{"xin": Xs[c], "wf": flat, "cvin": cv} for c in cores], core_ids=cores).results
        Xs = [res[c]["xout"] for c in cores]
        zlp = _pad_halo([res[c]["zl"] for c in cores], 2, 1)
        zcp = _pad_halo([res[c]["zc"] for c in cores], 1, 1)
        zxp = _pad_halo([res[c]["zx"] for c in cores], 1, 1)
        wp = WPack()
        pack_a(wp, inp["lru_wa"][l], inp["lru_wx"][l])
        flat = wp.flat()
        nc = _get_nc(T, "P2", flat.size, wp.groups)
        res = run_bass_kernel_spmd(nc, [{"zlp": zlp[c], "wf": flat, "cvin": cv} for c in cores], core_ids=cores).results
        hs = [res[c]["hs"] for c in cores]
        pf = [res[c]["pf"] for c in cores]
        pb = [res[c]["pb"] for c in cores]
        summ = [res[c]["summ"] for c in cores]
        e2 = [res[c]["e2"] for c in cores]
        rcv = []
        for c in cores:
            r = np.zeros((128, 6, 4 * DC), np.float32)
            for n in range(3):
                if c % cps - n - 1 >= 0:
                    r[:, n] = e2[c - n - 1]
                if c % cps + n + 1 < cps:
                    r[:, 3 + n] = e2[c + n + 1]
            rcv.append(r.reshape(128, -1))
        wp = WPack()
        pack_b(wp, inp["w_in"][l], inp["lru_w_out"][l], inp["sc_w_out"][l], inp["sgu_w_out"][l], inp["w_o"][l])
        pack_ffn(wp, "ffn2", inp["ffn2_w_gate"][l], inp["ffn2_w_up"][l], inp["ffn2_w_down"][l])
        flat = wp.flat()
        lnc = np.concatenate([np.broadcast_to(inp["sgu_ln_g"][l][None, :], (128, 512)),
                              np.broadcast_to(inp["sgu_ln_b"][l][None, :], (128, 512))], axis=1).astype(np.float32)
        wst = np.ascontiguousarray(np.transpose(inp["sgu_w_s"][l], (2, 0, 1))).reshape(128, 512).astype(np.float32)
        bsr = np.ascontiguousarray(inp["sgu_b"][l].reshape(1, 512)).astype(np.float32)
        nc = _get_nc(T, "P3", flat.size, wp.groups)
        res = run_bass_kernel_spmd(nc, [{"xin": Xs[c], "wf": flat, "cvin": cv, "hs": hs[c], "pf": pf[c], "pb": pb[c],
                                         "zcp": zcp[c], "zxp": zxp[c], "summ": summ[c], "rcv": rcv[c],
                                         "lnc": np.ascontiguousarray(lnc), "wst": wst, "bsr": bsr} for c in cores],
                                   core_ids=cores).results
        Xs = [res[c]["xout"] for c in cores]
    out = np.zeros((Bsz, S, D), np.float32)
    for c in cores:
        out[c // cps, (c % cps) * T:(c % cps + 1) * T, :] = Xs[c].T
    return out
```

```python
import numpy as np
import ml_dtypes
import concourse.bass as bass
import concourse.mybir as mybir
from concourse.bass_utils import run_bass_kernel_spmd

F32 = mybir.dt.float32
BF16 = mybir.dt.bfloat16
AF = mybir.ActivationFunctionType
ALU = mybir.AluOpType

D = 1024
DC = 8
FF = 2816
FC = 22
NT = 512
EPS = 1e-6
DIN = 7680
NCORES = 8
SLOT_BLKS = 24
NSLOT = 3

ENGS = ("pe", "act", "dve", "pool", "sp")


class Prog:
    def __init__(self):
        self.q = {e: [] for e in ENGS}
        self.cnt = {}
        self.waited = {e: {} for e in ENGS}
        self.lastw = {}
        self.readers = {}
        self.sem = {}
        self.semnames = []

    def new_sem(self, name):
        self.semnames.append(name)
        self.cnt[name] = 0
        return name

    def _deps(self, eng, reads, writes, skip_same=False):
        need = {}
        for k in reads:
            t = self.lastw.get(k)
            if t:
                need[t[0]] = max(need.get(t[0], 0), t[1])
        for k in writes:
            t = self.lastw.get(k)
            if t:
                need[t[0]] = max(need.get(t[0], 0), t[1])
            for e, c in self.readers.get(k, {}).items():
                if e == eng:
                    continue
                need[e] = max(need.get(e, 0), c)
        for p, c in need.items():
            if p == eng and (skip_same or eng == "pe"):
                continue
            if self.waited[eng].get(p, 0) < c:
                self.waited[eng][p] = c
                self.q[eng].append(("wait", p, c))

    def op(self, eng, fn, reads=(), writes=()):
        self._deps(eng, reads, writes)
        self.cnt[eng] += 1
        c = self.cnt[eng]
        self.q[eng].append(("op", fn, eng, 1))
        for k in reads:
            self.readers.setdefault(k, {})[eng] = c
        for k in writes:
            self.lastw[k] = (eng, c)
            self.readers[k] = {}
        return (eng, c)

    def mm_group(self, fns, reads=(), writes=()):
        self._deps("pe", reads, writes)
        self.cnt["pe"] += 1
        c = self.cnt["pe"]
        for f in fns[:-1]:
            self.q["pe"].append(("raw", f))
        self.q["pe"].append(("op", fns[-1], "pe", 1))
        for k in reads:
            self.readers.setdefault(k, {})["pe"] = c
        for k in writes:
            self.lastw[k] = ("pe", c)
            self.readers[k] = {}

    def dma(self, queue, semname, fn, reads=(), writes=()):
        self._deps(queue, reads, writes)
        if semname[:2] in ("st", "ld", "xl", "xs") and self.cnt[semname] > self.waited[queue].get(semname, 0):
            self.waited[queue][semname] = self.cnt[semname]
            self.q[queue].append(("wait", semname, self.cnt[semname]))
        self.cnt[semname] += 16
        c = self.cnt[semname]
        self.q[queue].append(("op", fn, semname, 16))
        for k in reads:
            self.readers.setdefault(k, {})[semname] = c
        for k in writes:
            self.lastw[k] = (semname, c)
            self.readers[k] = {}
        return (semname, c)

    def wait_all(self, eng, names):
        for p in names:
            c = self.cnt[p]
            if c and self.waited[eng].get(p, 0) < c:
                self.waited[eng][p] = c
                self.q[eng].append(("wait", p, c))

    def replay(self, eng, h, sems):
        for it in self.q[eng]:
            if it[0] == "wait":
                h.wait_ge(sems[it[1]], it[2])
            elif it[0] == "raw":
                it[1](h)
            else:
                it[1](h).then_inc(sems[it[2]], it[3])


def _blk(W, kc, mc):
    return W[kc * 128:(kc + 1) * 128, mc * 128:(mc + 1) * 128]


def _load_img(blocks):
    return np.concatenate(blocks, axis=1)


class WPack:
    def __init__(self):
        self.parts = []
        self.off = 0
        self.groups = {}

    def add(self, group, blocks):
        assert len(blocks) <= SLOT_BLKS
        img = _load_img(blocks).astype(np.float32)
        self.groups.setdefault(group, []).append((self.off, len(blocks)))
        self.parts.append(img.reshape(-1))
        self.off += img.size

    def flat(self):
        return np.concatenate(self.parts)


def pack_ffn(wp, pref, wg, wu, wd):
    for j in range(FC):
        wp.add(pref + "gu", [_blk(wg, k, j) for k in range(DC)] + [_blk(wu, k, j) for k in range(DC)])
    for m in range(DC):
        wp.add(pref + "dn", [_blk(wd, k, m) for k in range(FC)])


ZC = dict(gate=0, lrux=8, scb=16, scc=20, scx=24, sgu=28, sgv=32, mrg=36)


def pack_a0(wp, w_in):
    for grp in ([0, 1, 2], [3, 4, 5], [6, 7]):
        wp.add("A_lrux", [_blk(w_in, k, ZC["lrux"] + m) for m in grp for k in range(DC)])
    for grp in ([20, 21, 22], [23, 24, 25], [26, 27]):
        wp.add("A_sccx", [_blk(w_in, k, m) for m in grp for k in range(DC)])


def pack_a(wp, wa, wx):
    for hd in range(4):
        bl = []
        for d in range(2):
            for W in (wa, wx):
                Wh = W[d, hd]
                for mm in range(2):
                    for kk in range(2):
                        bl.append(_blk(Wh, kk, mm))
        wp.add("A_gates", bl)


BZ_COLS = list(range(16, 20)) + list(range(28, 32)) + list(range(0, 8))


def pack_b(wp, w_in, lru_w_out, sc_w_out, sgu_w_out, w_o):
    for i in range(0, len(BZ_COLS), 3):
        grp = BZ_COLS[i:i + 3]
        wp.add("B_z", [_blk(w_in, k, m) for m in grp for k in range(DC)])
    for kg in ([0, 1, 2, 3, 4, 5], [6, 7]):
        wp.add("B_sgv", [_blk(w_in, k, ZC["sgv"] + m) for k in kg for m in range(4)])
    for m in range(DC):
        wp.add("B_mrg", [_blk(w_in, k, ZC["mrg"] + br * 8 + m) for br in range(3) for k in range(DC)])
        wp.add("B_br", [_blk(lru_w_out, k, m) for k in range(8)] + [_blk(sc_w_out, k, m) for k in range(4)]
               + [_blk(sgu_w_out, k, m) for k in range(4)])
    for grp in ([0, 1, 2], [3, 4, 5], [6, 7]):
        wp.add("B_wo", [_blk(w_o, k, m) for m in grp for k in range(DC)])


def chan_major(v):
    v = np.asarray(v, np.float32)
    return np.ascontiguousarray(v.reshape(-1, 128).T)


NU = 25
NU_BIG = 25 + 32
UW = 516
NCV = 180
CV = dict(ffn1_pre=0, ffn1_post=8, mix_pre=16, mix_post=24, ffn2_pre=32, ffn2_post=40,
          cw=48, cb=80, ba=88, bx=104, lam=120, scw=136, sc=148, sc2=164)


class Builder:
    def __init__(self, T, phase, wsize, groups, last=False):
        self.T = T
        self.NTL = T // NT
        self.phase = phase
        self.groups = groups
        self.wsize = wsize

    def U(self, i, w=NT, off=0):
        return self.ar[:, i, off:off + w]

    def B(self, i, half):
        return self.arb[:, i, half * NT:(half + 1) * NT]

    def hv(self, k):
        return self.B(k // 2, k % 2)

    def hkey(self, k):
        return ("ar", k // 2)

    def cvc(self, name, idx):
        o = CV[name] + idx
        return self.cv[:, o:o + 1]

    def mm(self, out, pairs, reads, writes):
        n = len(pairs)
        self.P.mm_group([lambda h, a=a, b=b, i=i: h.matmul(out, a, b, start=(i == 0), stop=(i == n - 1))
                         for i, (a, b) in enumerate(pairs)], reads=reads, writes=writes)

    def wload(self, group, idx):
        P = self.P
        off, nblk = self.groups[group][idx]
        s = self.wslot
        self.wslot = (s + 1) % NSLOT
        dst = self.wring[:, s, 0:nblk * 128]
        src = self.wbf[off:off + nblk * 128 * 128].rearrange("(p f) -> p f", p=128)
        P.dma("sp", f"w{s}{self.sfx}", lambda h, dst=dst, src=src: h.dma_start(out=dst, in_=src),
              reads=[self.wbfkey], writes=[("wr", s)])
        return s

    def wblk(self, s, i, n=1):
        return self.wring[:, s, i * 128:(i + n) * 128]

    def norm_stats(self, src_fn, src_keys, hl):
        P = self.P
        RS = 24
        for c in range(DC):
            u = hl + c % 2
            P.op("act", lambda h, c=c, u=u: h.activation(self.B(u, 0), src_fn(c), AF.Square),
                 reads=[src_keys(c)], writes=[("ar", u)])
            P.mm_group([lambda h, c=c, u=u: h.matmul(self.ps[6][:], self.onesb[:], self.B(u, 0),
                                                     start=(c == 0), stop=(c == DC - 1))],
                       reads=[("ar", u), "ones"], writes=[("ps", 6)])
        P.op("act", lambda h: h.activation(self.U(RS), self.ps[6][:], AF.Sqrt, bias=self.epsb[:, 0:1], scale=1.0),
             reads=[("ps", 6), "ones"], writes=[("ar", RS)])
        P.op("dve", lambda h: h.reciprocal(self.U(RS), self.U(RS)), reads=[("ar", RS)], writes=[("ar", RS)])

    def Xv(self, c, j):
        return self.X[:, c, j * NT:(j + 1) * NT]

    def Xk(self, c, j):
        return ("X", c, j)

    zeros_key = ("ar", 21)

    def zeros_ap(self):
        return self.U(21)

    def x_need(self, j):
        pass

    def x_done(self, j):
        pass

    def make_h(self, j, gname, hl):
        P = self.P
        self.x_need(j)
        ts = slice(j * NT, (j + 1) * NT)
        self.norm_stats(lambda c: self.Xv(c, j), lambda c: self.Xk(c, j), hl)
        for c in range(DC):
            P.op("dve", lambda h, c=c: h.scalar_tensor_tensor(self.hv(c), self.Xv(c, j), self.cvc(gname, c),
                                                              self.U(24), ALU.mult, ALU.mult),
                 reads=[self.Xk(c, j), ("ar", 24), "cv"], writes=[self.hkey(c)])

    def post_residual(self, j, gname, f1u, hl):
        P = self.P
        ts = slice(j * NT, (j + 1) * NT)
        self.norm_stats(lambda c: self.U(f1u + c), lambda c: ("ar", f1u + c), hl)
        for c in range(DC):
            P.op("dve", lambda h, c=c: h.scalar_tensor_tensor(self.U(f1u + c), self.U(f1u + c), self.cvc(gname, c),
                                                              self.U(24), ALU.mult, ALU.mult),
                 reads=[("ar", f1u + c), ("ar", 24), "cv"], writes=[("ar", f1u + c)])
            P.op("pool", lambda h, c=c: h.tensor_tensor(self.Xv(c, j), self.Xv(c, j), self.U(f1u + c), ALU.add),
                 reads=[("ar", f1u + c), self.Xk(c, j)], writes=[self.Xk(c, j)])
        self.x_done(j)

    def ffn_tile(self, j, pref):
        P = self.P
        ACT0, F1, SIL = 4, 15, 22
        self.make_h(j, pref + "_pre", 4)
        hreads = [self.hkey(k) for k in range(DC)]
        for f in range(FC):
            s = self.wload(pref + "gu", f)
            pb = f % 2
            self.mm(self.ps[pb][:], [(self.wblk(s, k), self.hv(k)) for k in range(DC)],
                    reads=[("wr", s)] + hreads, writes=[("ps", pb)])
            self.mm(self.ps[2 + pb][:], [(self.wblk(s, DC + k), self.hv(k)) for k in range(DC)],
                    reads=[("wr", s)] + hreads, writes=[("ps", 2 + pb)])
            P.op("act", lambda h, pb=pb: h.activation(self.U(SIL), self.ps[pb][:], AF.Silu),
                 reads=[("ps", pb)], writes=[("ar", SIL)])
            P.op("dve", lambda h, pb=pb, f=f: h.tensor_tensor(self.B(ACT0 + f // 2, f % 2), self.U(SIL), self.ps[2 + pb][:], ALU.mult),
                 reads=[("ar", SIL), ("ps", 2 + pb)], writes=[("ar", ACT0 + f // 2)])
        areads = [("ar", ACT0 + k) for k in range(FC // 2)]
        for m in range(DC):
            s = self.wload(pref + "dn", m)
            pb = 4 + m % 2
            self.mm(self.ps[pb][:], [(self.wblk(s, k), self.B(ACT0 + k // 2, k % 2)) for k in range(FC)],
                    reads=[("wr", s)] + areads, writes=[("ps", pb)])
            P.op("act", lambda h, pb=pb, m=m: h.activation(self.U(F1 + m), self.ps[pb][:], AF.Copy),
                 reads=[("ps", pb)], writes=[("ar", F1 + m)])
        self.post_residual(j, pref + "_post", F1, 4)

    def a0_tile(self, j):
        P = self.P
        self.make_h(j, "mix_pre", 8)
        hreads = [self.hkey(k) for k in range(DC)]
        ts = slice(j * NT, (j + 1) * NT)
        oc = 0
        for grp_name, nld in (("A_lrux", 3), ("A_sccx", 3)):
            for li in range(nld):
                s = self.wload(grp_name, li)
                nout = self.groups[grp_name][li][1] // DC
                for mi in range(nout):
                    pb = oc % 2
                    st = 4 + oc % 3
                    self.mm(self.ps[pb][:], [(self.wblk(s, mi * DC + k), self.hv(k)) for k in range(DC)],
                            reads=[("wr", s)] + hreads, writes=[("ps", pb)])
                    P.op("act", lambda h, pb=pb, st=st: h.activation(self.U(st), self.ps[pb][:], AF.Copy),
                         reads=[("ps", pb)], writes=[("ar", st)])
                    if oc < 8:
                        dst = self.zl[oc * 128:(oc + 1) * 128, self.zo[0] + j * NT:self.zo[0] + (j + 1) * NT]
                    elif oc < 12:
                        dst = self.zc[(oc - 8) * 128:(oc - 7) * 128, self.zo[1] + j * NT:self.zo[1] + (j + 1) * NT]
                    else:
                        dst = self.zx[(oc - 12) * 128:(oc - 11) * 128, self.zo[1] + j * NT:self.zo[1] + (j + 1) * NT]
                    P.dma("pool", f"st{oc % 3}{self.sfx}", lambda h, dst=dst, st=st: h.dma_start(out=dst, in_=self.U(st)),
                          reads=[("ar", st)], writes=[("z", oc, j)])
                    oc += 1

    def stage_a_all(self):
        seq = [(j, hd) for j in range(self.NTL) for hd in range(4)]
        self.sa_front(*seq[0])
        for k in range(len(seq)):
            if k + 1 < len(seq):
                self.sa_front(*seq[k + 1])
            self.sa_back(*seq[k])

    def sa_front(self, j, hd):
        P = self.P
        if True:
            rot = hd % 2
            XC = [3 + 3 * rot, 4 + 3 * rot]
            XCB = 5 + 3 * rot
            for mm_ in range(2):
                c = 2 * hd + mm_
                lx = (self.lxi % 3)
                self.lxi += 1
                P.dma("pool", f"ld{lx}{self.sfx}", lambda h, c=c, lx=lx: h.dma_start(out=self.U(lx, NT + 3),
                                                                           in_=self.zlp[c * 128:(c + 1) * 128, j * NT:j * NT + NT + 3]),
                      reads=[("z", c, jj) for jj in (j - 1, j, j + 1)] + ["zpad"], writes=[("ar", lx)])
                xc = XC[mm_]
                P.op("dve", lambda h, c=c, lx=lx, xc=xc: h.tensor_scalar(self.U(xc), self.U(lx, NT, 0), self.cvc("cw", c),
                                                                         self.cvc("cb", c), ALU.mult, ALU.add),
                     reads=[("ar", lx), "cv"], writes=[("ar", xc)])
                for k in range(1, 4):
                    P.op("dve", lambda h, c=c, lx=lx, xc=xc, k=k: h.scalar_tensor_tensor(
                        self.U(xc), self.U(lx, NT, k), self.cvc("cw", k * 8 + c), self.U(xc), ALU.mult, ALU.add),
                        reads=[("ar", lx), ("ar", xc), "cv"], writes=[("ar", xc)])
                P.op("pool", lambda h, xc=xc, mm_=mm_, XCB=XCB: h.tensor_copy(self.B(XCB, mm_), self.U(xc)),
                     reads=[("ar", xc)], writes=[("ar", XCB)])

    def sa_back(self, j, hd):
        P = self.P
        rev = lambda t: bass.AP(t.tensor, t.offset + NT - 1, [list(t.ap[0]), [-1, NT]])
        ts = slice(j * NT, (j + 1) * NT)
        if True:
            rot = hd % 2
            XC = [3 + 3 * rot, 4 + 3 * rot]
            XCB = 5 + 3 * rot
            s = self.wload("A_gates", hd)
            sets = {}
            for mm_ in range(2):
                for d in range(2):
                    S0 = self.SA0 + 4 * ((hd % 2) * 4 + mm_ * 2 + d)
                    sets[(mm_, d)] = (S0, S0 + 1, S0 + 2, S0 + 3)
                    for g in range(2):
                        pb = (d * 2 + g) * 2 + mm_
                        base = ((d * 2 + g) * 2 + mm_) * 2
                        self.mm(self.ps[pb][:], [(self.wblk(s, base + kk), self.B(XCB, kk)) for kk in range(2)],
                                reads=[("wr", s), ("ar", XCB)], writes=[("ps", pb)])
            for mm_ in range(2):
                c = 2 * hd + mm_
                for d in range(2):
                    RA, SH, IU, PP = sets[(mm_, d)]
                    for g in range(2):
                        pb = (d * 2 + g) * 2 + mm_
                        dstu = RA if g == 0 else IU
                        bname = "ba" if g == 0 else "bx"
                        P.op("act", lambda h, pb=pb, dstu=dstu, bname=bname, d=d, c=c: h.activation(
                            self.U(dstu), self.ps[pb][:], AF.Sigmoid, bias=self.cvc(bname, d * 8 + c), scale=1.0),
                            reads=[("ps", pb), "cv"], writes=[("ar", dstu)])
            for mm_ in range(2):
                c = 2 * hd + mm_
                for d in range(2):
                    RA, SH, IU, PP = sets[(mm_, d)]
                    P.op("act", lambda h, d=d, c=c, RA=RA, SH=SH: h.activation(self.U(SH), self.U(RA), AF.Exp, scale=self.cvc("sc2", d * 8 + c)),
                         reads=[("ar", RA), "cv"], writes=[("ar", SH)])
                    P.op("act", lambda h, d=d, c=c, RA=RA: h.activation(self.U(RA), self.U(RA), AF.Exp, scale=self.cvc("sc", d * 8 + c)),
                         reads=[("ar", RA), "cv"], writes=[("ar", RA)])
            for mm_ in range(2):
                for d in range(2):
                    RA, SH, IU, PP = sets[(mm_, d)]
                    P.op("act", lambda h, SH=SH: h.activation(self.U(SH), self.U(SH), AF.Sqrt, bias=self.oneb[:, 0:1], scale=-1.0),
                         reads=[("ar", SH), "ones"], writes=[("ar", SH)])
            for mm_ in range(2):
                c = 2 * hd + mm_
                xc = XC[mm_]
                for d in range(2):
                    RA, SH, IU, PP = sets[(mm_, d)]
                    P.op("dve", lambda h, IU=IU, xc=xc: h.tensor_tensor(self.U(IU), self.U(IU), self.U(xc), ALU.mult),
                         reads=[("ar", IU), ("ar", xc)], writes=[("ar", IU)])
                    P.op("dve", lambda h, IU=IU, SH=SH: h.tensor_tensor(self.U(IU), self.U(IU), self.U(SH), ALU.mult),
                         reads=[("ar", IU), ("ar", SH)], writes=[("ar", IU)])
                    w = (lambda t: t) if d == 0 else rev
                    P.op("dve", lambda h, w=w, SH=SH, RA=RA, IU=IU: h.tensor_tensor_scan(w(self.U(SH)), w(self.U(RA)), w(self.U(IU)), 0.0, ALU.mult, ALU.add),
                         reads=[("ar", RA), ("ar", IU), ("ar", SH)], writes=[("ar", SH)])
                    P.op("dve", lambda h, w=w, PP=PP, RA=RA: h.tensor_tensor_scan(w(self.U(PP)), w(self.U(RA)), w(self.zeros_ap()), 1.0, ALU.mult, ALU.add),
                         reads=[("ar", RA), self.zeros_key], writes=[("ar", PP)])
                    e = NT - 1 if d == 0 else 0
                    for q, src in ((2 * d, PP), (2 * d + 1, SH)):
                        o = (q * DC + c) * self.NTL + j
                        P.op("pool", lambda h, o=o, src=src, e=e: h.tensor_copy(self.summ[:, o:o + 1], self.U(src, 1, e)),
                             reads=[("ar", src)], writes=["summ"])
                (RAf, SHf, IUf, PPf), (RAb, SHb, IUb, PPb) = sets[(mm_, 0)], sets[(mm_, 1)]
                P.op("pool", lambda h, SHf=SHf, SHb=SHb: h.tensor_tensor(self.U(SHf), self.U(SHf), self.U(SHb), ALU.add),
                     reads=[("ar", SHf), ("ar", SHb)], writes=[("ar", SHf)])
                for dst, src in ((self.hs, SHf), (self.pf, PPf), (self.pb, PPb)):
                    k3 = self.sti % 3
                    self.sti += 1
                    P.dma("pool", f"st{k3}{self.sfx}", lambda h, dst=dst, src=src, c=c: h.dma_start(out=dst[c * 128:(c + 1) * 128, ts], in_=self.U(src)),
                          reads=[("ar", src)], writes=[("spill", c, j)])

    def sv(self, q, j):
        t = self.summ
        base = t[:, (q * DC) * self.NTL + j:(q * DC) * self.NTL + j + 1]
        return bass.AP(base.tensor, base.offset, [list(base.ap[0]), [self.NTL, DC]])

    def compose_e2(self):
        P = self.P
        NTL = self.NTL
        e = lambda q: self.e2t[:, q * DC:(q + 1) * DC]
        ops = []
        ops.append(lambda h: h.tensor_copy(e(0), self.sv(0, 0)))
        ops.append(lambda h: h.tensor_copy(e(1), self.sv(1, 0)))
        for j in range(1, NTL):
            ops.append(lambda h, j=j: h.tensor_tensor(e(1), e(1), self.sv(0, j), ALU.mult))
            ops.append(lambda h, j=j: h.tensor_tensor(e(1), e(1), self.sv(1, j), ALU.add))
            ops.append(lambda h, j=j: h.tensor_tensor(e(0), e(0), self.sv(0, j), ALU.mult))
        ops.append(lambda h: h.tensor_copy(e(2), self.sv(2, NTL - 1)))
        ops.append(lambda h: h.tensor_copy(e(3), self.sv(3, NTL - 1)))
        for j in range(NTL - 2, -1, -1):
            ops.append(lambda h, j=j: h.tensor_tensor(e(3), e(3), self.sv(2, j), ALU.mult))
            ops.append(lambda h, j=j: h.tensor_tensor(e(3), e(3), self.sv(3, j), ALU.add))
            ops.append(lambda h, j=j: h.tensor_tensor(e(2), e(2), self.sv(2, j), ALU.mult))
        for f in ops:
            P.op("dve", f, reads=["summ", "e2t"], writes=["e2t"])

    def carries(self):
        P = self.P
        NTL = self.NTL
        r = lambda n, q: self.rcv[:, (n * 4 + q) * DC:(n * 4 + q + 1) * DC]
        tmp = self.e2t[:, 0:DC]

        def ct(d, j):
            b = self.ct[:, (d * DC) * NTL + j:(d * DC) * NTL + j + 1]
            return bass.AP(b.tensor, b.offset, [list(b.ap[0]), [NTL, DC]])
        ops = []
        for d, (n0, qa, qh) in enumerate(((0, 0, 1), (3, 2, 3))):
            ops.append(lambda h, n0=n0, qa=qa, qh=qh: h.tensor_tensor(tmp, r(n0 + 1, qa), r(n0 + 2, qh), ALU.mult))
            ops.append(lambda h, n0=n0, qh=qh: h.tensor_tensor(tmp, tmp, r(n0 + 1, qh), ALU.add))
            ops.append(lambda h, n0=n0, qa=qa: h.tensor_tensor(tmp, tmp, r(n0, qa), ALU.mult))
            j0 = 0 if d == 0 else NTL - 1
            ops.append(lambda h, n0=n0, qh=qh, d=d, j0=j0: h.tensor_tensor(ct(d, j0), tmp, r(n0, qh), ALU.add))
            seq = range(0, NTL - 1) if d == 0 else range(NTL - 1, 0, -1)
            for j in seq:
                jn = j + 1 if d == 0 else j - 1
                ops.append(lambda h, d=d, j=j, jn=jn, qa=qa: h.tensor_tensor(ct(d, jn), ct(d, j), self.sv(qa, j), ALU.mult))
                ops.append(lambda h, d=d, j=j, jn=jn, qh=qh: h.tensor_tensor(ct(d, jn), ct(d, jn), self.sv(qh, j), ALU.add))
        for f in ops:
            P.op("dve", f, reads=["summ", "e2t", "rcv", "ct"], writes=["ct", "e2t"])
        self.ctv = lambda d, c, j: self.ct[:, (d * DC + c) * NTL + j:(d * DC + c) * NTL + j + 1]

    def stage_b_tile(self, j):
        P = self.P
        ts = slice(j * NT, (j + 1) * NT)
        self.make_h(j, "mix_pre", 8)
        hreads = [self.hkey(k) for k in range(DC)]
        YA, YB, YC, MM, UU = 4, 8, 10, 12, 16
        ST = (20, 21, 22)
        WC, WX, TMP = 23, 24, 22
        oc = 0
        nld = len(self.groups["B_z"])
        for li in range(nld):
            s = self.wload("B_z", li)
            nout = self.groups["B_z"][li][1] // DC
            for mi in range(nout):
                pb = oc % 2
                self.mm(self.ps[pb][:], [(self.wblk(s, mi * DC + k), self.hv(k)) for k in range(DC)],
                        reads=[("wr", s)] + hreads, writes=[("ps", pb)])
                if oc < 4:
                    c = oc
                    for (u, src) in ((WC, self.zcp), (WX, self.zxp)):
                        zi = (8 if u == WC else 12) + c
                        P.dma("pool", f"ld{u - 23}{self.sfx}", lambda h, u=u, src=src, c=c: h.dma_start(
                            out=self.U(u, NT + 2), in_=src[c * 128:(c + 1) * 128, j * NT:j * NT + NT + 2]),
                            reads=[("z", zi, jj) for jj in (j - 1, j, j + 1)] + ["zpad"], writes=[("ar", u)])
                    P.op("dve", lambda h: h.tensor_tensor(self.U(WC, NT + 2), self.U(WC, NT + 2), self.U(WX, NT + 2), ALU.mult),
                         reads=[("ar", WC), ("ar", WX)], writes=[("ar", WC)])
                    P.op("dve", lambda h, c=c: h.tensor_scalar(self.U(TMP), self.U(WC, NT, 0), self.cvc("scw", c), None, ALU.mult),
                         reads=[("ar", WC), "cv"], writes=[("ar", TMP)])
                    for k in (1, 2):
                        P.op("dve", lambda h, c=c, k=k: h.scalar_tensor_tensor(self.U(TMP), self.U(WC, NT, k), self.cvc("scw", k * 4 + c),
                                                                               self.U(TMP), ALU.mult, ALU.add),
                             reads=[("ar", WC), ("ar", TMP), "cv"], writes=[("ar", TMP)])
                    P.op("dve", lambda h, c=c, pb=pb: h.tensor_tensor(self.B(YB + c // 2, c % 2), self.U(TMP), self.ps[pb][:], ALU.mult),
                         reads=[("ar", TMP), ("ps", pb)], writes=[("ar", YB + c // 2)])
                elif oc < 8:
                    c = oc - 4
                    P.op("act", lambda h, c=c, pb=pb: h.activation(self.U(UU + c), self.ps[pb][:], AF.Gelu_apprx_tanh),
                         reads=[("ps", pb)], writes=[("ar", UU + c)])
                else:
                    c = oc - 8
                    for u, src in zip(ST, (self.hs, self.pf, self.pb)):
                        P.dma("pool", f"st{u - 20}{self.sfx}", lambda h, u=u, src=src, c=c: h.dma_start(out=self.U(u), in_=src[c * 128:(c + 1) * 128, ts]),
                              reads=[("spill", c, j)], writes=[("ar", u)])
                    P.op("dve", lambda h, c=c: h.scalar_tensor_tensor(self.U(ST[0]), self.U(ST[1]), self.ctv(0, c, j), self.U(ST[0]), ALU.mult, ALU.add),
                         reads=[("ar", ST[0]), ("ar", ST[1]), "ct"], writes=[("ar", ST[0])])
                    P.op("dve", lambda h, c=c: h.scalar_tensor_tensor(self.U(ST[0]), self.U(ST[2]), self.ctv(1, c, j), self.U(ST[0]), ALU.mult, ALU.add),
                         reads=[("ar", ST[0]), ("ar", ST[2]), "ct"], writes=[("ar", ST[0])])
                    P.op("act", lambda h, pb=pb: h.activation(self.U(TMP), self.ps[pb][:], AF.Gelu_apprx_tanh),
                         reads=[("ps", pb)], writes=[("ar", TMP)])
                    P.op("dve", lambda h, c=c: h.tensor_tensor(self.B(YA + c // 2, c % 2), self.U(ST[0]), self.U(TMP), ALU.mult),
                         reads=[("ar", ST[0]), ("ar", TMP)], writes=[("ar", YA + c // 2)])
                oc += 1
        s0 = self.wload("B_sgv", 0)
        s1 = self.wload("B_sgv", 1)
        V, VN = 20, 21
        for n in range(4):
            pb = 2 + n % 2
            pairs = []
            for k in range(DC):
                sl, kk = (s0, k) if k < 6 else (s1, k - 6)
                pairs.append((self.hv(k)[:, n * 128:(n + 1) * 128], self.wblk(sl, kk * 4, 4)))
            self.mm(self.ps[pb][:], pairs, reads=[("wr", s0), ("wr", s1)] + hreads, writes=[("ps", pb)])
            P.op("act", lambda h, pb=pb: h.activation(self.U(V), self.ps[pb][:], AF.Gelu_apprx_tanh),
                 reads=[("ps", pb)], writes=[("ar", V)])
            P.op("dve", lambda h: h.bn_stats(self.bst[:, 0:6], self.U(V)), reads=[("ar", V)], writes=["bst"])
            P.op("dve", lambda h: h.bn_aggr(self.bst[:, 6:8], self.bst[:, 0:6]), reads=["bst"], writes=["bst"])
            P.op("act", lambda h: h.activation(self.bst[:, 8:9], self.bst[:, 7:8], AF.Sqrt, bias=self.eps1[:, 0:1], scale=1.0),
                 reads=["bst", "ones"], writes=["bst"])
            P.op("dve", lambda h: h.reciprocal(self.bst[:, 8:9], self.bst[:, 8:9]), reads=["bst"], writes=["bst"])
            P.op("dve", lambda h: h.tensor_scalar(self.U(V), self.U(V), self.bst[:, 6:7], self.bst[:, 8:9], ALU.subtract, ALU.mult),
                 reads=["bst", ("ar", V)], writes=[("ar", V)])
            P.op("dve", lambda h: h.tensor_tensor(self.U(V), self.U(V), self.lnc[:, 0, :], ALU.mult),
                 reads=[("ar", V), "lnc"], writes=[("ar", V)])
            P.op("dve", lambda h, n=n: h.tensor_tensor(self.B(VN + n // 2, n % 2), self.U(V), self.lnc[:, 1, :], ALU.add),
                 reads=[("ar", V), "lnc"], writes=[("ar", VN + n // 2)])
        for g in range(4):
            pb = 2 + g
            fns = []
            for n in range(4):
                o = self.ps[pb][:, n * 128:(n + 1) * 128]
                fns.append(lambda h, o=o, g=g, n=n: h.matmul(o, self.B(VN + n // 2, n % 2)[:, g * 128:(g + 1) * 128], self.wst[:, g * 128:(g + 1) * 128],
                                                            start=True, stop=False))
                fns.append(lambda h, o=o, g=g: h.matmul(o, self.ones[0:1, :], self.bsr[0:1, g * 128:(g + 1) * 128], start=False, stop=True))
            P.mm_group(fns, reads=[("ar", VN), ("ar", VN + 1), "wst", "ones", "bsr"], writes=[("ps", pb)])
            P.op("dve", lambda h, g=g, pb=pb: h.tensor_tensor(self.B(YC + g // 2, g % 2), self.U(UU + g), self.ps[pb][:], ALU.mult),
                 reads=[("ar", UU + g), ("ps", pb)], writes=[("ar", YC + g // 2)])
        GM = (16, 17, 18)
        T1, T2 = 19, 20
        yar = [("ar", YA + k) for k in range(4)]
        for m in range(DC):
            sg = self.wload("B_mrg", m)
            sb_ = self.wload("B_br", m)
            for br in range(3):
                self.mm(self.ps[br][:], [(self.wblk(sg, br * DC + k), self.hv(k)) for k in range(DC)],
                        reads=[("wr", sg)] + hreads, writes=[("ps", br)])
                P.op("act", lambda h, br=br: h.activation(self.U(GM[br]), self.ps[br][:], AF.Sigmoid),
                     reads=[("ps", br)], writes=[("ar", GM[br])])
            self.mm(self.ps[3][:], [(self.wblk(sb_, k), self.B(YA + k // 2, k % 2)) for k in range(8)],
                    reads=[("wr", sb_)] + yar, writes=[("ps", 3)])
            self.mm(self.ps[4][:], [(self.wblk(sb_, 8 + k), self.B(YB + k // 2, k % 2)) for k in range(4)],
                    reads=[("wr", sb_), ("ar", YB), ("ar", YB + 1)], writes=[("ps", 4)])
            self.mm(self.ps[5][:], [(self.wblk(sb_, 12 + k), self.B(YC + k // 2, k % 2)) for k in range(4)],
                    reads=[("wr", sb_), ("ar", YC), ("ar", YC + 1)], writes=[("ps", 5)])
            P.op("dve", lambda h: h.tensor_tensor(self.U(T1), self.U(GM[0]), self.ps[3][:], ALU.mult),
                 reads=[("ar", GM[0]), ("ps", 3)], writes=[("ar", T1)])
            P.op("dve", lambda h: h.tensor_tensor(self.U(T2), self.U(GM[1]), self.ps[4][:], ALU.mult),
                 reads=[("ar", GM[1]), ("ps", 4)], writes=[("ar", T2)])
            P.op("pool", lambda h: h.tensor_tensor(self.U(T1), self.U(T1), self.U(T2), ALU.add),
                 reads=[("ar", T1), ("ar", T2)], writes=[("ar", T1)])
            P.op("dve", lambda h: h.tensor_tensor(self.U(T2), self.U(GM[2]), self.ps[5][:], ALU.mult),
                 reads=[("ar", GM[2]), ("ps", 5), ("ar", T2)], writes=[("ar", T2)])
            P.op("pool", lambda h, m=m: h.tensor_tensor(self.B(MM + m // 2, m % 2), self.U(T1), self.U(T2), ALU.add),
                 reads=[("ar", T1), ("ar", T2)], writes=[("ar", MM + m // 2)])
        F1 = 4
        mreads = [("ar", MM + k) for k in range(4)]
        oc = 0
        for li in range(3):
            s = self.wload("B_wo", li)
            nout = self.groups["B_wo"][li][1] // DC
            for mi in range(nout):
                pb = oc % 2
                self.mm(self.ps[pb][:], [(self.wblk(s, mi * DC + k), self.B(MM + k // 2, k % 2)) for k in range(DC)],
                        reads=[("wr", s)] + mreads, writes=[("ps", pb)])
                P.op("act", lambda h, pb=pb, oc=oc: h.activation(self.U(F1 + oc), self.ps[pb][:], AF.Copy),
                     reads=[("ps", pb)], writes=[("ar", F1 + oc)])
                oc += 1
        self.post_residual(j, "mix_post", F1, 12)

    def build(self):
        T, NTL, ph = self.T, self.NTL, self.phase
        nc = bass.Bass("TRN2", target_bir_lowering=False)
        self.nc = nc
        P = self.P = Prog()
        ein = lambda n, sh: nc.dram_tensor(n, sh, F32, kind="ExternalInput").ap()
        eout = lambda n, sh: nc.dram_tensor(n, sh, F32, kind="ExternalOutput").ap()
        wf = ein("wf", [self.wsize])
        self.wbf = nc.dram_tensor("wbf", [self.wsize], BF16).ap()
        cvin = ein("cvin", [128, NCV])
        if ph in ("P1", "P3"):
            xin = ein("xin", [D, T])
            xout = eout("xout", [D, T])
        if ph == "P1":
            self.zl, self.zc, self.zx = eout("zl", [D, T]), eout("zc", [512, T]), eout("zx", [512, T])
        if ph == "P2":
            self.zlp = ein("zlp", [D, T + 3])
            self.hs, self.pf, self.pb = eout("hs", [D, T]), eout("pf", [D, T]), eout("pb", [D, T])
            summ_o = eout("summ", [128, 4 * DC * NTL])
            e2_o = eout("e2", [128, 4 * DC])
        if ph == "P3":
            self.hs, self.pf, self.pb = ein("hs", [D, T]), ein("pf", [D, T]), ein("pb", [D, T])
            self.zcp, self.zxp = ein("zcp", [512, T + 2]), ein("zxp", [512, T + 2])
            summ_i = ein("summ", [128, 4 * DC * NTL])
            rcv_i = ein("rcv", [128, 6 * 4 * DC])
            lnc_i = ein("lnc", [128, 2 * 512])
            wst_i = ein("wst", [128, 4 * 128])
            bsr_i = ein("bsr", [1, 512])
        self.CVT = 128 * 8192
        self.sfx = ""
        self.wbfkey = "wbf"
        self.zo = (0, 0)

        for e in ENGS:
            P.new_sem(e)
        for s in range(NSLOT):
            P.new_sem(f"w{s}")
        for n in ("xio", "cvt", "cst", "st0", "st1", "st2", "ld0", "ld1", "ld2"):
            P.new_sem(n)

        from contextlib import ExitStack
        with ExitStack() as es:
            def sb(name, shape, dt):
                return es.enter_context(nc.sbuf_tensor(name, shape, dt))
            if ph in ("P1", "P3"):
                self.X = sb("X", [128, DC, T], F32)
            nu = NU_BIG if ph == "P2" else NU
            self.SA0 = 25
            self.ar = sb("arena", [128, nu, UW], F32)
            self.arb = self.ar[:].bitcast(BF16)
            self.wring = sb("wring", [128, NSLOT, SLOT_BLKS * 128], BF16)
            self.ones = sb("ones", [128, 128], F32)
            self.onesb = sb("onesb", [128, 128], BF16)
            self.epsb = sb("epsb", [128, 1], F32)
            self.eps1 = sb("eps1", [128, 1], F32)
            self.oneb = sb("oneb", [128, 1], F32)
            self.cv = sb("cv", [128, NCV], F32)
            self.summ = sb("summ_t", [128, 4 * DC * NTL], F32)
            self.e2t = sb("e2t", [128, 4 * DC], F32)
            self.ct = sb("ct", [128, 2 * DC * NTL], F32)
            self.bst = sb("bst", [128, 16], F32)
            if ph == "P3":
                self.rcv = sb("rcv_t", [128, 6 * 4 * DC], F32)
                self.lnc = sb("lnc_t", [128, 2, 512], F32)
                self.wst = sb("wst_t", [128, 512], BF16)
                self.bsr = sb("bsr_t", [1, 512], F32)
            self.ps = [es.enter_context(nc.psum_tensor(f"ps{i}", [128, NT], F32)) for i in range(8)]
            self.wslot = 0
            self.lxi = self.seti = self.gpi = self.sti = 0
            sems = {n: es.enter_context(nc.semaphore(n)) for n in P.semnames}

            P.op("pool", lambda h: h.memset(self.epsb[:], float(D * EPS)), writes=["ones"])
            P.op("pool", lambda h: h.memset(self.eps1[:], float(EPS)), writes=["ones"])
            P.op("pool", lambda h: h.memset(self.oneb[:], 1.0), writes=["ones"])
            P.op("pool", lambda h: h.memset(self.ones[:], 1.0), writes=["ones"])
            P.op("pool", lambda h: h.memset(self.onesb[:], 1.0), writes=["ones"])
            P.op("pool", lambda h: h.memset(self.U(21, UW), 0.0), writes=[("ar", 21)])
            P.dma("pool", "cst", lambda h: h.dma_start(out=self.cv[:], in_=cvin[:, :]), writes=["cv"])
            if ph == "P3":
                P.dma("pool", "cst", lambda h: h.dma_start(out=self.rcv[:], in_=rcv_i[:, :]), writes=["rcv"])
                P.dma("pool", "cst", lambda h: h.dma_start(out=self.summ[:], in_=summ_i[:, :]), writes=["summ"])
                P.dma("pool", "cst", lambda h: h.dma_start(out=self.lnc[:], in_=lnc_i.rearrange("p (a b) -> p a b", a=2)), writes=["lnc"])
                P.dma("pool", "cst", lambda h: h.dma_start(out=self.U(0), in_=wst_i[:, :]), writes=["wstf", ("ar", 0)])
                P.dma("pool", "cst", lambda h: h.dma_start(out=self.bsr[:], in_=bsr_i[:, :]), writes=["ones"])
            for k in ("cv", "rcv", "summ", "lnc", "wstf", "ones", ("ar", 0)):
                if k in P.lastw and P.lastw[k][0] == "cst":
                    P.lastw[k] = ("cst", P.cnt["cst"])
            n = self.wsize
            npieces = (n + self.CVT - 1) // self.CVT
            for p in range(npieces):
                a, b = p * self.CVT, min(n, (p + 1) * self.CVT)
                src = wf[a:b].rearrange("(p f) -> p f", p=128)
                dst = self.wbf[a:b].rearrange("(p f) -> p f", p=128)
                P.dma("pool", "cvt", lambda h, src=src, dst=dst: h.dma_start(out=dst, in_=src), writes=["wbf"])
            P.op("dve", lambda h: h.tensor_scalar(self.cv[:, 0:48], self.cv[:, 0:48], 32.0, None, ALU.mult), reads=["cv"], writes=["cv"])
            for nm in ("ffn1_post", "ffn2_post"):
                P.op("dve", lambda h, nm=nm: h.tensor_scalar(self.cv[:, CV[nm]:CV[nm] + 8], self.cv[:, CV[nm]:CV[nm] + 8], 0.5, None, ALU.mult),
                     reads=["cv"], writes=["cv"])
            if ph == "P2":
                sc = self.cv[:, CV["sc"]:CV["sc"] + 16]
                sc2 = self.cv[:, CV["sc2"]:CV["sc2"] + 16]
                lam = self.cv[:, CV["lam"]:CV["lam"] + 16]
                P.op("act", lambda h: h.activation(sc, lam, AF.Exp, scale=-1.0), reads=["cv"], writes=["cv"])
                P.op("act", lambda h: h.activation(sc, sc, AF.Ln, bias=self.oneb[:, 0:1], scale=1.0), reads=["cv", "ones"], writes=["cv"])
                P.op("dve", lambda h: h.tensor_scalar(sc2, sc, -16.0, None, ALU.mult), reads=["cv"], writes=["cv"])
                P.op("dve", lambda h: h.tensor_scalar(sc, sc, -8.0, None, ALU.mult), reads=["cv"], writes=["cv"])
            if ph == "P3":
                P.op("act", lambda h: h.activation(self.wst[:], self.U(0), AF.Copy), reads=["wstf", ("ar", 0)], writes=["wst"])
                self.carries()
            if ph in ("P1", "P3"):
                for j in range(NTL):
                    for c in range(DC):
                        P.dma("sp", "xio", lambda h, c=c, j=j: h.dma_start(out=self.X[:, c, j * NT:(j + 1) * NT],
                                                                           in_=xin[c * 128:(c + 1) * 128, j * NT:(j + 1) * NT]),
                              writes=[("X", c, j)])
                for j in range(NTL):
                    for c in range(DC):
                        P.lastw[("X", c, j)] = ("xio", P.cnt["xio"])
            if ph == "P1":
                for j in range(NTL):
                    self.ffn_tile(j, "ffn1")
                for j in range(NTL):
                    self.a0_tile(j)
            elif ph == "P2":
                self.stage_a_all()
                self.compose_e2()
                P.dma("pool", "cst", lambda h: h.dma_start(out=summ_o[:, :], in_=self.summ[:]), reads=["summ"])
                P.dma("pool", "cst", lambda h: h.dma_start(out=e2_o[:, :], in_=self.e2t[:]), reads=["e2t"])
            else:
                for j in range(NTL):
                    self.stage_b_tile(j)
                for j in range(NTL):
                    self.ffn_tile(j, "ffn2")
            if ph in ("P1", "P3"):
                for j in range(NTL):
                    for c in range(DC):
                        P.dma("sp", "xio", lambda h, c=c, j=j: h.dma_start(out=xout[c * 128:(c + 1) * 128, j * NT:(j + 1) * NT],
                                                                           in_=self.X[:, c, j * NT:(j + 1) * NT]),
                              reads=[("X", c, j)])
            P.wait_all("sp", ["xio"])
            P.wait_all("pool", ["cvt", "cst", "st0", "st1", "st2", "ld0", "ld1", "ld2"])

            with nc.Block() as block:
                @block.sync
                def _(e):
                    P.replay("sp", e, sems)

                @block.gpsimd
                def _(e):
                    P.replay("pool", e, sems)

                @block.tensor
                def _(e):
                    P.replay("pe", e, sems)

                @block.scalar
                def _(e):
                    P.replay("act", e, sems)

                @block.vector
                def _(e):
                    P.replay("dve", e, sems)
        return nc


class FusedBuilder(Builder):
    def __init__(self, T, L, wsizes, lgroups):
        self.T = T
        self.NTL = T // NT
        self.L = L
        self.phase = "F"
        self.wsizes = wsizes
        self.lgroups = lgroups

    zeros_key = "zeros"

    def zeros_ap(self):
        return self.zt[:]

    def Xv(self, c, j):
        return self.XT[:, j % 2, c, :]

    def Xk(self, c, j):
        return ("XT", j % 2, c)

    def _xload(self, j):
        if j in self.xloaded or j >= self.NTL:
            return
        self.xloaded.add(j)
        sl = j % 2
        src = self.xsrc.rearrange("(c p) t -> p c t", p=128)[:, :, j * NT:(j + 1) * NT]
        self.P.dma("sp", f"xl{sl}{self.sfx}", lambda h, sl=sl, src=src: h.dma_start(out=self.XT[:, sl, :, :], in_=src),
                   reads=[(self.xsrc_key, j)], writes=[("XT", sl, c) for c in range(DC)])

    def x_need(self, j):
        self._xload(j)
        self._xload(j + 1)

    def x_done(self, j):
        sl = j % 2
        dst = self.xdst.rearrange("(c p) t -> p c t", p=128)[:, :, j * NT:(j + 1) * NT]
        self.P.dma("pool", f"xs{sl}{self.sfx}", lambda h, sl=sl, dst=dst: h.dma_start(out=dst, in_=self.XT[:, sl, :, :]),
                   reads=[("XT", sl, c) for c in range(DC)], writes=[(self.xdst_key, j)])

    def build(self):
        T, NTL, L = self.T, self.NTL, self.L
        nc = bass.Bass("TRN2", target_bir_lowering=False)
        self.nc = nc
        P = self.P = Prog()
        ein = lambda n, sh: nc.dram_tensor(n, sh, F32, kind="ExternalInput").ap()
        xin = ein("xin", [D, T])
        xout = nc.dram_tensor("xout", [D, T], F32, kind="ExternalOutput").ap()
        wfs = [ein(f"wf{l}", [self.wsizes[l]]) for l in range(L)]
        self.wbfs = [nc.dram_tensor(f"wbf{l}", [self.wsizes[l]], BF16).ap() for l in range(L)]
        cvin = ein("cvin", [L * 128, NCV])
        lnc_i = ein("lnc", [L * 128, 2 * 512])
        wst_i = ein("wst", [L * 128, 4 * 128])
        bsr_i = ein("bsr", [L, 512])
        scr = lambda n, sh: nc.dram_tensor(n, sh, F32).ap()
        Xd = scr("Xd", [D, T])
        self.zlp = self.zl = scr("zlp", [D, T + 3])
        self.zcp = self.zc = scr("zcp", [512, T + 2])
        self.zxp = self.zx = scr("zxp", [512, T + 2])
        self.hs, self.pf, self.pb = scr("hs", [D, T]), scr("pf", [D, T]), scr("pb", [D, T])
        self.zo = (2, 1)
        self.CVT = 128 * 8192

        for e in ENGS:
            P.new_sem(e)
        P.new_sem("cst")
        for l in range(L):
            P.new_sem(f"cvt{l}")
            P.new_sem(f"cst_{l}")
            for s in range(NSLOT):
                P.new_sem(f"w{s}_{l}")
            for n in ("st0", "st1", "st2", "ld0", "ld1", "ld2", "xl0", "xl1", "xs0", "xs1"):
                P.new_sem(f"{n}_{l}")

        from contextlib import ExitStack
        with ExitStack() as es:
            def sb(name, shape, dt):
                return es.enter_context(nc.sbuf_tensor(name, shape, dt))
            self.XT = sb("XT", [128, 2, DC, NT], F32)
            self.zt = sb("zt", [128, NT], F32)
            self.SA0 = 25
            self.ar = sb("arena", [128, NU_BIG, UW], F32)
            self.arb = self.ar[:].bitcast(BF16)
            self.wring = sb("wring", [128, NSLOT, SLOT_BLKS * 128], BF16)
            self.ones = sb("ones", [128, 128], F32)
            self.onesb = sb("onesb", [128, 128], BF16)
            self.epsb = sb("epsb", [128, 1], F32)
            self.eps1 = sb("eps1", [128, 1], F32)
            self.oneb = sb("oneb", [128, 1], F32)
            self.cv = sb("cv", [128, NCV], F32)
            self.summ = sb("summ_t", [128, 4 * DC * NTL], F32)
            self.e2t = sb("e2t", [128, 4 * DC], F32)
            self.ct = sb("ct", [128, 2 * DC * NTL], F32)
            self.bst = sb("bst", [128, 16], F32)
            self.rcv = sb("rcv_t", [128, 6 * 4 * DC], F32)
            self.lnc = sb("lnc_t", [128, 2, 512], F32)
            self.wst = sb("wst_t", [128, 512], BF16)
            self.bsr = sb("bsr_t", [1, 512], F32)
            self.ps = [es.enter_context(nc.psum_tensor(f"ps{i}", [128, NT], F32)) for i in range(8)]
            self.wslot = 0
            self.lxi = self.seti = self.gpi = self.sti = 0
            sems = {n: es.enter_context(nc.semaphore(n)) for n in P.semnames}

            P.op("pool", lambda h: h.memset(self.epsb[:], float(D * EPS)), writes=["ones"])
            P.op("pool", lambda h: h.memset(self.eps1[:], float(EPS)), writes=["ones"])
            P.op("pool", lambda h: h.memset(self.oneb[:], 1.0), writes=["ones"])
            P.op("pool", lambda h: h.memset(self.ones[:], 1.0), writes=["ones"])
            P.op("pool", lambda h: h.memset(self.onesb[:], 1.0), writes=["ones"])
            P.op("pool", lambda h: h.memset(self.rcv[:], 0.0), writes=["rcv"])
            P.op("pool", lambda h: h.memset(self.zt[:], 0.0), writes=["zeros"])
            P.op("pool", lambda h: h.memset(self.U(21, UW), 0.0), writes=[("ar", 21)])
            for (zt, nch, cols) in ((self.zlp, DC, ((0, 2), (T + 2, T + 3))), (self.zcp, 4, ((0, 1), (T + 1, T + 2))),
                                    (self.zxp, 4, ((0, 1), (T + 1, T + 2)))):
                for (a, b) in cols:
                    dst = zt.rearrange("(c p) t -> p c t", p=128)[:, :, a:b]
                    srcz = self.ar[:, 21, 0:nch * (b - a)].rearrange("p (c t) -> p c t", c=nch)
                    P.dma("pool", "cst", lambda h, dst=dst, srcz=srcz: h.dma_start(out=dst, in_=srcz, allow_slow_non_contiguous=True),
                          reads=[("ar", 21)], writes=["zpad"])
            P.lastw["zpad"] = ("cst", P.cnt["cst"])
            for l in range(L):
                n = self.wsizes[l]
                npieces = (n + self.CVT - 1) // self.CVT
                for p in range(npieces):
                    a, b = p * self.CVT, min(n, (p + 1) * self.CVT)
                    src = wfs[l][a:b].rearrange("(p f) -> p f", p=128)
                    dst = self.wbfs[l][a:b].rearrange("(p f) -> p f", p=128)
                    P.dma("pool", f"cvt{l}", lambda h, src=src, dst=dst: h.dma_start(out=dst, in_=src), writes=[("wbf", l)])

            for l in range(L):
                self.sfx = f"_{l}"
                self.groups = self.lgroups[l]
                self.wbf = self.wbfs[l]
                self.wbfkey = ("wbf", l)
                cs = f"cst_{l}"
                P.dma("pool", cs, lambda h, l=l: h.dma_start(out=self.cv[:], in_=cvin[l * 128:(l + 1) * 128, :]), writes=["cv"])
                P.dma("pool", cs, lambda h, l=l: h.dma_start(out=self.lnc[:], in_=lnc_i[l * 128:(l + 1) * 128, :].rearrange("p (a b) -> p a b", a=2)),
                      writes=["lnc"])
                P.dma("pool", cs, lambda h, l=l: h.dma_start(out=self.U(0), in_=wst_i[l * 128:(l + 1) * 128, :]), writes=["wstf", ("ar", 0)])
                P.dma("pool", cs, lambda h, l=l: h.dma_start(out=self.bsr[:], in_=bsr_i[l:l + 1, :]), writes=["bsr"])
                for k in ("cv", "lnc", "wstf", "bsr", ("ar", 0)):
                    P.lastw[k] = (cs, P.cnt[cs])
                P.op("dve", lambda h: h.tensor_scalar(self.cv[:, 0:48], self.cv[:, 0:48], 32.0, None, ALU.mult), reads=["cv"], writes=["cv"])
                for nm in ("ffn1_post", "ffn2_post"):
                    P.op("dve", lambda h, nm=nm: h.tensor_scalar(self.cv[:, CV[nm]:CV[nm] + 8], self.cv[:, CV[nm]:CV[nm] + 8], 0.5, None, ALU.mult),
                         reads=["cv"], writes=["cv"])
                sc = self.cv[:, CV["sc"]:CV["sc"] + 16]
                sc2 = self.cv[:, CV["sc2"]:CV["sc2"] + 16]
                lam = self.cv[:, CV["lam"]:CV["lam"] + 16]
                P.op("act", lambda h: h.activation(sc, lam, AF.Exp, scale=-1.0), reads=["cv"], writes=["cv"])
                P.op("act", lambda h: h.activation(sc, sc, AF.Ln, bias=self.oneb[:, 0:1], scale=1.0), reads=["cv", "ones"], writes=["cv"])
                P.op("dve", lambda h: h.tensor_scalar(sc2, sc, -16.0, None, ALU.mult), reads=["cv"], writes=["cv"])
                P.op("dve", lambda h: h.tensor_scalar(sc, sc, -8.0, None, ALU.mult), reads=["cv"], writes=["cv"])
                P.op("act", lambda h: h.activation(self.wst[:], self.U(0), AF.Copy), reads=["wstf", ("ar", 0)], writes=["wst"])
                self.xsrc, self.xsrc_key = (xin, "Xin") if l == 0 else (Xd, "Xd")
                self.xdst, self.xdst_key = Xd, "Xd"
                self.xloaded = set()
                for j in range(NTL):
                    self.ffn_tile(j, "ffn1")
                self.xsrc, self.xsrc_key = Xd, "Xd"
                self.xloaded = set()
                for j in range(NTL):
                    self.a0_tile(j)
                self.stage_a_all()
                self.carries()
                self.xloaded = set()
                for j in range(NTL):
                    self.stage_b_tile(j)
                if l == L - 1:
                    self.xdst, self.xdst_key = xout, "Xout"
                self.xloaded = set()
                for j in range(NTL):
                    self.ffn_tile(j, "ffn2")
            P.wait_all("sp", [n for n in P.semnames if n[:2] in ("xl",) or n[0] == "w"])
            P.wait_all("pool", [n for n in P.semnames if n[:2] in ("cv", "cs", "st", "ld", "xs")])

            with nc.Block() as block:
                @block.sync
                def _(e):
                    P.replay("sp", e, sems)

                @block.gpsimd
                def _(e):
                    P.replay("pool", e, sems)

                @block.tensor
                def _(e):
                    P.replay("pe", e, sems)

                @block.scalar
                def _(e):
                    P.replay("act", e, sems)

                @block.vector
                def _(e):
                    P.replay("dve", e, sems)
        return nc


_NC_CACHE = {}


def _get_nc(T, phase, wsize, groups):
    key = (T, phase, wsize)
    if key not in _NC_CACHE:
        _NC_CACHE[key] = Builder(T, phase, wsize, groups).build()
    return _NC_CACHE[key]


def _cvin(inp, l):
    cv = np.zeros((128, NCV), np.float32)
    for nm in ("ffn1_pre", "ffn1_post", "mix_pre", "mix_post", "ffn2_pre", "ffn2_post"):
        cv[:, CV[nm]:CV[nm] + 8] = chan_major(inp[nm + "_g"][l])
    for k in range(4):
        cv[:, CV["cw"] + k * 8:CV["cw"] + (k + 1) * 8] = chan_major(inp["lru_conv_w"][l, k])
    cv[:, CV["cb"]:CV["cb"] + 8] = chan_major(inp["lru_conv_b"][l])
    for d in range(2):
        cv[:, CV["ba"] + d * 8:CV["ba"] + (d + 1) * 8] = chan_major(inp["lru_ba"][l, d])
        cv[:, CV["bx"] + d * 8:CV["bx"] + (d + 1) * 8] = chan_major(inp["lru_bx"][l, d])
        cv[:, CV["lam"] + d * 8:CV["lam"] + (d + 1) * 8] = chan_major(inp["lru_lambda"][l, d])
    for k in range(3):
        cv[:, CV["scw"] + k * 4:CV["scw"] + (k + 1) * 4] = chan_major(inp["sc_conv_w"][l, k])
    return cv


def _pad_halo(zs, nl, nr):
    out = []
    for c in range(NCORES):
        z = zs[c]
        C, T = z.shape
        p = np.zeros((C, nl + T + nr), np.float32)
        p[:, nl:nl + T] = z
        if c % 4 != 0:
            p[:, 0:nl] = zs[c - 1][:, T - nl:T]
        if c % 4 != 3:
            p[:, nl + T:] = zs[c + 1][:, 0:nr]
        out.append(p)
    return out


FUSED = False


def _pack_layer(inp, l):
    wp = WPack()
    pack_ffn(wp, "ffn1", inp["ffn1_w_gate"][l], inp["ffn1_w_up"][l], inp["ffn1_w_down"][l])
    pack_a0(wp, inp["w_in"][l])
    pack_a(wp, inp["lru_wa"][l], inp["lru_wx"][l])
    pack_b(wp, inp["w_in"][l], inp["lru_w_out"][l], inp["sc_w_out"][l], inp["sgu_w_out"][l], inp["w_o"][l])
    pack_ffn(wp, "ffn2", inp["ffn2_w_gate"][l], inp["ffn2_w_up"][l], inp["ffn2_w_down"][l])
    return wp


def kernel_fused(inp):
    x = inp["x"]
    Bsz, S, _ = x.shape
    L = inp["w_in"].shape[0]
    T = S
    packs = [_pack_layer(inp, l) for l in range(L)]
    flats = [wp.flat() for wp in packs]
    key = ("F", T, L)
    if key not in _NC_CACHE:
        _NC_CACHE[key] = FusedBuilder(T, L, [f.size for f in flats], [wp.groups for wp in packs]).build()
    nc = _NC_CACHE[key]
    cvin = np.concatenate([_cvin(inp, l) for l in range(L)], axis=0)
    lnc = np.concatenate([np.concatenate([np.broadcast_to(inp["sgu_ln_g"][l][None, :], (128, 512)),
                                          np.broadcast_to(inp["sgu_ln_b"][l][None, :], (128, 512))], axis=1) for l in range(L)],
                         axis=0).astype(np.float32)
    wst = np.concatenate([np.transpose(inp["sgu_w_s"][l], (2, 0, 1)).reshape(128, 512) for l in range(L)], axis=0).astype(np.float32)
    bsr = np.ascontiguousarray(inp["sgu_b"].reshape(L, 512)).astype(np.float32)
    common = {"cvin": np.ascontiguousarray(cvin), "lnc": np.ascontiguousarray(lnc), "wst": np.ascontiguousarray(wst), "bsr": bsr}
    for l in range(L):
        common[f"wf{l}"] = flats[l]
    cores = list(range(Bsz))
    in_maps = [dict(common, xin=np.ascontiguousarray(x[b].T)) for b in cores]
    res = run_bass_kernel_spmd(nc, in_maps, core_ids=cores).results
    out = np.zeros((Bsz, S, D), np.float32)
    for b in cores:
        out[b] = res[b]["xout"].T
    return out


def kernel(**inp):
    inp = {k: np.asarray(v) for k, v in inp.items()}
    if FUSED:
        return kernel_fused(inp)
    return kernel_unfused(inp)


def kernel_unfused(inp):
    x = inp["x"]
    Bsz, S, _ = x.shape
    L = inp["w_in"].shape[0]
    T = S * Bsz // NCORES
    cps = NCORES // Bsz
    assert cps == 4
    cores = list(range(NCORES))
    Xs = [np.ascontiguousarray(x[c // cps, (c % cps) * T:(c % cps + 1) * T, :].T) for c in cores]
    for l in range(L):
        cv = _cvin(inp, l)
        wp = WPack()
        pack_ffn(wp, "ffn1", inp["ffn1_w_gate"][l], inp["ffn1_w_up"][l], inp["ffn1_w_down"][l])
        pack_a0(wp, inp["w_in"][l])
        flat = wp.flat()
        nc = _get_nc(T, "P1", flat.size, wp.groups)
        res = run_bass_kernel_spmd(nc, [{"xin": Xs[c], "wf": flat, "cvin": cv} for c in cores], core_ids=cores).results
        Xs = [res[c]["xout"] for c in cores]
        zlp = _pad_halo([res[c]["zl"] for c in cores], 2, 1)
        zcp = _pad_halo([res[c]["zc"] for c in cores], 1, 1)
        zxp = _pad_halo([res[c]["zx"] for c in cores], 1, 1)
        wp = WPack()
        pack_a(wp, inp["lru_wa"][l], inp["lru_wx"][l])
        flat = wp.flat()
        nc = _get_nc(T, "P2", flat.size, wp.groups)
        res = run_bass_kernel_spmd(nc, [{"zlp": zlp[c], "wf": flat, "cvin": cv} for c in cores], core_ids=cores).results
        hs = [res[c]["hs"] for c in cores]
        pf = [res[c]["pf"] for c in cores]
        pb = [res[c]["pb"] for c in cores]
        summ = [res[c]["summ"] for c in cores]
        e2 = [res[c]["e2"] for c in cores]
        rcv = []
        for c in cores:
            r = np.zeros((128, 6, 4 * DC), np.float32)
            for n in range(3):
                if c % cps - n - 1 >= 0:
                    r[:, n] = e2[c - n - 1]
                if c % cps + n + 1 < cps:
                    r[:, 3 + n] = e2[c + n + 1]
            rcv.append(r.reshape(128, -1))
        wp = WPack()
        pack_b(wp, inp["w_in"][l], inp["lru_w_out"][l], inp["sc_w_out"][l], inp["sgu_w_out"][l], inp["w_o"][l])
        pack_ffn(wp, "ffn2", inp["ffn2_w_gate"][l], inp["ffn2_w_up"][l], inp["ffn2_w_down"][l])
        flat = wp.flat()
        lnc = np.concatenate([np.broadcast_to(inp["sgu_ln_g"][l][None, :], (128, 512)),
                              np.broadcast_to(inp["sgu_ln_b"][l][None, :], (128, 512))], axis=1).astype(np.float32)
        wst = np.ascontiguousarray(np.transpose(inp["sgu_w_s"][l], (2, 0, 1))).reshape(128, 512).astype(np.float32)
        bsr = np.ascontiguousarray(inp["sgu_b"][l].reshape(1, 512)).astype(np.float32)
        nc = _get_nc(T, "P3", flat.size, wp.groups)
        res = run_bass_kernel_spmd(nc, [{"xin": Xs[c], "wf": flat, "cvin": cv, "hs": hs[c], "pf": pf[c], "pb": pb[c],
                                         "zcp": zcp[c], "zxp": zxp[c], "summ": summ[c], "rcv": rcv[c],
                                         "lnc": np.ascontiguousarray(lnc), "wst": wst, "bsr": bsr} for c in cores],
                                   core_ids=cores).results
        Xs = [res[c]["xout"] for c in cores]
    out = np.zeros((Bsz, S, D), np.float32)
    for c in cores:
        out[c // cps, (c % cps) * T:(c % cps + 1) * T, :] = Xs[c].T
    return out
```
